# Optimizing a Trainium2 kernel written in Bass

```python
import jax, jax.numpy as jnp
from jax import lax
import numpy as np

D_MODEL = 1024
BATCH = 2
SEQ = 8192
DEPTH = 1

N_HEADS_A = 8
HEAD_DIM_A = 64
ROT_DIM = HEAD_DIM_A // 4
ROPE_THETA = 500000.0
N_HEADS_IDX = 8
HEAD_DIM_IDX = 64
TOPK_MAX = 256
Q_BLOCK = 128
N_HEADS_R = 4
HEAD_DIM_RK = 128
HEAD_DIM_RV = 256
RET_CHUNK = 128
RET_THETA = 10000.0
D_FF = 2816
NORM_EPS = 1e-6
NEG = -1e30

SPLIT_SIZES = (
    N_HEADS_A * HEAD_DIM_A,
    N_HEADS_A * HEAD_DIM_A,
    N_HEADS_A * HEAD_DIM_A,
    N_HEADS_IDX * HEAD_DIM_IDX,
    HEAD_DIM_IDX,
    N_HEADS_IDX,
    N_HEADS_R * HEAD_DIM_RK,
    N_HEADS_R * HEAD_DIM_RK,
    N_HEADS_R * HEAD_DIM_RV,
    N_HEADS_R * HEAD_DIM_RV,
    D_MODEL,
    D_MODEL,
)
D_IN = sum(SPLIT_SIZES)

kernel_name = "hybrid_dsa_retention_gated_layer"


def _rmsnorm(x, w):
    xf = x.astype(jnp.float32)
    y = xf * lax.rsqrt(jnp.mean(xf * xf, axis=-1, keepdims=True) + NORM_EPS)
    return (y * w.astype(jnp.float32)).astype(x.dtype)


def _rope_tables(seq_len, dim, theta):
    inv = 1.0 / (theta ** (jnp.arange(0, dim, 2, dtype=jnp.float32) / dim))
    ang = jnp.arange(seq_len, dtype=jnp.float32)[:, None] * inv[None, :]
    return jnp.cos(ang), jnp.sin(ang)


def _rotate(x, cos, sin):
    x1, x2 = jnp.split(x, 2, axis=-1)
    cs, sn = cos[:, None, :], sin[:, None, :]
    return jnp.concatenate([x1 * cs - x2 * sn, x1 * sn + x2 * cs], axis=-1).astype(x.dtype)


def _partial_rope(x, cos, sin):
    return jnp.concatenate([_rotate(x[..., :ROT_DIM], cos, sin), x[..., ROT_DIM:]], axis=-1)


def _dsa_attention(q, k, v, iq, ik, iw, topk):
    B, S, H, Dh = q.shape
    n_blk = S // Q_BLOCK
    kpos = jnp.arange(S)
    ikf = ik.astype(jnp.float32)
    idx_scale = HEAD_DIM_IDX ** -0.5
    w_scale = N_HEADS_IDX ** -0.5
    att_scale = Dh ** -0.5

    def block(i):
        start = i * Q_BLOCK
        qb = lax.dynamic_slice_in_dim(q, start, Q_BLOCK, axis=1)
        iqb = lax.dynamic_slice_in_dim(iq, start, Q_BLOCK, axis=1).astype(jnp.float32)
        iwb = lax.dynamic_slice_in_dim(iw, start, Q_BLOCK, axis=1).astype(jnp.float32)
        qpos = start + jnp.arange(Q_BLOCK)
        causal = kpos[None, :] <= qpos[:, None]
        s_h = jnp.einsum('bqhd,bsd->bqhs', iqb, ikf) * idx_scale
        score = jnp.einsum('bqh,bqhs->bqs', iwb * w_scale, jax.nn.relu(s_h))
        score = jnp.where(causal[None], score, NEG)
        _, idx = lax.top_k(score, topk)
        valid = idx <= qpos[None, :, None]
        kg = jax.vmap(lambda kb, ib: kb[ib])(k, idx)
        vg = jax.vmap(lambda vb, ib: vb[ib])(v, idx)
        logits = jnp.einsum('bqhd,bqkhd->bqhk', qb, kg).astype(jnp.float32) * att_scale
        logits = jnp.where(valid[:, :, None, :], logits, NEG)
        p = jax.nn.softmax(logits, axis=-1).astype(v.dtype)
        return jnp.einsum('bqhk,bqkhd->bqhd', p, vg)

    out = lax.map(block, jnp.arange(n_blk))
    return jnp.moveaxis(out, 0, 1).reshape(B, S, H, Dh)


def _retention(q, k, v):
    B, S, H, Dk = q.shape
    Dv = v.shape[-1]
    C = RET_CHUNK
    N = S // C
    log_g = jnp.log(1.0 - 2.0 ** (-5.0 - jnp.arange(H, dtype=jnp.float32)))
    qc = q.astype(jnp.float32).reshape(B, N, C, H, Dk)
    kc = k.astype(jnp.float32).reshape(B, N, C, H, Dk)
    vc = v.astype(jnp.float32).reshape(B, N, C, H, Dv)
    j = jnp.arange(C, dtype=jnp.float32)
    diff = j[:, None] - j[None, :]
    dmask = jnp.where(diff[None] >= 0,
                      jnp.exp(log_g[:, None, None] * jnp.maximum(diff, 0.0)[None]), 0.0)
    inner = jnp.einsum('bnihd,bnjhd->bnhij', qc, kc) * dmask[None, None]
    o_in = jnp.einsum('bnhij,bnjhe->bnihe', inner, vc)
    kdec = jnp.exp(log_g[:, None] * (C - 1.0 - j)[None, :])
    u = jnp.einsum('bnjhd,hj,bnjhe->bnhde', kc, kdec, vc)
    cdec = jnp.exp(log_g * C)[None, :, None, None]

    def step(r, u_n):
        return r * cdec + u_n, r

    _, r_prev = lax.scan(step, jnp.zeros((B, H, Dk, Dv), jnp.float32), jnp.moveaxis(u, 1, 0))
    r_prev = jnp.moveaxis(r_prev, 0, 1)
    qdec = jnp.exp(log_g[None, :] * (j + 1.0)[:, None])
    o_x = jnp.einsum('bnihd,bnhde->bnihe', qc, r_prev) * qdec[None, None, :, :, None]
    return (o_in + o_x).reshape(B, S, H, Dv).astype(v.dtype)


def _head_groupnorm(o, w):
    of = o.astype(jnp.float32)
    mu = jnp.mean(of, axis=-1, keepdims=True)
    var = jnp.mean(jnp.square(of - mu), axis=-1, keepdims=True)
    y = (of - mu) * lax.rsqrt(var + NORM_EPS)
    B, S, H, Dv = o.shape
    return (y.reshape(B, S, H * Dv) * w.astype(jnp.float32)).astype(o.dtype)


def setup_inputs(seed: int = 0) -> dict:
    key = jax.random.key(seed)
    ks = jax.random.split(key, 16)
    f32 = jnp.float32
    D = D_MODEL

    def lin(k, fi, fo):
        return jax.random.normal(k, (fi, fo), f32) * fi ** -0.5

    def gain(k, n):
        return 1.0 + 0.02 * jax.random.normal(k, (n,), f32)

    return {
        "x": jax.random.normal(ks[0], (BATCH, SEQ, D), f32),
        "c": jax.random.normal(ks[1], (BATCH, D), f32),
        "w_ada": lin(ks[2], D, 6 * D) * 0.5,
        "b_ada": 0.01 * jax.random.normal(ks[3], (6 * D,), f32),
        "norm1_w": gain(ks[4], D),
        "w_in": lin(ks[5], D, D_IN),
        "w_attn_proj": lin(ks[6], N_HEADS_A * HEAD_DIM_A, D),
        "w_ret_proj": lin(ks[7], N_HEADS_R * HEAD_DIM_RV, D),
        "gn_w": gain(ks[8], N_HEADS_R * HEAD_DIM_RV),
        "w_out": lin(ks[9], D, D),
        "norm2_w": gain(ks[10], D),
        "w_ffn_gate": lin(ks[11], D, D_FF),
        "w_ffn_up": lin(ks[12], D, D_FF),
        "w_ffn_down": lin(ks[13], D_FF, D),
        "final_norm_w": gain(ks[14], D),
    }


def reference(x, c, w_ada, b_ada, norm1_w, w_in, w_attn_proj, w_ret_proj, gn_w, w_out,
              norm2_w, w_ffn_gate, w_ffn_up, w_ffn_down, final_norm_w):
    B, S, D = x.shape
    topk = min(TOPK_MAX, S // 4)
    cos_a, sin_a = _rope_tables(S, ROT_DIM, ROPE_THETA)
    cos_r, sin_r = _rope_tables(S, HEAD_DIM_RK, RET_THETA)

    mod = jnp.dot(jax.nn.silu(c), w_ada) + b_ada
    shift1, scale1, gate1, shift2, scale2, gate2 = [m[:, None, :] for m in jnp.split(mod, 6, axis=-1)]

    for _ in range(DEPTH):
        h = _rmsnorm(x, norm1_w) * (1.0 + scale1) + shift1
        proj = jnp.dot(h, w_in)
        (qa, ka, va, iq, ik, iw, qr, kr, vr, gr, ga, gb) = jnp.split(
            proj, np.cumsum(SPLIT_SIZES)[:-1], axis=-1)

        qa = _partial_rope(qa.reshape(B, S, N_HEADS_A, HEAD_DIM_A), cos_a, sin_a)
        ka = _partial_rope(ka.reshape(B, S, N_HEADS_A, HEAD_DIM_A), cos_a, sin_a)
        va = va.reshape(B, S, N_HEADS_A, HEAD_DIM_A)
        iq = _partial_rope(iq.reshape(B, S, N_HEADS_IDX, HEAD_DIM_IDX), cos_a, sin_a)
        ik = _partial_rope(ik[:, :, None, :], cos_a, sin_a)[:, :, 0, :]
        att = _dsa_attention(qa, ka, va, iq, ik, iw, topk)
        y_a = jnp.dot(att.reshape(B, S, N_HEADS_A * HEAD_DIM_A), w_attn_proj)

        qr = _rotate(qr.reshape(B, S, N_HEADS_R, HEAD_DIM_RK), cos_r, sin_r)
        kr = _rotate(kr.reshape(B, S, N_HEADS_R, HEAD_DIM_RK), cos_r, sin_r) * (HEAD_DIM_RK ** -0.5)
        ret = _retention(qr, kr, vr.reshape(B, S, N_HEADS_R, HEAD_DIM_RV))
        y_r = jnp.dot(jax.nn.silu(gr) * _head_groupnorm(ret, gn_w), w_ret_proj)

        merged = jax.nn.sigmoid(ga) * y_a + jax.nn.sigmoid(gb) * y_r
        x = x + gate1 * jnp.dot(merged, w_out)

        h2 = _rmsnorm(x, norm2_w) * (1.0 + scale2) + shift2
        ffn = jnp.dot(jax.nn.silu(jnp.dot(h2, w_ffn_gate)) * jnp.dot(h2, w_ffn_up), w_ffn_down)
        x = x + gate2 * ffn

    return _rmsnorm(x, final_norm_w)
```

```python
import bisect
from contextlib import ExitStack

import numpy as np
import ml_dtypes

import concourse.bass as bass
import concourse.mybir as mybir
from concourse.bass_utils import run_bass_kernel_spmd

F32 = mybir.dt.float32
BF16 = mybir.dt.bfloat16
ALU = mybir.AluOpType
AF = mybir.ActivationFunctionType
AX = mybir.AxisListType

D = 1024
S = 8192
NB = 64
NOWN = 16
DIN = 7240
DFF = 2816
EPS = 1e-6
TOPK = 256
NIT = 13
NEGM = -30000.0
C_QA, C_KA, C_VA, C_IQ, C_IK, C_IW, C_QR, C_KR, C_VR, C_GR, C_GA, C_GB = (
    0, 512, 1024, 1536, 2048, 2112, 2120, 2632, 3144, 4168, 5192, 6216)
GAMMA = [1.0 - 2.0 ** (-5.0 - h) for h in range(4)]


class FW:
    def __init__(self, nc, ndma=8):
        self.nc = nc
        self.E = {"pe": nc.tensor, "act": nc.scalar, "dve": nc.vector, "pool": nc.gpsimd, "sp": nc.sync}
        self.sem, self.cnt, self.seq, self.sigs = {}, {}, {}, {}
        for e in ("pe", "act", "dve", "pool"):
            self.sem[e] = nc.alloc_semaphore("s_" + e)
            self.cnt[e] = 0
            self.seq[e] = 0
            self.sigs[e] = []
        self.dsem, self.duse, self.dnext = {}, {}, {}
        for q in ("sp", "pool", "act"):
            self.dsem[q] = [nc.alloc_semaphore("d_%s%d" % (q, i)) for i in range(ndma)]
            self.duse[q] = [0] * ndma
            self.dnext[q] = 0
        self.seen = {e: {} for e in self.E}
        self.lastw = {}
        self.readers = {}
        self.alldma = []
        self.n = 0

    def _resolve(self, tok):
        if tok[0] == "d":
            return tok[1], tok[2]
        _, eng, seq = tok
        arr = self.sigs[eng]
        i = bisect.bisect_left(arr, (seq, -1))
        if i >= len(arr):
            raise RuntimeError("dependency on unsignaled %s op seq %d" % (eng, seq))
        return self.sem[eng], arr[i][1]

    def _wait(self, eng, tok, kind, dma=False):
        if (not dma) and tok[0] == "c" and tok[1] == eng and eng == "pe":
            return
        sem, val = self._resolve(tok)
        key = sem.num
        if self.seen[eng].get(key, 0) >= val:
            return
        self.E[eng].wait_ge(sem, val)
        self.seen[eng][key] = val

    def op(self, eng, fn, reads=(), writes=(), sig=None, dma=False):
        self.n += 1
        for r in reads:
            w = self.lastw.get(r)
            if w is not None:
                self._wait(eng, w, "raw", dma)
        for r in writes:
            w = self.lastw.get(r)
            if w is not None:
                self._wait(eng, w, "waw", dma)
            for rd in self.readers.get(r, ()):
                self._wait(eng, rd, "war", dma)
        if dma:
            q = eng
            i = self.dnext[q]
            self.dnext[q] = (i + 1) % len(self.dsem[q])
            sem = self.dsem[q][i]
            if self.duse[q][i] > 0:
                self._wait(eng, ("d", sem, 16 * self.duse[q][i]), "raw")
            ins = fn(self.E[eng])
            self.duse[q][i] += 1
            ins.then_inc(sem, 16)
            tok = ("d", sem, 16 * self.duse[q][i])
            self.alldma.append(tok)
        else:
            ins = fn(self.E[eng])
            self.seq[eng] += 1
            if sig is None:
                sig = eng != "pe"
            if sig:
                self.cnt[eng] += 1
                ins.then_inc(self.sem[eng], 1)
                self.sigs[eng].append((self.seq[eng], self.cnt[eng]))
            tok = ("c", eng, self.seq[eng])
        for r in writes:
            self.lastw[r] = tok
            self.readers[r] = []
        for r in reads:
            if r in writes:
                continue
            self.readers.setdefault(r, []).append(tok)
        return ins

    def dma(self, q, out, in_, reads=(), writes=()):
        return self.op(q, lambda e: e.dma_start(out=out, in_=in_), reads, writes, dma=True)

    def barrier(self):
        toks = []
        for e in ("pe", "act", "dve", "pool"):
            if self.seq[e] > 0:
                if not self.sigs[e] or self.sigs[e][-1][0] != self.seq[e]:
                    raise RuntimeError("barrier: last %s op not signaled" % e)
                toks.append(("c", e, self.seq[e]))
        toks += self.alldma
        self.alldma = []
        for eng in self.E:
            for t in toks:
                if t[0] == "c" and t[1] == eng:
                    continue
                self._wait(eng, t, "raw")
        self.lastw = {}
        self.readers = {}

    def finish(self):
        for t in self.alldma:
            self._wait("sp", t, "raw")
        self.alldma = []

    def mm(self, out, lhsT, rhs, start, stop, r, w, sig=None):
        if sig is None:
            sig = stop
        return self.op("pe", lambda e: e.matmul(out, lhsT=lhsT, rhs=rhs, start=start, stop=stop), r, w, sig=sig)

    def tr(self, out, in_, ident, r, w, sig):
        return self.op("pe", lambda e: e.transpose(out=out, in_=in_, identity=ident), r, w, sig=sig)

    def act(self, out, in_, func, r, w, **kw):
        return self.op("act", lambda e: e.activation(out=out, in_=in_, func=func, **kw), r, w)

    def copy(self, eng, out, in_, r, w):
        if eng == "act":
            return self.op("act", lambda e: e.copy(out=out, in_=in_), r, w)
        return self.op(eng, lambda e: e.tensor_copy(out=out, in_=in_), r, w)

    def tt(self, eng, out, in0, in1, op, r, w):
        return self.op(eng, lambda e: e.tensor_tensor(out=out, in0=in0, in1=in1, op=op), r, w)

    def ts(self, eng, out, in0, s1, s2, op0, op1, r, w, **kw):
        if op1 is None:
            return self.op(eng, lambda e: e.tensor_scalar(out=out, in0=in0, scalar1=s1, scalar2=None, op0=op0, **kw), r, w)
        return self.op(eng, lambda e: e.tensor_scalar(out=out, in0=in0, scalar1=s1, scalar2=s2, op0=op0, op1=op1, **kw), r, w)

    def stt(self, out, in0, scalar, in1, op0, op1, r, w):
        return self.op("dve", lambda e: e.scalar_tensor_tensor(out=out, in0=in0, scalar=scalar, in1=in1, op0=op0, op1=op1), r, w)


def build_program(stop_after=99, dbg=False, nb_lim=NB, nown_lim=NOWN):
    nc = bass.Bass("TRN2", target_bir_lowering=False)
    kind_s = "ExternalOutput" if dbg else "Internal"

    def din(name, shape, dt=F32):
        return nc.dram_tensor(name, list(shape), dt, kind="ExternalInput").ap()

    def dscr(name, shape, dt):
        return nc.dram_tensor(name, list(shape), dt, kind=kind_s).ap()

    xf = din("xf", [S, D])
    xo = din("xo", [NOWN * 128, D])
    c_l = din("c_l", [128, 8])
    w_ada = din("w_ada", [D, 6 * D])
    b_ada = din("b_ada", [1, 6 * D])
    nws_d = din("nws", [1, 4, D])
    w_in = din("w_in", [D, DIN])
    w_ap = din("w_attn_proj", [512, D])
    w_rp = din("w_ret_proj", [D, D])
    w_o = din("w_out", [D, D])
    w_g = din("w_ffn_gate", [D, DFF])
    w_u = din("w_ffn_up", [D, DFF])
    w_d = din("w_ffn_down", [DFF, D])
    ident_d = din("ident", [128, 128], BF16)
    ropeA_d = din("ropeA", [128, NB, 32])
    ropeR_d = din("ropeR", [NB, 128, 256])
    ropeAo_d = din("ropeAo", [128, NOWN, 32])
    ropeRq_d = din("ropeRq", [NOWN, 128, 256])
    ropeRk_d = din("ropeRk", [NOWN, 128, 256])
    cst_d = din("cst", [128, 8 + 512 + 512 + 16])
    maskd_d = din("maskd", [NOWN, 128, 512])
    out_d = nc.dram_tensor("out", [NOWN * 128, D], F32, kind="ExternalOutput").ap()

    MODd = dscr("MODd", [7, 128, D], F32)
    KTd = dscr("KTd", [4, 128, S], BF16)
    Vd = dscr("Vd", [4, 128, NB, 130], BF16)
    IKTd = dscr("IKTd", [128, S], BF16)
    Rd = dscr("Rd", [NOWN, 128, D], BF16)
    QTd = dscr("QTd", [NOWN, 128, 512], BF16)
    IQTd = dscr("IQTd", [NOWN, 128, 512], BF16)
    IWd = dscr("IWd", [NOWN, 128, 8], F32)
    ZTd = dscr("ZTd", [NOWN, 128, D], BF16)
    SGAd = dscr("SGAd", [NOWN, 128, D], BF16)
    SGBd = dscr("SGBd", [NOWN, 128, D], BF16)
    NMd = dscr("NMd", [128, 1], F32)
    X1d = dscr("X1d", [NOWN, 128, D], F32)
    WGd = dscr("WGd", [128, 8, DFF], BF16)
    WUd = dscr("WUd", [128, 8, DFF], BF16)
    WDd = dscr("WDd", [128, 22, D], BF16)

    fw = FW(nc)
    top = ExitStack()

    def T(es, name, shape, dt):
        if dbg:
            print("alloc", name, shape, nc.sbuf_bytes_remaining)
        return es.enter_context(nc.sbuf_tensor("sb_" + name, list(shape), dt))

    PS = [top.enter_context(nc.psum_tensor("ps%d" % i, [128, 512], F32)) for i in range(8)]

    def psb(i):
        return PS[i][:].bitcast(BF16)

    ident = T(top, "ident", [128, 128], BF16)
    fw.dma("sp", ident[:], ident_d, writes=["ident"])
    cst = T(top, "cst", [128, 8 + 1024 + 16], F32)
    fw.dma("sp", cst[:], cst_d, writes=["cst"])
    oh = cst[:, 0:4]
    kdec = cst[:, 4:8]
    qdecT = cst[:, 8:520]
    dmaskT = cst[:, 520:1032]

    with ExitStack() as es:
        cl = T(es, "cl", [128, 8], F32)
        sc = T(es, "sc", [128, 8], F32)
        scb = T(es, "scb", [128, 8, 128], F32)
        ones1 = T(es, "ones1", [1, 128], F32)
        bada = T(es, "bada", [1, 6 * D], F32)
        nws = T(es, "nws", [1, 4, D], F32)
        modbc = T(es, "modbc", [128, 6, D], F32)
        nwbc = T(es, "nwbc", [128, 4, D], F32)
        mo = T(es, "mo", [128, 2, D], F32)
        wa = [T(es, "wa%d" % i, [128, 8, 512], F32) for i in range(2)]
        fw.dma("sp", cl[:], c_l, writes=["cl"])
        fw.dma("sp", bada[:], b_ada, writes=["bada"])
        fw.dma("sp", nws[:], nws_d, writes=["nws"])
        fw.op("pool", lambda e: e.memset(ones1[:], 1.0), writes=["ones1"])
        fw.act(sc[:], cl[:], AF.Silu, ["cl"], ["sc"])
        for kc in range(8):
            fw.copy("dve", scb[:, kc, :], sc[:, kc:kc + 1].to_broadcast([128, 128]), ["sc"], [("scb", kc)])
        for ncx in range(12):
            s = ncx % 2
            n0 = ncx * 512
            fw.dma("sp" if s == 0 else "pool", wa[s][:],
                   w_ada[:, n0:n0 + 512].rearrange("(kc p) n -> p kc n", p=128), writes=[("wa", s)])
            pb = PS[s]
            for kc in range(8):
                fw.mm(pb[:], scb[:, kc, :], wa[s][:, kc, :], kc == 0, False, [("scb", kc), ("wa", s)], [("ps", s)])
            fw.mm(pb[:], ones1[0:1, :], bada[0:1, n0:n0 + 512], False, True, ["ones1", "bada"], [("ps", s)])
            fw.copy("act", modbc[:, ncx // 2, (ncx % 2) * 512:(ncx % 2) * 512 + 512], pb[:], [("ps", s)], [("modbc", ncx // 2)])
        for v in range(4):
            for hf in range(2):
                s = hf
                fw.mm(PS[s][:], ones1[0:1, :], nws[0:1, v, hf * 512:hf * 512 + 512], True, True, ["ones1", "nws"], [("ps", s)])
                fw.copy("act", nwbc[:, v, hf * 512:hf * 512 + 512], PS[s][:], [("ps", s)], [("nwbc", v)])
        fw.stt(mo[:, 0, :], modbc[:, 1, :], 1.0, nwbc[:, 0, :], ALU.add, ALU.mult, [("modbc", 1), ("nwbc", 0)], [("mo", 0)])
        fw.stt(mo[:, 1, :], modbc[:, 4, :], 1.0, nwbc[:, 1, :], ALU.add, ALU.mult, [("modbc", 4), ("nwbc", 1)], [("mo", 1)])
        fw.dma("sp", MODd[0], mo[:, 0, :], reads=[("mo", 0)])
        fw.dma("sp", MODd[1], modbc[:, 0, :], reads=[("modbc", 0)])
        fw.dma("sp", MODd[2], modbc[:, 2, :], reads=[("modbc", 2)])
        fw.dma("sp", MODd[3], mo[:, 1, :], reads=[("mo", 1)])
        fw.dma("sp", MODd[4], modbc[:, 3, :], reads=[("modbc", 3)])
        fw.dma("sp", MODd[5], modbc[:, 5, :], reads=[("modbc", 5)])
        fw.dma("sp", MODd[6], nwbc[:, 2, :], reads=[("nwbc", 2)])
        GNd = dscr("GNd", [128, D], F32)
        fw.dma("sp", GNd, nwbc[:, 3, :], reads=[("nwbc", 3)])
        fw.barrier()
    if stop_after <= 0:
        fw.finish()
        return nc

    with ExitStack() as es:
        win = T(es, "win", [128, 8, DIN], BF16)
        A1 = T(es, "A1", [128, D], F32)
        B1 = T(es, "B1", [128, D], F32)
        fw.dma("sp", A1[:], MODd[0], writes=["A1"])
        fw.dma("sp", B1[:], MODd[1], writes=["B1"])
        WIN_R = [("win", kc) for kc in range(8)]

        xt = [T(es, "xt%d" % i, [128, D], F32) for i in range(2)]
        tmpf = T(es, "tmpf", [128, D], F32)
        hb = [T(es, "hb%d" % i, [128, D], BF16) for i in range(2)]
        hT = [T(es, "hT%d" % i, [128, 8, 128], BF16) for i in range(2)]
        st4 = T(es, "st4", [128, 8], F32)
        ropeAo = T(es, "ropeAo", [128, NOWN, 32], F32)
        fw.dma("sp", ropeAo[:], ropeAo_d, writes=["ropeAo"])
        rq = [T(es, "rq%d" % i, [128, 256], F32) for i in range(2)]
        kb = T(es, "kb", [128, 512], BF16)
        rotf = T(es, "rotf", [128, 8, 16], F32)
        tC = T(es, "tC", [128, 512], F32)
        tS = T(es, "tS", [128, 512], F32)
        krf = T(es, "krf", [128, 4, 128], F32)
        krb2 = [T(es, "krb%d" % i, [128, 4, 128], BF16) for i in range(2)]
        krb = krb2[0]
        sq = T(es, "sq", [128, 512], F32)
        ks8 = T(es, "ks8", [128, 8], F32)
        kmx = T(es, "kmx", [128, 8], F32)
        qmx = T(es, "qmx", [128, 8], F32)
        rk = [T(es, "rk%d" % i, [128, 256], F32) for i in range(2)]
        identf = T(es, "identf", [128, 128], F32)
        fw.copy("dve", identf[:], ident[:], ["ident"], ["identf"])
        es_a = ExitStack()
        stg = [T(es_a, "stg%d" % i, [128, 1810], F32) for i in range(2)]
        ropeA = T(es_a, "ropeA", [128, NB, 32], F32)
        rr = [T(es_a, "rr%d" % i, [128, 256], F32) for i in range(2)]
        kT = [T(es_a, "kT%d" % i, [128, 4, 128], BF16) for i in range(2)]
        vx = [T(es_a, "vx%d" % i, [128, 8, 65], BF16) for i in range(2)]
        ikf = T(es_a, "ikf", [128, 64], F32)
        ikb2 = T(es_a, "ikb2", [128, 2, 64], BF16)
        ikT = [T(es_a, "ikT%d" % i, [128, 128], BF16) for i in range(2)]
        vdb2 = [T(es_a, "vdb%d" % i, [128, 4, 256], BF16) for i in range(2)]
        Rst = T(es_a, "Rst", [128, 4, 256], F32)
        Rown = [T(es_a, "Rown%d" % i, [128, D], BF16) for i in range(2)]
        sbufs = [(stg[0][:, 0:905], ("stg", 0, "a")), (stg[0][:, 905:1810], ("stg", 0, "b")),
                 (stg[1][:, 0:905], ("stg", 1, "a")), (stg[1][:, 905:1810], ("stg", 1, "b")),
                 (xt[0][:, 0:905], ("xt", 0)), (xt[1][:, 0:905], ("xt", 1)), (tmpf[:, 0:905], "tmpf")]
        k = 0
        for kc in range(8):
            for part in range(8):
                sap, sres = sbufs[k % 7]
                c0 = part * 905
                fw.dma("sp" if k % 2 == 0 else "pool", sap, w_in[kc * 128:(kc + 1) * 128, c0:c0 + 905], writes=[sres])
                fw.copy(("act", "dve", "act", "dve", "pool")[k % 5], win[:, kc, c0:c0 + 905], sap, [sres], [("win", kc)])
                k += 1
        fw.dma("sp", ropeA[:], ropeA_d, writes=["ropeA"])
        fw.op("pool", lambda e: e.memset(Rst[:], 0.0), writes=["Rst"])
        fw.op("pool", lambda e: e.memset(kmx[:], 0.0), writes=["kmx"])
        fw.op("pool", lambda e: e.memset(qmx[:], 0.0), writes=["qmx"])
        for i in range(2):
            fw.op("pool", lambda e, i=i: e.memset(vx[i][:], 1.0), writes=[("vx", i)])

        def load_x(xsrc, s):
            fw.dma("sp", xt[s][:], xsrc, writes=[("xt", s)])

        def elem(s):
            fw.act(hb[s][:], xt[s][:], AF.Square, [("xt", s)], [("hb", s), "ss"], accum_out=st4[:, 0:1])
            fw.ts("dve", st4[:, 1:2], st4[:, 0:1], 1.0 / D, EPS, ALU.mult, ALU.add, ["ss"], ["vv"])
            fw.act(st4[:, 2:3], st4[:, 1:2], AF.Sqrt, ["vv"], ["sd"])
            fw.op("dve", lambda e: e.reciprocal(out=st4[:, 3:4], in_=st4[:, 2:3]), ["sd"], ["rstd"])
            fw.stt(tmpf[:], xt[s][:], st4[:, 3:4], A1[:], ALU.mult, ALU.mult, [("xt", s), "rstd", "A1"], ["tmpf"])
            fw.tt("pool", hb[s][:], tmpf[:], B1[:], ALU.add, ["tmpf", "B1"], [("hb", s)])

        def trans(s):
            tb_ = psb(0).rearrange("p (a b) -> p a b", b=128)
            for kc in range(8):
                fw.tr(tb_[:, kc, :], hb[s][:, kc * 128:(kc + 1) * 128], ident[:], [("hb", s), "ident"], [("ps", 0)], sig=(kc == 7))
            fw.copy("act", hT[s][:], tb_, [("ps", 0)], [("hT", s)])

        cbank = [1, 2, 3, 7]
        cstate = {"i": 0}

        def proj(s, c0, ncol):
            b = cbank[cstate["i"] % 4]
            cstate["i"] += 1
            for kc in range(8):
                fw.mm(PS[b][:, 0:ncol], hT[s][:, kc, :], win[:, kc, c0:c0 + ncol], kc == 0, kc == 7,
                      [("hT", s), ("win", kc)], [("ps", b)])
            return b

        def rope16(src_f, dst_b, tab, nh, nm, tres):
            CC = tab[:, 0:16].unsqueeze(1).to_broadcast([128, nh, 16])
            SSa = tab[:, 16:24].unsqueeze(1).to_broadcast([128, nh, 8])
            SSb = tab[:, 24:32].unsqueeze(1).to_broadcast([128, nh, 8])
            tCv = tC[:, 0:nh * 16].rearrange("p (h d) -> p h d", d=16)
            tSv = tS[:, 0:nh * 16].rearrange("p (h d) -> p h d", d=16)
            fw.tt("pool", tCv, src_f, CC, ALU.mult, [nm + "_f", tres], ["tC"])
            fw.tt("pool", tSv[:, :, 0:8], src_f[:, :, 8:16], SSa, ALU.mult, [nm + "_f", tres], ["tS"])
            fw.tt("pool", tSv[:, :, 8:16], src_f[:, :, 0:8], SSb, ALU.mult, [nm + "_f", tres], ["tS"])
            fw.tt("pool", dst_b, tCv, tSv, ALU.add, ["tC", "tS"], [nm + "_b"])

        def rope128(src_f, dst_b, tab, nm, tres, fres=None):
            fres = fres or nm
            CC = tab[:, 0:128].unsqueeze(1).to_broadcast([128, 4, 128])
            SSa = tab[:, 128:192].unsqueeze(1).to_broadcast([128, 4, 64])
            SSb = tab[:, 192:256].unsqueeze(1).to_broadcast([128, 4, 64])
            tCv = tC[:].rearrange("p (h d) -> p h d", d=128)
            tSv = tS[:].rearrange("p (h d) -> p h d", d=128)
            fw.tt("pool", tCv, src_f, CC, ALU.mult, [fres + "_f", tres], ["tC"])
            fw.tt("pool", tSv[:, :, 0:64], src_f[:, :, 64:128], SSa, ALU.mult, [fres + "_f", tres], ["tS"])
            fw.tt("pool", tSv[:, :, 64:128], src_f[:, :, 0:64], SSb, ALU.mult, [fres + "_f", tres], ["tS"])
            fw.tt("pool", dst_b, tCv, tSv, ALU.add, ["tC", "tS"], [nm + "_b"])

        def sumsq_max(srcb, acc, accname, nm):
            fw.tt("pool", sq[:], srcb, srcb, ALU.mult, [nm + "_b"], ["sq"])
            fw.op("dve", lambda e: e.tensor_reduce(out=ks8[:], in_=sq[:].rearrange("p (h d) -> p h d", d=64),
                                                   axis=AX.X, op=ALU.add), ["sq"], ["ks8"])
            fw.tt("dve", acc[:], acc[:], ks8[:], ALU.max, ["ks8", accname], [accname])

        load_x(xf[0:128, :], 0)
        if nb_lim > 1:
            load_x(xf[128:256, :], 1)
        fw.dma("pool", rr[0][:], ropeR_d[0], writes=[("rr", 0)])
        elem(0)
        trans(0)
        cvp = []
        for (wsrc, wdst) in ((w_g, WGd), (w_u, WUd)):
            for kc in range(8):
                for hf in range(2):
                    cvp.append((wsrc[kc * 128:(kc + 1) * 128, hf * 1408:(hf + 1) * 1408], wdst[:, kc, hf * 1408:(hf + 1) * 1408], 1408))
        for fc in range(22):
            cvp.append((w_d[fc * 128:(fc + 1) * 128, :], WDd[:, fc, :], D))

        def conv_in(k):
            if k < len(cvp):
                wsrc_, wdst_, n_ = cvp[k]
                fw.dma("sp", stg[0][:, 0:n_], wsrc_, writes=[("stg", 0), ("stg", 0, "a"), ("stg", 0, "b")])

        def conv_cast(k):
            if k < len(cvp):
                wsrc_, wdst_, n_ = cvp[k]
                fw.copy("dve", stg[1][:].bitcast(BF16)[:, 0:n_], stg[0][:, 0:n_], [("stg", 0)], [("stg", 1), ("stg", 1, "a"), ("stg", 1, "b")])

        def conv_out(k):
            if k < len(cvp):
                wsrc_, wdst_, n_ = cvp[k]
                fw.dma("sp", wdst_, stg[1][:].bitcast(BF16)[:, 0:n_], reads=[("stg", 1)])

        if nb_lim == NB:
            conv_in(0)
        upending = []
        for tb in range(nb_lim):
            s = tb % 2
            if tb + 1 < nb_lim:
                fw.dma("pool", rr[1 - s][:], ropeR_d[tb + 1], writes=[("rr", 1 - s)])
                elem(1 - s)
            b = proj(s, C_KA, 512)
            fw.copy("act", kb[:], PS[b][:], [("ps", b)], ["k_b"])
            fw.copy("act", rotf[:], PS[b][:].rearrange("p (h d) -> p h d", d=64)[:, :, 0:16], [("ps", b)], ["k_f"])
            rope16(rotf[:], kb[:].rearrange("p (h d) -> p h d", d=64)[:, :, 0:16], ropeA[:, tb, :], 8, "k", "ropeA")
            sumsq_max(kb[:], kmx, "kmx", "k")
            for fn_ in upending:
                fn_()
            upending = []
            b = proj(s, C_VA, 512)
            fw.copy("act", vx[s][:, :, 0:64], PS[b][:].rearrange("p (h d) -> p h d", d=64), [("ps", b)], [("vx", s)])
            fw.dma("sp", Vd[:, :, tb, :].rearrange("q p c -> p q c"), vx[s][:].rearrange("p (q a) c -> p q (a c)", a=2),
                   reads=[("vx", s)])
            b = proj(s, C_IK, 64)
            fw.copy("act", ikf[:], PS[b][:, 0:64], [("ps", b)], ["ik_f"])
            fw.copy("act", ikb2[:], PS[b][:, 0:64].unsqueeze(1).to_broadcast([128, 2, 64]), [("ps", b)], ["ik_b"])
            rope16(ikf[:, 0:16].unsqueeze(1).to_broadcast([128, 2, 16]), ikb2[:, :, 0:16], ropeA[:, tb, :], 2, "ik", "ropeA")
            b = proj(s, C_KR, 512)
            fw.copy("act", krf[:], PS[b][:].rearrange("p (h d) -> p h d", d=128), [("ps", b)], ["kr_f"])
            rope128(krf[:], krb2[s][:], rr[s], "kr%d" % s, ("rr", s), fres="kr")
            for half in range(2):
                b = proj(s, C_VR + half * 512, 512)
                for hh in range(2):
                    h = half * 2 + hh
                    fw.act(vdb2[s][:, h, :], PS[b][:, hh * 256:(hh + 1) * 256], AF.Copy, [("ps", b), "cst"], [("vdb", s, h)],
                           scale=kdec[:, h:h + 1])
            if tb + 2 < nb_lim:
                load_x(xf[(tb + 2) * 128:(tb + 3) * 128, :], s)
            t2 = psb(4).rearrange("p (a b) -> p a b", b=128)
            for pr in range(4):
                fw.tr(t2[:, pr, :], kb[:, pr * 128:(pr + 1) * 128], ident[:], ["k_b", "ident"], [("ps", 4)], sig=False)
            fw.tr(t2[:, 4, :], ikb2[:].rearrange("p a d -> p (a d)"), ident[:], ["ik_b", "ident"], [("ps", 4)], sig=True)
            if nb_lim == NB:
                conv_cast(tb)
            fw.copy("dve", kT[s][:], t2[:, 0:4, :], [("ps", 4)], [("kT", s)])
            fw.copy("dve", ikT[s][:], t2[:, 4, :], [("ps", 4)], [("ikT", s)])
            fw.dma("sp", KTd[:, :, tb * 128:(tb + 1) * 128].rearrange("q p t -> p q t"), kT[s][:], reads=[("kT", s)])
            fw.dma("sp", IKTd[:, tb * 128:(tb + 1) * 128], ikT[s][:], reads=[("ikT", s)])
            if nb_lim == NB:
                conv_out(tb)
                conv_in(tb + 1)
            def ustep(tb=tb, s=s):
                g, m = tb // 4, tb % 4
                rs = g % 2
                Rflat = Rst[:].rearrange("p h e -> p (h e)")
                if m == 0:
                    fw.ts("dve", Rown[rs][:], Rflat, oh[:, 0:1], None, ALU.mult, None, ["Rst", "cst"], [("Rown", rs)])
                else:
                    fw.stt(Rown[rs][:], Rflat, oh[:, m:m + 1], Rown[rs][:], ALU.mult, ALU.add, ["Rst", "cst", ("Rown", rs)], [("Rown", rs)])
                if m == 3:
                    fw.dma("sp", Rd[g], Rown[rs][:], reads=[("Rown", rs)])
                for h in range(4):
                    ub = 5 + h // 2
                    fw.mm(PS[ub][:, (h % 2) * 256:(h % 2) * 256 + 256], krb2[s][:, h, :], vdb2[s][:, h, :], True, True,
                          ["kr%d_b" % s, ("vdb", s, h)], [("ps", ub)])
                for h in range(4):
                    ub = 5 + h // 2
                    fw.stt(Rst[:, h, :], Rst[:, h, :], float(GAMMA[h] ** 128), PS[ub][:, (h % 2) * 256:(h % 2) * 256 + 256],
                           ALU.mult, ALU.add, ["Rst", ("ps", ub)], ["Rst"])
            upending.append(ustep)
            if tb + 1 < nb_lim:
                trans(1 - s)
        for fn_ in upending:
            fn_()
        fw.barrier()
        es_a.close()
        if stop_after <= 1:
            fw.finish()
            return nc

        with ExitStack() as es_b:
            gnbc = T(es_b, "gnbc", [128, D], F32)
            fw.dma("sp", gnbc[:], GNd, writes=["gnbc"])
            iqb = T(es_b, "iqb", [128, 512], BF16)
            rotf2 = T(es_b, "rotf2", [128, 8, 16], F32)
            qT = [T(es_b, "qT%d" % i, [128, 4, 128], BF16) for i in range(2)]
            iqT = [T(es_b, "iqT%d" % i, [128, 4, 128], BF16) for i in range(2)]
            iwf = [T(es_b, "iwf%d" % i, [128, 8], F32) for i in range(2)]
            qrf = T(es_b, "qrf", [128, 4, 128], F32)
            qrb = T(es_b, "qrb", [128, 4, 128], BF16)
            qrT = T(es_b, "qrT", [128, 4, 128], BF16)
            qxT = T(es_b, "qxT", [128, 4, 128], BF16)
            krT = T(es_b, "krT", [128, 4, 128], BF16)
            vrb = T(es_b, "vrb", [128, 4, 256], BF16)
            innT = T(es_b, "innT", [128, 4, 128], BF16)
            rb = [T(es_b, "rb%d" % i, [128, D], BF16) for i in range(2)]
            bst = T(es_b, "bst", [128, 4, 6], F32)
            mv = T(es_b, "mv", [128, 4, 2], F32)
            gv = T(es_b, "gv", [128, 12], F32)
            yn = T(es_b, "yn", [128, D], F32)
            sg = T(es_b, "sg", [128, D], BF16)
            zb = T(es_b, "zb", [128, D], BF16)
            zT = [T(es_b, "zT%d" % i, [128, 8, 128], BF16) for i in range(2)]
            sga = [T(es_b, "sga%d" % i, [128, D], BF16) for i in range(2)]
            sgb = [T(es_b, "sgb%d" % i, [128, D], BF16) for i in range(2)]

            def tr4(src_b, nm):
                t2 = psb(4).rearrange("p (a b) -> p a b", b=128)
                for pr in range(4):
                    fw.tr(t2[:, pr, :], src_b[:, pr * 128:(pr + 1) * 128], ident[:], [nm, "ident"], [("ps", 4)], sig=(pr == 3))
                return t2[:, 0:4, :]

            load_x(xo[0:128, :], 0)
            if nown_lim > 1:
                load_x(xo[128:256, :], 1)
            fw.dma("pool", rq[0][:], ropeRq_d[0], writes=[("rq", 0)])
            fw.dma("pool", rk[0][:], ropeRk_d[0], writes=[("rk", 0)])
            elem(0)
            trans(0)
            pending = []
            for i in range(nown_lim):
                s = i % 2
                if i + 1 < nown_lim:
                    fw.dma("pool", rq[1 - s][:], ropeRq_d[i + 1], writes=[("rq", 1 - s)])
                    fw.dma("pool", rk[1 - s][:], ropeRk_d[i + 1], writes=[("rk", 1 - s)])
                    elem(1 - s)
                fw.dma("sp", rb[s][:], Rd[i], writes=[("rb", s)])
                b = proj(s, C_QA, 512)
                fw.copy("act", kb[:], PS[b][:], [("ps", b)], ["k_b"])
                fw.copy("act", rotf[:], PS[b][:].rearrange("p (h d) -> p h d", d=64)[:, :, 0:16], [("ps", b)], ["k_f"])
                rope16(rotf[:], kb[:].rearrange("p (h d) -> p h d", d=64)[:, :, 0:16], ropeAo[:, i, :], 8, "k", "ropeAo")
                sumsq_max(kb[:], qmx, "qmx", "k")
                b = proj(s, C_IQ, 512)
                fw.copy("act", iqb[:], PS[b][:], [("ps", b)], ["iq_b"])
                fw.copy("act", rotf2[:], PS[b][:].rearrange("p (h d) -> p h d", d=64)[:, :, 0:16], [("ps", b)], ["iq_f"])
                rope16(rotf2[:], iqb[:].rearrange("p (h d) -> p h d", d=64)[:, :, 0:16], ropeAo[:, i, :], 8, "iq", "ropeAo")
                b = proj(s, C_IW, 8)
                fw.op("act", lambda e, b=b, s=s: e.mul(out=iwf[s][:], in_=PS[b][:, 0:8], mul=float(8 ** -0.5 * 64 ** -0.5)),
                      [("ps", b)], [("iwf", s)])
                fw.dma("sp", IWd[i], iwf[s][:], reads=[("iwf", s)])
                b = proj(s, C_QR, 512)
                fw.copy("act", qrf[:], PS[b][:].rearrange("p (h d) -> p h d", d=128), [("ps", b)], ["qr_f"])
                rope128(qrf[:], qrb[:], rq[s], "qr", ("rq", s))
                b = proj(s, C_KR, 512)
                fw.copy("act", krf[:], PS[b][:].rearrange("p (h d) -> p h d", d=128), [("ps", b)], ["kr_f"])
                rope128(krf[:], krb[:], rk[s], "kr", ("rk", s))
                for half in range(2):
                    b = proj(s, C_VR + half * 512, 512)
                    fw.copy("act", vrb[:, 2 * half:2 * half + 2, :], PS[b][:].rearrange("p (h e) -> p h e", e=256), [("ps", b)], [("vrb", half)])
                if i + 2 < nown_lim:
                    load_x(xo[(i + 2) * 128:(i + 3) * 128, :], s)
                for fn_ in pending:
                    fn_()
                pending = []
                tv = tr4(kb, "k_b")
                fw.copy("dve", qT[s][:], tv, [("ps", 4)], [("qT", s)])
                fw.dma("sp", QTd[i], qT[s][:].rearrange("p a b -> p (a b)"), reads=[("qT", s)])
                tv = tr4(iqb, "iq_b")
                fw.copy("dve", iqT[s][:], tv, [("ps", 4)], [("iqT", s)])
                fw.dma("sp", IQTd[i], iqT[s][:].rearrange("p a b -> p (a b)"), reads=[("iqT", s)])
                for half in range(2):
                    b = proj(s, C_GR + half * 512, 512)
                    fw.act(sg[:, half * 512:(half + 1) * 512], PS[b][:], AF.Silu, [("ps", b)], [("sg", half)])
                tv = tr4(qrb[:].rearrange("p h d -> p (h d)"), "qr_b")
                fw.copy("dve", qrT[:], tv, [("ps", 4)], ["qrT"])
                fw.tt("dve", qxT[:], tv, qdecT.rearrange("p (h t) -> p h t", t=128), ALU.mult, [("ps", 4), "cst"], ["qxT"])
                tv = tr4(krb[:].rearrange("p h d -> p (h d)"), "kr_b")
                fw.copy("dve", krT[:], tv, [("ps", 4)], ["krT"])
                for half in range(2):
                    b = proj(s, C_GA + half * 512, 512)
                    fw.act(sga[s][:, half * 512:(half + 1) * 512], PS[b][:], AF.Sigmoid, [("ps", b)], [("sga", s, half)])
                fw.dma("sp", SGAd[i], sga[s][:], reads=[("sga", s, 0), ("sga", s, 1)])
                b = cbank[cstate["i"] % 4]
                cstate["i"] += 1
                for h in range(4):
                    fw.mm(PS[b][:, h * 128:(h + 1) * 128], krT[:, h, :], qrT[:, h, :], True, True, ["krT", "qrT"], [("ps", b)], sig=(h == 3))
                fw.tt("dve", innT[:], PS[b][:].rearrange("p (h t) -> p h t", t=128), dmaskT.rearrange("p (h t) -> p h t", t=128),
                      ALU.mult, [("ps", b), "cst"], ["innT"])
                for half in range(2):
                    b = proj(s, C_GB + half * 512, 512)
                    fw.act(sgb[s][:, half * 512:(half + 1) * 512], PS[b][:], AF.Sigmoid, [("ps", b)], [("sgb", s, half)])
                fw.dma("sp", SGBd[i], sgb[s][:], reads=[("sgb", s, 0), ("sgb", s, 1)])
                for h in range(4):
                    ob = 5 + h // 2
                    osl = PS[ob][:, (h % 2) * 256:(h % 2) * 256 + 256]
                    fw.mm(osl, innT[:, h, :], vrb[:, h, :], True, False, ["innT", ("vrb", h // 2)], [("ps", ob)], sig=False)
                    fw.mm(osl, qxT[:, h, :], rb[s][:, h * 256:(h + 1) * 256], False, True, ["qxT", ("rb", s)], [("ps", ob)], sig=True)
                if i + 1 < nown_lim:
                    trans(1 - s)
                for h in range(4):
                    ob = 5 + h // 2
                    osl = PS[ob][:, (h % 2) * 256:(h % 2) * 256 + 256]
                    fw.op("dve", lambda e, h=h, osl=osl: e.bn_stats(out=bst[:, h, :], in_=osl), [("ps", ob)], [("bst", h)])
                    fw.op("dve", lambda e, h=h: e.bn_aggr(out=mv[:, h, :], in_=bst[:, h, :]), [("bst", h)], [("mv", h)])
                MV = [("mv", h) for h in range(4)]
                fw.ts("dve", gv[:, 0:4], mv[:, :, 1], EPS, None, ALU.add, None, MV, ["gv0"])
                fw.act(gv[:, 4:8], gv[:, 0:4], AF.Sqrt, ["gv0"], ["gv1"])
                fw.op("dve", lambda e: e.reciprocal(out=gv[:, 8:12], in_=gv[:, 4:8]), ["gv1"], ["gv2"])
                for h in range(4):
                    ob = 5 + h // 2
                    osl = PS[ob][:, (h % 2) * 256:(h % 2) * 256 + 256]
                    fw.ts("dve", yn[:, h * 256:(h + 1) * 256], osl, mv[:, h, 0:1], gv[:, 8 + h:9 + h], ALU.subtract, ALU.mult,
                          [("ps", ob), ("mv", h), "gv2"], [("yn", h)])
                YN = [("yn", h) for h in range(4)]
                fw.tt("dve", yn[:], yn[:], gnbc[:], ALU.mult, YN + ["gnbc"], YN)
                fw.tt("dve", zb[:], yn[:], sg[:], ALU.mult, YN + [("sg", 0), ("sg", 1)], ["zb"])

                def ztail(i=i, s=s):
                    t8 = psb(4).rearrange("p (a b) -> p a b", b=128)
                    for kc in range(8):
                        fw.tr(t8[:, kc, :], zb[:, kc * 128:(kc + 1) * 128], ident[:], ["zb", "ident"], [("ps", 4)], sig=(kc == 7))
                    fw.copy("dve", zT[s][:], t8, [("ps", 4)], [("zT", s)])
                    fw.dma("sp", ZTd[i], zT[s][:].rearrange("p a b -> p (a b)"), reads=[("zT", s)])
                pending.append(ztail)
            for fn_ in pending:
                fn_()

            m2 = T(es_b, "m2", [128, 2], F32)
            r2 = T(es_b, "r2", [1, 8], F32)
            onesr = T(es_b, "onesr", [1, 128], F32)
            nmt = T(es_b, "nmt", [128, 1], F32)
            fw.op("pool", lambda e: e.memset(onesr[:], 1.0), writes=["onesr"])
            fw.op("dve", lambda e: e.tensor_reduce(out=m2[:, 0:1], in_=qmx[:], axis=AX.X, op=ALU.max), ["qmx"], ["m2a"])
            fw.op("dve", lambda e: e.tensor_reduce(out=m2[:, 1:2], in_=kmx[:], axis=AX.X, op=ALU.max), ["kmx"], ["m2b"])
            fw.tr(PS[1][0:1, 0:128], m2[:, 0:1], identf[:], ["m2a", "identf"], [("ps", 1)], sig=False)
            fw.tr(PS[1][0:1, 128:256], m2[:, 1:2], identf[:], ["m2b", "identf"], [("ps", 1)], sig=True)
            fw.op("dve", lambda e: e.tensor_reduce(out=r2[:, 0:2], in_=PS[1][0:1, 0:256].rearrange("p (a t) -> p a t", t=128),
                                                   axis=AX.X, op=ALU.max), [("ps", 1)], ["r2a"])
            fw.tt("dve", r2[:, 2:3], r2[:, 0:1], r2[:, 1:2], ALU.mult, ["r2a"], ["r2b"])
            fw.act(r2[:, 3:4], r2[:, 2:3], AF.Sqrt, ["r2b"], ["r2c"])
            fw.ts("dve", r2[:, 4:5], r2[:, 3:4], -0.125, None, ALU.mult, None, ["r2c"], ["r2d"])
            fw.mm(PS[2][:, 0:1], onesr[0:1, :], r2[0:1, 4:5], True, True, ["onesr", "r2d"], [("ps", 2)])
            fw.copy("dve", nmt[:], PS[2][:, 0:1], [("ps", 2)], ["nmt"])
            fw.dma("sp", NMd, nmt[:], reads=["nmt"])
            fw.barrier()
    if stop_after <= 2:
        fw.finish()
        return nc

    with ExitStack() as es:
        scores = T(es, "scores", [128, S], F32)
        mb = [T(es, "mb%d" % i, [128, S], BF16) for i in range(2)]
        ikT2 = T(es, "ikT2", [128, S], BF16)
        KTc = [T(es, "KTc%d" % i, [128, 4, 1024], BF16) for i in range(2)]
        Vc = [T(es, "Vc%d" % i, [128, 4, 8 * 130], BF16) for i in range(2)]
        mbT = [T(es, "mbT%d" % i, [128, 1024], BF16) for i in range(2)]
        PT = [T(es, "PT%d" % i, [128, 512], BF16) for i in range(3)]
        rl = [T(es, "rl%d" % i, [128, 512], F32) for i in range(2)]
        iqTt = [T(es, "iqTt%d" % i, [128, 4, 128], BF16) for i in range(2)]
        qTt = [T(es, "qTt%d" % i, [128, 4, 128], BF16) for i in range(2)]
        iwt = [T(es, "iwt%d" % i, [128, 8], F32) for i in range(2)]
        maskt = [T(es, "maskt%d" % i, [128, 512], F32) for i in range(2)]
        bs = T(es, "bs", [128, 24], F32)
        steps = T(es, "steps", [128, NIT], F32)
        negm = T(es, "negm", [128, 1], F32)
        Wp = T(es, "Wp", [64, 8, D], BF16)
        Wr = T(es, "Wr", [128, 8, D], BF16)
        Wo = T(es, "Wo", [128, 8, D], BF16)
        OTs8 = T(es, "OTs8", [64, 8, 128], BF16)
        rinv8 = T(es, "rinv8", [128, 8], F32)
        one_t = T(es, "one_t", [128, 1], F32)
        yacc = T(es, "yacc", [128, D], F32)
        zTt = T(es, "zTt", [128, 8, 128], BF16)
        sgat = T(es, "sgat", [128, D], BF16)
        sgbt = T(es, "sgbt", [128, D], BF16)
        t1 = T(es, "t1", [128, D], F32)
        t2 = T(es, "t2", [128, D], F32)
        mg = T(es, "mg", [128, D], BF16)

        fw.dma("sp", negm[:], NMd, writes=["negm"])
        fw.dma("sp", yacc[:], MODd[2], writes=[("yacc", 0), ("yacc", 1)])
        fw.dma("pool", ikT2[:, 0:nb_lim * 128], IKTd[:, 0:nb_lim * 128], writes=["ikT2"])
        fw.op("pool", lambda e: e.memset(one_t[:], 1.0), writes=["one_t"])
        slots = [(t1, [("t", 0)]), (t2, [("t", 1)])] + [
            (scores[:, j * 1024:(j + 1) * 1024], [("sc", 2 * j), ("sc", 2 * j + 1)]) for j in range(8)]
        k = 0
        for h in range(8):
            sap, sres = slots[k % 10]
            fw.dma(("sp", "pool")[k % 2], sap[0:64, :], w_ap[h * 64:(h + 1) * 64, :], writes=sres)
            fw.copy(("act", "dve")[k % 2], Wp[:, h, :], sap[0:64, :], sres, [("Wp", h)])
            k += 1
        for kc in range(8):
            sap, sres = slots[k % 10]
            fw.dma(("sp", "pool")[k % 2], sap[:, :], w_rp[kc * 128:(kc + 1) * 128, :], writes=sres)
            fw.copy(("act", "dve")[k % 2], Wr[:, kc, :], sap[:, :], sres, [("Wr", kc)])
            k += 1
        for kc in range(8):
            sap, sres = slots[k % 10]
            fw.dma(("sp", "pool")[k % 2], sap[:, :], w_o[kc * 128:(kc + 1) * 128, :], writes=sres)
            fw.tt("dve", Wo[:, kc, :], sap[:, :], yacc[:], ALU.mult, sres + [("yacc", 0), ("yacc", 1)], [("Wo", kc)])
            k += 1
        pw = cst[:, 1032:1032 + NIT]
        cnt_ = {"rl": 0, "pt": 0, "kv": 0, "ib": 0, "lb": 0, "mt": 0, "ifl": None}

        def thread_b(i):
            s = i % 2
            nch = i + 1
            nk = 512 * nch
            mbi = mb[s]
            fw.dma("sp", iqTt[s][:].rearrange("p a b -> p (a b)"), IQTd[i], writes=[("iqTt", s)])
            fw.dma("sp", iwt[s][:], IWd[i], writes=[("iwt", s)])
            fw.dma("sp", maskt[s][:], maskd_d[i], writes=[("maskt", s)])
            items = [(ch, h) for ch in range(nch) for h in range(8)]
            ibk = (0, 5)

            def idx_mm(k):
                ch, h = items[k]
                pr, base = h // 2, 64 * (h % 2)
                bk = ibk[k % 2]
                cnt_["ifl"] = bk
                fw.mm(PS[bk][:], iqTt[s][base:base + 64, pr, :], ikT2[base:base + 64, ch * 512:(ch + 1) * 512], True, True,
                      [("iqTt", s), "ikT2"], [("ps", bk)])

            def fma(k, r):
                ch, h = items[k]
                sc_ = scores[:, ch * 512:(ch + 1) * 512]
                if h == 0:
                    fw.ts("dve", sc_, rl[r][:], iwt[s][:, 0:1], None, ALU.mult, None, [("rl", r), ("iwt", s)], [("sc", ch)])
                else:
                    fw.stt(sc_, rl[r][:], iwt[s][:, h:h + 1], sc_, ALU.mult, ALU.add, [("rl", r), ("iwt", s), ("sc", ch)], [("sc", ch)])

            idx_mm(0)
            yield
            prev_r = None
            for k, (ch, h) in enumerate(items):
                bk = ibk[k % 2]
                r = cnt_["rl"] % 2
                cnt_["rl"] += 1
                fw.act(rl[r][:], PS[bk][:], AF.Relu, [("ps", bk)], [("rl", r)])
                if prev_r is not None:
                    fma(k - 1, prev_r)
                prev_r = r
                cnt_["ifl"] = None
                if k + 1 < len(items):
                    idx_mm(k + 1)
                yield
            fma(len(items) - 1, prev_r)
            SC = [("sc", ch) for ch in range(nch)]
            fw.op("dve", lambda e: e.tensor_reduce(out=bs[:, 0:1], in_=scores[:, 0:nk], axis=AX.X, op=ALU.min), SC, ["mn"])
            fw.tt("dve", scores[:, nk - 512:nk], scores[:, nk - 512:nk], maskt[s][:], ALU.add, [("sc", nch - 1), ("maskt", s)], [("sc", nch - 1)])
            yield
            fw.op("dve", lambda e: e.tensor_reduce(out=bs[:, 1:2], in_=scores[:, 0:nk], axis=AX.X, op=ALU.max), SC, ["mx"])
            fw.tt("dve", bs[:, 2:3], bs[:, 1:2], bs[:, 0:1], ALU.subtract, ["mx", "mn"], ["w0"])
            fw.ts("dve", steps[:], pw, bs[:, 2:3], None, ALU.mult, None, ["w0", "cst"], ["steps"])
            fw.tt("dve", bs[:, 5:6], bs[:, 0:1], steps[:, 0:1], ALU.add, ["mn", "steps"], [("mid", 0)])
            yield
            nA = nk if nk <= 1024 else ((nk // 2 + 511) // 512) * 512
            pieces = []
            c0 = 0
            while c0 < nA:
                c1 = min(nA, c0 + 1024)
                pieces.append(("act", c0, c1))
                c0 = c1
            while c0 < nk:
                c1 = min(nk, c0 + 1024)
                pieces.append(("dve", c0, c1))
                c0 = c1
            pa = [p for p in pieces if p[0] == "act"]
            pd = [p for p in pieces if p[0] == "dve"]
            order = []
            for j in range(max(len(pa), len(pd))):
                if j < len(pa):
                    order.append(pa[j] + (8 + j,))
                if j < len(pd):
                    order.append(pd[j] + (14 + j,))
            cthr = float(2 * TOPK - nA) - 0.5
            for k in range(NIT):
                m0, m1 = k % 2, (k + 1) % 2
                for (eng, c0, c1, col) in order:
                    if eng == "act":
                        fw.act(mbi[:, c0:c1], scores[:, c0:c1], AF.Sign, SC + [("mid", m0)], [("mbp", s, c0), ("cntp", col)],
                               bias=bs[:, 5 + m0:6 + m0], scale=-1.0, accum_out=bs[:, col:col + 1])
                    else:
                        fw.op("dve", lambda e, c0=c0, c1=c1, col=col, m0=m0: e.tensor_scalar(
                            out=mbi[:, c0:c1], in0=scores[:, c0:c1], scalar1=bs[:, 5 + m0:6 + m0], scalar2=None,
                            op0=ALU.is_ge, op1=ALU.add, accum_out=bs[:, col:col + 1]),
                            SC + [("mid", m0)], [("mbp", s, c0), ("cntp", col)])
                    yield
                CA = [("cntp", 8 + j) for j in range(len(pa))]
                CD = [("cntp", 14 + j) for j in range(len(pd))]
                if len(pa) > 1:
                    fw.op("dve", lambda e: e.tensor_reduce(out=bs[:, 3:4], in_=bs[:, 8:8 + len(pa)], axis=AX.X, op=ALU.add), CA, ["sA"])
                    sA, sAr = bs[:, 3:4], ["sA"]
                else:
                    sA, sAr = bs[:, 8:9], CA
                if len(pd) == 0:
                    fw.ts("dve", bs[:, 7:8], sA, -1.0, None, ALU.mult, None, sAr, ["comb"])
                else:
                    if len(pd) > 1:
                        fw.op("dve", lambda e: e.tensor_reduce(out=bs[:, 19:20], in_=bs[:, 14:14 + len(pd)], axis=AX.X, op=ALU.add), CD, ["sD"])
                        sD, sDr = bs[:, 19:20], ["sD"]
                    else:
                        sD, sDr = bs[:, 14:15], CD
                    fw.stt(bs[:, 7:8], sD, 2.0, sA, ALU.mult, ALU.subtract, sDr + sAr, ["comb"])
                fw.stt(bs[:, 4:5], bs[:, 7:8], cthr, steps[:, k:k + 1], ALU.is_ge, ALU.mult, ["comb", "steps"], ["incr"])
                kn = min(k + 1, NIT - 1)
                fw.stt(bs[:, 5 + m1:6 + m1], bs[:, 4:5], steps[:, kn:kn + 1], bs[:, 5 + m0:6 + m0], ALU.subtract, ALU.add,
                       ["incr", "steps", ("mid", m0)], [("mid", m1)])
                yield
            mf = NIT % 2
            fw.ts("dve", mbi[:, 0:nk], scores[:, 0:nk], bs[:, 5 + mf:6 + mf], NEGM, ALU.is_lt, ALU.mult, SC + [("mid", mf)],
                  [("mb", s)] + [("mbp", s, p[1]) for p in pieces])
            yield

        def run_all(gen):
            for _ in gen:
                pass

        def thread_a(i, tick):
            s = i % 2
            nkb = 4 * (i + 1)
            mbi = mb[s]
            fw.dma("sp", qTt[s][:].rearrange("p a b -> p (a b)"), QTd[i], writes=[("qTt", s)])
            nc8 = (nkb + 7) // 8
            kvbuf = {}

            def nbk(c8):
                return min(8, nkb - c8 * 8)

            def prep_load(c8):
                kb0, nb_ = c8 * 8, nbk(c8)
                kv = c8 % 2
                fw.dma("sp", KTc[kv][:, :, 0:nb_ * 128], KTd[:, :, kb0 * 128:(kb0 + nb_) * 128].rearrange("q p t -> p q t"), writes=[("KTc", kv)])
                fw.dma("pool", Vc[kv][:, :, 0:nb_ * 130], Vd[:, :, kb0:kb0 + nb_, :].rearrange("q p b c -> p q (b c)"), writes=[("Vc", kv)])

            def prep_mask(c8):
                kb0, nb_ = c8 * 8, nbk(c8)
                mt = c8 % 2
                tbk = 0 if cnt_.get("ifl") == 5 else 5
                t8 = psb(tbk).rearrange("p (a b) -> p a b", b=128)
                for jj in range(nb_):
                    kbg = kb0 + jj
                    fw.tr(t8[:, jj, :], mbi[:, kbg * 128:(kbg + 1) * 128], ident[:], [("mb", s), "ident"], [("ps", tbk)], sig=(jj == nb_ - 1))
                fw.copy("dve", mbT[mt][:, 0:nb_ * 128], psb(tbk)[:, 0:nb_ * 128], [("ps", tbk)], [("mbT", mt)])

            units = [(c8, pr, g) for c8 in range(nc8) for pr in range(4) for g in range(nbk(c8) // 4)]
            LB = {}

            def qk(u):
                c8, pr, g = u
                kv = mt = c8 % 2
                lbs = []
                for a in range(2):
                    lb = 1 + cnt_["lb"] % 4
                    cnt_["lb"] += 1
                    lbs.append(lb)
                    fw.mm(PS[lb][:], ident[:], mbT[mt][:, g * 512:(g + 1) * 512], True, False, [("mbT", mt), "ident"], [("ps", lb)], sig=False)
                for jj in range(4):
                    kbl = g * 4 + jj
                    for a in range(2):
                        base = 64 * a
                        fw.mm(PS[lbs[a]][:, jj * 128:(jj + 1) * 128], KTc[kv][base:base + 64, pr, kbl * 128:(kbl + 1) * 128],
                              qTt[s][base:base + 64, pr, :], False, jj == 3, [("KTc", kv), ("qTt", s)], [("ps", lbs[a])], sig=(jj == 3))
                LB[u] = lbs

            def pv(u):
                c8, pr, g = u
                kv = c8 % 2
                nb_ = nbk(c8)
                lbs = LB.pop(u)
                ps_ = []
                for a in range(2):
                    p = cnt_["pt"] % 3
                    cnt_["pt"] += 1
                    ps_.append(p)
                    fw.act(PT[p][:], PS[lbs[a]][:], AF.Exp, [("ps", lbs[a]), "negm"], [("PT", p)], bias=negm[:, 0:1], scale=0.125)
                for a in range(2):
                    p = ps_[a]
                    for jj in range(4):
                        kbl = g * 4 + jj
                        vcol = kbl * 130 + a * 65
                        fw.mm(PS[6 + a][0:65, 0:128], Vc[kv][:, pr, vcol:vcol + 65], PT[p][:, jj * 128:(jj + 1) * 128],
                              kbl == 0, kbl == nb_ - 1, [("Vc", kv), ("PT", p)], [("ps", 6 + a)], sig=(jj == 3))
                tick()
                tick()
                if g == nb_ // 4 - 1:
                    for a in range(2):
                        h = 2 * pr + a
                        acc = t1[0:65, h * 128:(h + 1) * 128]
                        if c8 == 0:
                            fw.copy("dve", acc, PS[6 + a][0:65, 0:128], [("ps", 6 + a)], [("t", 0)])
                        else:
                            fw.tt("dve", acc, acc, PS[6 + a][0:65, 0:128], ALU.add, [("ps", 6 + a), ("t", 0)], [("t", 0)])

            prep_load(0)
            if nc8 > 1:
                prep_load(1)
            prep_mask(0)
            qk(units[0])
            for idx, u in enumerate(units):
                c8, pr, g = u
                if idx + 1 < len(units):
                    qk(units[idx + 1])
                pv(u)
                first = (pr == 0 and g == 0)
                last = (idx + 1 == len(units)) or units[idx + 1][0] != c8
                if first and c8 + 1 < nc8:
                    prep_mask(c8 + 1)
                if last and c8 + 2 < nc8:
                    prep_load(c8 + 2)
            fw.copy("act", OTs8[:, 0:4, :], t1[0:64, 0:512].rearrange("p (h t) -> p h t", t=128), [("t", 0)], [("OTs8", 0)])
            fw.copy("pool", OTs8[:, 4:8, :], t1[0:64, 512:1024].rearrange("p (h t) -> p h t", t=128), [("t", 0)], [("OTs8", 1)])
            for h in range(8):
                fw.mm(PS[6][:, h:h + 1], t1[64:65, h * 128:(h + 1) * 128], one_t[64:65, 0:1], True, True, [("t", 0), "one_t"], [("ps", 6)], sig=(h == 7))
            fw.op("dve", lambda e: e.reciprocal(out=rinv8[:], in_=PS[6][:, 0:8]), [("ps", 6)], ["rinv8"])
            tick()
            for h in range(8):
                for nn in range(2):
                    yb = 1 + 2 * (h % 2) + nn
                    fw.mm(PS[yb][:], OTs8[:, h, :], Wp[:, h, nn * 512:(nn + 1) * 512], True, True, [("OTs8", h // 4), ("Wp", h)], [("ps", yb)])
                    ya = yacc[:, nn * 512:(nn + 1) * 512]
                    if h == 0:
                        fw.ts("dve", ya, PS[yb][:], rinv8[:, h:h + 1], None, ALU.mult, None, [("ps", yb), "rinv8"], [("yacc", nn)])
                    else:
                        fw.stt(ya, PS[yb][:], rinv8[:, h:h + 1], ya, ALU.mult, ALU.add, [("ps", yb), "rinv8", ("yacc", nn)], [("yacc", nn)])
                tick()
            fw.dma("sp", zTt[:].rearrange("p a b -> p (a b)"), ZTd[i], writes=["zTt"])
            fw.dma("sp", sgat[:], SGAd[i], writes=["sgat"])
            fw.dma("sp", sgbt[:], SGBd[i], writes=["sgbt"])
            for nn in range(2):
                for kc in range(8):
                    fw.mm(PS[1 + nn][:], zTt[:, kc, :], Wr[:, kc, nn * 512:(nn + 1) * 512], kc == 0, kc == 7, ["zTt", ("Wr", kc)], [("ps", 1 + nn)])
                hs = slice(nn * 512, (nn + 1) * 512)
                fw.tt("dve", t1[:, hs], PS[1 + nn][:], sgbt[:, hs], ALU.mult, [("ps", 1 + nn), "sgbt"], [("t", 0)])
                fw.tt("pool", t2[:, hs], yacc[:, hs], sgat[:, hs], ALU.mult, [("yacc", nn), "sgat"], [("t", 1)])
                fw.tt("pool", mg[:, hs], t1[:, hs], t2[:, hs], ALU.add, [("t", 0), ("t", 1)], [("mg", nn)])
                tick()
            tbk = 0 if cnt_.get("ifl") == 5 else 5
            t8 = psb(tbk).rearrange("p (a b) -> p a b", b=128)
            for kc in range(8):
                fw.tr(t8[:, kc, :], mg[:, kc * 128:(kc + 1) * 128], ident[:], [("mg", kc // 4), "ident"], [("ps", tbk)], sig=(kc == 7))
            fw.copy("act", zTt[:], t8, [("ps", tbk)], ["zTt"])
            fw.dma("sp", yacc[:], xo[i * 128:(i + 1) * 128, :], writes=[("yacc", 0), ("yacc", 1)])
            for nn in range(2):
                for kc in range(8):
                    fw.mm(PS[1 + nn][:], zTt[:, kc, :], Wo[:, kc, nn * 512:(nn + 1) * 512], kc == 0, kc == 7, ["zTt", ("Wo", kc)], [("ps", 1 + nn)])
                hs = slice(nn * 512, (nn + 1) * 512)
                fw.tt("dve", t2[:, hs], PS[1 + nn][:], yacc[:, hs], ALU.add, [("ps", 1 + nn), ("yacc", nn)], [("t", 1)])
                tick()
            fw.dma("sp", X1d[i], t2[:], reads=[("t", 1)])

        run_all(thread_b(0))
        for i in range(nown_lim):
            if i + 1 < nown_lim:
                gb = thread_b(i + 1)
                n_b = 8 * (i + 2) + NIT * ((512 * (i + 2) + 1023) // 1024 + 2) + 4
                n_a = 8 * (i + 1) + 8 + 4
                state = {"acc": 0.0, "done": False}

                def tick(gb=gb, state=state, ratio=n_b / n_a):
                    if state["done"]:
                        return
                    state["acc"] += ratio
                    while state["acc"] >= 1.0:
                        state["acc"] -= 1.0
                        try:
                            next(gb)
                        except StopIteration:
                            state["done"] = True
                            return
                thread_a(i, tick)
                run_all(gb)
            else:
                thread_a(i, lambda: None)
        fw.barrier()
    if stop_after <= 3:
        fw.finish()
        return nc

    with ExitStack() as es:
        Wg = T(es, "Wg", [128, 8, DFF], BF16)
        Wu = T(es, "Wu", [128, 8, DFF], BF16)
        Wd = T(es, "Wd", [128, 22, D], BF16)
        A2 = T(es, "A2", [128, D], F32)
        B2 = T(es, "B2", [128, D], F32)
        G2 = T(es, "G2", [128, D], F32)
        FN = T(es, "FN", [128, D], F32)
        fw.dma("pool", A2[:], MODd[3], writes=["A2"])
        fw.dma("pool", B2[:], MODd[4], writes=["B2"])
        fw.dma("pool", G2[:], MODd[5], writes=["G2"])
        fw.dma("pool", FN[:], MODd[6], writes=["FN"])
        if nb_lim == NB:
            for kc in range(8):
                fw.dma(("sp", "pool")[kc % 2], Wg[:, kc, :], WGd[:, kc, :], writes=[("Wg", kc)])
            for kc in range(8):
                fw.dma(("sp", "pool")[kc % 2], Wu[:, kc, :], WUd[:, kc, :], writes=[("Wu", kc)])
            for q4 in range(2):
                fw.dma(("sp", "pool")[q4 % 2], Wd[:, q4 * 11:(q4 + 1) * 11, :], WDd[:, q4 * 11:(q4 + 1) * 11, :], writes=[("Wd", fc) for fc in range(q4 * 11, (q4 + 1) * 11)])
        else:
            with ExitStack() as es_w:
                stg3 = [T(es_w, "stg3_%d" % i, [128, DFF], F32) for i in range(2)]
                k = 0
                for (wsrc, wdst, nm) in ((w_g, Wg, "Wg"), (w_u, Wu, "Wu")):
                    for kc in range(8):
                        s = k % 2
                        fw.dma("sp", stg3[s][:], wsrc[kc * 128:(kc + 1) * 128, :], writes=[("stg3", s)])
                        fw.copy(("act", "dve")[k % 2], wdst[:, kc, :], stg3[s][:], [("stg3", s)], [(nm, kc)])
                        k += 1
                for fc in range(22):
                    s = k % 2
                    fw.dma("sp", stg3[s][:, 0:D], w_d[fc * 128:(fc + 1) * 128, :], writes=[("stg3", s)])
                    fw.copy(("act", "dve")[k % 2], Wd[:, fc, :], stg3[s][:, 0:D], [("stg3", s)], [("Wd", fc)])
                    k += 1
                fw.barrier()
        x1t = [T(es, "x1t%d" % i, [128, D], F32) for i in range(2)]
        h2T = T(es, "h2T", [128, 8, 512], BF16)
        aT = T(es, "aT", [128, 22, 512], BF16)
        tmp3 = T(es, "tmp3", [128, D], F32)
        hb3 = T(es, "hb3", [128, D], BF16)
        t3 = T(es, "t3", [128, D], F32)
        sl = [T(es, "sl%d" % i, [128, 512], F32) for i in range(2)]
        st3 = T(es, "st3", [128, 8], F32)
        ngrp = (nown_lim + 3) // 4

        def norm_group(g):
            nbg = min(4, nown_lim - 4 * g)
            for bi in range(nbg):
                    i = 4 * g + bi
                    s = i % 2
                    fw.dma("sp", x1t[s][:], X1d[i], writes=[("x1t", s)])
                    fw.act(hb3[:], x1t[s][:], AF.Square, [("x1t", s)], ["hb3", "ss"], accum_out=st3[:, 0:1])
                    fw.ts("dve", st3[:, 1:2], st3[:, 0:1], 1.0 / D, EPS, ALU.mult, ALU.add, ["ss"], ["vv"])
                    fw.act(st3[:, 2:3], st3[:, 1:2], AF.Sqrt, ["vv"], ["sd"])
                    fw.op("dve", lambda e: e.reciprocal(out=st3[:, 3:4], in_=st3[:, 2:3]), ["sd"], ["rstd"])
                    fw.stt(tmp3[:], x1t[s][:], st3[:, 3:4], A2[:], ALU.mult, ALU.mult, [("x1t", s), "rstd", "A2"], ["tmp3"])
                    fw.tt("pool", hb3[:], tmp3[:], B2[:], ALU.add, ["tmp3", "B2"], ["hb3"])
                    tb_ = psb(7).rearrange("p (a b) -> p a b", b=128)
                    for kc in range(8):
                        fw.tr(tb_[:, kc, :], hb3[:, kc * 128:(kc + 1) * 128], ident[:], ["hb3", "ident"], [("ps", 7)], sig=(kc == 7))
                    fw.copy("act", h2T[:, :, bi * 128:(bi + 1) * 128], tb_, [("ps", 7)], [("h2T", bi)])

        norm_group(0)
        for g in range(ngrp):
            nbg = min(4, nown_lim - 4 * g)
            NT = nbg * 128
            H2 = [("h2T", bi) for bi in range(nbg)]
            for fc in range(22):
                gb_, ub_ = fc % 2, 2 + fc % 2
                for kc in range(8):
                    fw.mm(PS[gb_][:, 0:NT], Wg[:, kc, fc * 128:(fc + 1) * 128], h2T[:, kc, 0:NT], kc == 0, kc == 7, H2 + [("Wg", kc)], [("ps", gb_)])
                for kc in range(8):
                    fw.mm(PS[ub_][:, 0:NT], Wu[:, kc, fc * 128:(fc + 1) * 128], h2T[:, kc, 0:NT], kc == 0, kc == 7, H2 + [("Wu", kc)], [("ps", ub_)])
                fw.act(sl[fc % 2][:, 0:NT], PS[gb_][:, 0:NT], AF.Silu, [("ps", gb_)], [("sl", fc % 2)])
                fw.tt("dve", aT[:, fc, 0:NT], sl[fc % 2][:, 0:NT], PS[ub_][:, 0:NT], ALU.mult, [("sl", fc % 2), ("ps", ub_)], [("aT", fc)])
            AT = [("aT", fc) for fc in range(22)]
            if g + 1 < ngrp:
                norm_group(g + 1)
            for bi in range(nbg):
                i = 4 * g + bi
                s = i % 2
                fw.dma("sp", x1t[s][:], X1d[i], writes=[("x1t", s)])
                for nn in range(2):
                    db = 4 + nn
                    for fc in range(22):
                        fw.mm(PS[db][:], aT[:, fc, bi * 128:(bi + 1) * 128], Wd[:, fc, nn * 512:(nn + 1) * 512], fc == 0, fc == 21,
                              [("aT", fc), ("Wd", fc)], [("ps", db)])
                    hs = slice(nn * 512, (nn + 1) * 512)
                    fw.tt("dve", t3[:, hs], PS[db][:], G2[:, hs], ALU.mult, [("ps", db), "G2"], [("t3", nn)])
                    fw.tt("pool", t3[:, hs], t3[:, hs], x1t[s][:, hs], ALU.add, [("t3", nn), ("x1t", s)], [("t3", nn)])
                T3 = [("t3", 0), ("t3", 1)]
                fw.act(hb3[:], t3[:], AF.Square, T3, ["hb3", "ss2"], accum_out=st3[:, 4:5])
                fw.ts("dve", st3[:, 5:6], st3[:, 4:5], 1.0 / D, EPS, ALU.mult, ALU.add, ["ss2"], ["vv2"])
                fw.act(st3[:, 6:7], st3[:, 5:6], AF.Sqrt, ["vv2"], ["sd2"])
                fw.op("dve", lambda e: e.reciprocal(out=st3[:, 7:8], in_=st3[:, 6:7]), ["sd2"], ["rstd2"])
                fw.stt(tmp3[:], t3[:], st3[:, 7:8], FN[:], ALU.mult, ALU.mult, T3 + ["rstd2", "FN"], ["tmp3"])
                fw.dma("sp", out_d[i * 128:(i + 1) * 128, :], tmp3[:], reads=["tmp3"])
        fw.barrier()

    fw.finish()
    return nc


def _rope_tab(pos, dim, theta, scale=1.0):
    inv = 1.0 / (theta ** (np.arange(0, dim, 2, dtype=np.float64) / dim))
    ang = pos.astype(np.float64)[:, None] * inv[None, :]
    cs, sn = np.cos(ang) * scale, np.sin(ang) * scale
    return np.concatenate([cs, cs, -sn, sn], axis=1).astype(np.float32)


def make_consts(j):
    cst = {}
    cst["ident"] = np.eye(128, dtype=np.float32).astype(ml_dtypes.bfloat16)
    pos = np.arange(S)
    tA = _rope_tab(pos, 16, 500000.0)
    cst["ropeA"] = np.ascontiguousarray(tA.reshape(NB, 128, 32).transpose(1, 0, 2))
    tRk = _rope_tab(pos, 128, 10000.0, scale=128 ** -0.5)
    tRq = _rope_tab(pos, 128, 10000.0)
    cst["ropeR"] = np.ascontiguousarray(tRk.reshape(NB, 128, 256))
    own = np.array([4 * i + j for i in range(NOWN)])
    cst["ropeAo"] = np.ascontiguousarray(tA.reshape(NB, 128, 32)[own].transpose(1, 0, 2))
    cst["ropeRq"] = np.ascontiguousarray(tRq.reshape(NB, 128, 256)[own])
    cst["ropeRk"] = np.ascontiguousarray(tRk.reshape(NB, 128, 256)[own])
    c = np.zeros((128, 8 + 1024 + 16), np.float32)
    c[:, 1032:1048] = (0.5 ** np.arange(1, 17))[None, :]
    c[:, j] = 1.0
    g = np.array(GAMMA, np.float64)
    p = np.arange(128, dtype=np.float64)
    c[:, 4:8] = (g[None, :] ** (127.0 - p)[:, None])
    qd = g[:, None] ** (p[None, :] + 1.0)
    c[:, 8:520] = qd.reshape(1, 512)
    diff = p[None, :] - p[:, None]
    dm = np.where(diff[None] >= 0, g[:, None, None] ** np.maximum(diff, 0.0)[None], 0.0)
    c[:, 520:1032] = dm.transpose(1, 0, 2).reshape(128, 512)
    cst["cst"] = c
    md = np.zeros((NOWN, 128, 512), np.float32)
    for i in range(NOWN):
        qpos = 128 * (4 * i + j) + np.arange(128)
        kpos = 512 * i + np.arange(512)
        md[i] = np.where(kpos[None, :] <= qpos[:, None], 0.0, -1e30)
    cst["maskd"] = md
    return cst


def make_in_maps(inputs):
    x = np.asarray(inputs["x"], np.float32)
    c = np.asarray(inputs["c"], np.float32)
    f = lambda k: np.ascontiguousarray(np.asarray(inputs[k], np.float32))
    shared = {
        "w_ada": f("w_ada"), "b_ada": f("b_ada").reshape(1, -1),
        "nws": np.stack([f("norm1_w"), f("norm2_w"), f("final_norm_w"), f("gn_w")], 0).reshape(1, 4, D),
        "w_in": f("w_in"), "w_attn_proj": f("w_attn_proj"), "w_ret_proj": f("w_ret_proj"), "w_out": f("w_out"),
        "w_ffn_gate": f("w_ffn_gate"), "w_ffn_up": f("w_ffn_up"), "w_ffn_down": f("w_ffn_down"),
    }
    maps = []
    for core in range(8):
        b, j = core // 4, core % 4
        m = dict(shared)
        m["xf"] = np.ascontiguousarray(x[b])
        m["xo"] = np.ascontiguousarray(x[b].reshape(NOWN, 4, 128, D)[:, j].reshape(NOWN * 128, D))
        m["c_l"] = np.ascontiguousarray(c[b].reshape(8, 128).T)
        m.update(make_consts(j))
        maps.append(m)
    return maps


def kernel(**inputs):
    nc = build_program()
    maps = make_in_maps(inputs)
    res = run_bass_kernel_spmd(nc, maps, core_ids=list(range(8)))
    out = np.zeros((2, S, D), np.float32)
    for core in range(8):
        b, j = core // 4, core % 4
        o = np.asarray(res.results[core]["out"]).reshape(NOWN, 128, D)
        out[b].reshape(NOWN, 4, 128, D)[:, j] = o
    return out
```

```python
import bisect
from contextlib import ExitStack

import numpy as np
import ml_dtypes

import concourse.bass as bass
import concourse.mybir as mybir
from concourse.bass_utils import run_bass_kernel_spmd

F32 = mybir.dt.float32
BF16 = mybir.dt.bfloat16
ALU = mybir.AluOpType
AF = mybir.ActivationFunctionType
AX = mybir.AxisListType

D = 1024
S = 8192
NB = 64
NOWN = 16
DIN = 7240
DFF = 2816
EPS = 1e-6
TOPK = 256
NIT = 13
NEGM = -30000.0
C_QA, C_KA, C_VA, C_IQ, C_IK, C_IW, C_QR, C_KR, C_VR, C_GR, C_GA, C_GB = (
    0, 512, 1024, 1536, 2048, 2112, 2120, 2632, 3144, 4168, 5192, 6216)
GAMMA = [1.0 - 2.0 ** (-5.0 - h) for h in range(4)]


class FW:
    def __init__(self, nc, ndma=8):
        self.nc = nc
        self.E = {"pe": nc.tensor, "act": nc.scalar, "dve": nc.vector, "pool": nc.gpsimd, "sp": nc.sync}
        self.sem, self.cnt, self.seq, self.sigs = {}, {}, {}, {}
        for e in ("pe", "act", "dve", "pool"):
            self.sem[e] = nc.alloc_semaphore("s_" + e)
            self.cnt[e] = 0
            self.seq[e] = 0
            self.sigs[e] = []
        self.dsem, self.duse, self.dnext = {}, {}, {}
        for q in ("sp", "pool", "act"):
            self.dsem[q] = [nc.alloc_semaphore("d_%s%d" % (q, i)) for i in range(ndma)]
            self.duse[q] = [0] * ndma
            self.dnext[q] = 0
        self.seen = {e: {} for e in self.E}
        self.lastw = {}
        self.readers = {}
        self.alldma = []
        self.n = 0

    def _resolve(self, tok):
        if tok[0] == "d":
            return tok[1], tok[2]
        _, eng, seq = tok
        arr = self.sigs[eng]
        i = bisect.bisect_left(arr, (seq, -1))
        if i >= len(arr):
            raise RuntimeError("dependency on unsignaled %s op seq %d" % (eng, seq))
        return self.sem[eng], arr[i][1]

    def _wait(self, eng, tok, kind, dma=False):
        if (not dma) and tok[0] == "c" and tok[1] == eng and eng == "pe":
            return
        sem, val = self._resolve(tok)
        key = sem.num
        if self.seen[eng].get(key, 0) >= val:
            return
        self.E[eng].wait_ge(sem, val)
        self.seen[eng][key] = val

    def op(self, eng, fn, reads=(), writes=(), sig=None, dma=False):
        self.n += 1
        for r in reads:
            w = self.lastw.get(r)
            if w is not None:
                self._wait(eng, w, "raw", dma)
        for r in writes:
            w = self.lastw.get(r)
            if w is not None:
                self._wait(eng, w, "waw", dma)
            for rd in self.readers.get(r, ()):
                self._wait(eng, rd, "war", dma)
        if dma:
            q = eng
            i = self.dnext[q]
            self.dnext[q] = (i + 1) % len(self.dsem[q])
            sem = self.dsem[q][i]
            if self.duse[q][i] > 0:
                self._wait(eng, ("d", sem, 16 * self.duse[q][i]), "raw")
            ins = fn(self.E[eng])
            self.duse[q][i] += 1
            ins.then_inc(sem, 16)
            tok = ("d", sem, 16 * self.duse[q][i])
            self.alldma.append(tok)
        else:
            ins = fn(self.E[eng])
            self.seq[eng] += 1
            if sig is None:
                sig = eng != "pe"
            if sig:
                self.cnt[eng] += 1
                ins.then_inc(self.sem[eng], 1)
                self.sigs[eng].append((self.seq[eng], self.cnt[eng]))
            tok = ("c", eng, self.seq[eng])
        for r in writes:
            self.lastw[r] = tok
            self.readers[r] = []
        for r in reads:
            if r in writes:
                continue
            self.readers.setdefault(r, []).append(tok)
        return ins

    def dma(self, q, out, in_, reads=(), writes=()):
        return self.op(q, lambda e: e.dma_start(out=out, in_=in_), reads, writes, dma=True)

    def barrier(self):
        toks = []
        for e in ("pe", "act", "dve", "pool"):
            if self.seq[e] > 0:
                if not self.sigs[e] or self.sigs[e][-1][0] != self.seq[e]:
                    raise RuntimeError("barrier: last %s op not signaled" % e)
                toks.append(("c", e, self.seq[e]))
        toks += self.alldma
        self.alldma = []
        for eng in self.E:
            for t in toks:
                if t[0] == "c" and t[1] == eng:
                    continue
                self._wait(eng, t, "raw")
        self.lastw = {}
        self.readers = {}

    def finish(self):
        for t in self.alldma:
            self._wait("sp", t, "raw")
        self.alldma = []

    def mm(self, out, lhsT, rhs, start, stop, r, w, sig=None):
        if sig is None:
            sig = stop
        return self.op("pe", lambda e: e.matmul(out, lhsT=lhsT, rhs=rhs, start=start, stop=stop), r, w, sig=sig)

    def tr(self, out, in_, ident, r, w, sig):
        return self.op("pe", lambda e: e.transpose(out=out, in_=in_, identity=ident), r, w, sig=sig)

    def act(self, out, in_, func, r, w, **kw):
        return self.op("act", lambda e: e.activation(out=out, in_=in_, func=func, **kw), r, w)

    def copy(self, eng, out, in_, r, w):
        if eng == "act":
            return self.op("act", lambda e: e.copy(out=out, in_=in_), r, w)
        return self.op(eng, lambda e: e.tensor_copy(out=out, in_=in_), r, w)

    def tt(self, eng, out, in0, in1, op, r, w):
        return self.op(eng, lambda e: e.tensor_tensor(out=out, in0=in0, in1=in1, op=op), r, w)

    def ts(self, eng, out, in0, s1, s2, op0, op1, r, w, **kw):
        if op1 is None:
            return self.op(eng, lambda e: e.tensor_scalar(out=out, in0=in0, scalar1=s1, scalar2=None, op0=op0, **kw), r, w)
        return self.op(eng, lambda e: e.tensor_scalar(out=out, in0=in0, scalar1=s1, scalar2=s2, op0=op0, op1=op1, **kw), r, w)

    def stt(self, out, in0, scalar, in1, op0, op1, r, w):
        return self.op("dve", lambda e: e.scalar_tensor_tensor(out=out, in0=in0, scalar=scalar, in1=in1, op0=op0, op1=op1), r, w)


def build_program(stop_after=99, dbg=False, nb_lim=NB, nown_lim=NOWN):
    nc = bass.Bass("TRN2", target_bir_lowering=False)
    kind_s = "ExternalOutput" if dbg else "Internal"

    def din(name, shape, dt=F32):
        return nc.dram_tensor(name, list(shape), dt, kind="ExternalInput").ap()

    def dscr(name, shape, dt):
        return nc.dram_tensor(name, list(shape), dt, kind=kind_s).ap()

    xf = din("xf", [S, D])
    xo = din("xo", [NOWN * 128, D])
    c_l = din("c_l", [128, 8])
    w_ada = din("w_ada", [D, 6 * D])
    b_ada = din("b_ada", [1, 6 * D])
    nws_d = din("nws", [1, 4, D])
    w_in = din("w_in", [D, DIN])
    w_ap = din("w_attn_proj", [512, D])
    w_rp = din("w_ret_proj", [D, D])
    w_o = din("w_out", [D, D])
    w_g = din("w_ffn_gate", [D, DFF])
    w_u = din("w_ffn_up", [D, DFF])
    w_d = din("w_ffn_down", [DFF, D])
    ident_d = din("ident", [128, 128], BF16)
    ropeA_d = din("ropeA", [128, NB, 32])
    ropeR_d = din("ropeR", [NB, 128, 256])
    ropeAo_d = din("ropeAo", [128, NOWN, 32])
    ropeRq_d = din("ropeRq", [NOWN, 128, 256])
    ropeRk_d = din("ropeRk", [NOWN, 128, 256])
    cst_d = din("cst", [128, 8 + 512 + 512 + 16])
    maskd_d = din("maskd", [NOWN, 128, 512])
    out_d = nc.dram_tensor("out", [NOWN * 128, D], F32, kind="ExternalOutput").ap()

    MODd = dscr("MODd", [7, 128, D], F32)
    KTd = dscr("KTd", [4, 128, S], BF16)
    Vd = dscr("Vd", [4, 128, NB, 130], BF16)
    IKTd = dscr("IKTd", [128, S], BF16)
    Rd = dscr("Rd", [NOWN, 128, D], BF16)
    QTd = dscr("QTd", [NOWN, 128, 512], BF16)
    IQTd = dscr("IQTd", [NOWN, 128, 512], BF16)
    IWd = dscr("IWd", [NOWN, 128, 8], F32)
    ZTd = dscr("ZTd", [NOWN, 128, D], BF16)
    SGAd = dscr("SGAd", [NOWN, 128, D], BF16)
    SGBd = dscr("SGBd", [NOWN, 128, D], BF16)
    NMd = dscr("NMd", [128, 1], F32)
    X1d = dscr("X1d", [NOWN, 128, D], F32)
    WGd = dscr("WGd", [128, 8, DFF], BF16)
    WUd = dscr("WUd", [128, 8, DFF], BF16)
    WDd = dscr("WDd", [128, 22, D], BF16)

    fw = FW(nc)
    top = ExitStack()

    def T(es, name, shape, dt):
        if dbg:
            print("alloc", name, shape, nc.sbuf_bytes_remaining)
        return es.enter_context(nc.sbuf_tensor("sb_" + name, list(shape), dt))

    PS = [top.enter_context(nc.psum_tensor("ps%d" % i, [128, 512], F32)) for i in range(8)]

    def psb(i):
        return PS[i][:].bitcast(BF16)

    ident = T(top, "ident", [128, 128], BF16)
    fw.dma("sp", ident[:], ident_d, writes=["ident"])
    cst = T(top, "cst", [128, 8 + 1024 + 16], F32)
    fw.dma("sp", cst[:], cst_d, writes=["cst"])
    oh = cst[:, 0:4]
    kdec = cst[:, 4:8]
    qdecT = cst[:, 8:520]
    dmaskT = cst[:, 520:1032]

    with ExitStack() as es:
        cl = T(es, "cl", [128, 8], F32)
        sc = T(es, "sc", [128, 8], F32)
        scb = T(es, "scb", [128, 8, 128], F32)
        ones1 = T(es, "ones1", [1, 128], F32)
        bada = T(es, "bada", [1, 6 * D], F32)
        nws = T(es, "nws", [1, 4, D], F32)
        modbc = T(es, "modbc", [128, 6, D], F32)
        nwbc = T(es, "nwbc", [128, 4, D], F32)
        mo = T(es, "mo", [128, 2, D], F32)
        wa = [T(es, "wa%d" % i, [128, 8, 512], F32) for i in range(2)]
        fw.dma("sp", cl[:], c_l, writes=["cl"])
        fw.dma("sp", bada[:], b_ada, writes=["bada"])
        fw.dma("sp", nws[:], nws_d, writes=["nws"])
        fw.op("pool", lambda e: e.memset(ones1[:], 1.0), writes=["ones1"])
        fw.act(sc[:], cl[:], AF.Silu, ["cl"], ["sc"])
        for kc in range(8):
            fw.copy("dve", scb[:, kc, :], sc[:, kc:kc + 1].to_broadcast([128, 128]), ["sc"], [("scb", kc)])
        for ncx in range(12):
            s = ncx % 2
            n0 = ncx * 512
            fw.dma("sp" if s == 0 else "pool", wa[s][:],
                   w_ada[:, n0:n0 + 512].rearrange("(kc p) n -> p kc n", p=128), writes=[("wa", s)])
            pb = PS[s]
            for kc in range(8):
                fw.mm(pb[:], scb[:, kc, :], wa[s][:, kc, :], kc == 0, False, [("scb", kc), ("wa", s)], [("ps", s)])
            fw.mm(pb[:], ones1[0:1, :], bada[0:1, n0:n0 + 512], False, True, ["ones1", "bada"], [("ps", s)])
            fw.copy("act", modbc[:, ncx // 2, (ncx % 2) * 512:(ncx % 2) * 512 + 512], pb[:], [("ps", s)], [("modbc", ncx // 2)])
        for v in range(4):
            for hf in range(2):
                s = hf
                fw.mm(PS[s][:], ones1[0:1, :], nws[0:1, v, hf * 512:hf * 512 + 512], True, True, ["ones1", "nws"], [("ps", s)])
                fw.copy("act", nwbc[:, v, hf * 512:hf * 512 + 512], PS[s][:], [("ps", s)], [("nwbc", v)])
        fw.stt(mo[:, 0, :], modbc[:, 1, :], 1.0, nwbc[:, 0, :], ALU.add, ALU.mult, [("modbc", 1), ("nwbc", 0)], [("mo", 0)])
        fw.stt(mo[:, 1, :], modbc[:, 4, :], 1.0, nwbc[:, 1, :], ALU.add, ALU.mult, [("modbc", 4), ("nwbc", 1)], [("mo", 1)])
        fw.dma("sp", MODd[0], mo[:, 0, :], reads=[("mo", 0)])
        fw.dma("sp", MODd[1], modbc[:, 0, :], reads=[("modbc", 0)])
        fw.dma("sp", MODd[2], modbc[:, 2, :], reads=[("modbc", 2)])
        fw.dma("sp", MODd[3], mo[:, 1, :], reads=[("mo", 1)])
        fw.dma("sp", MODd[4], modbc[:, 3, :], reads=[("modbc", 3)])
        fw.dma("sp", MODd[5], modbc[:, 5, :], reads=[("modbc", 5)])
        fw.dma("sp", MODd[6], nwbc[:, 2, :], reads=[("nwbc", 2)])
        GNd = dscr("GNd", [128, D], F32)
        fw.dma("sp", GNd, nwbc[:, 3, :], reads=[("nwbc", 3)])
        fw.barrier()
    if stop_after <= 0:
        fw.finish()
        return nc

    with ExitStack() as es:
        win = T(es, "win", [128, 8, DIN], BF16)
        A1 = T(es, "A1", [128, D], F32)
        B1 = T(es, "B1", [128, D], F32)
        fw.dma("sp", A1[:], MODd[0], writes=["A1"])
        fw.dma("sp", B1[:], MODd[1], writes=["B1"])
        WIN_R = [("win", kc) for kc in range(8)]

        xt = [T(es, "xt%d" % i, [128, D], F32) for i in range(2)]
        tmpf = T(es, "tmpf", [128, D], F32)
        hb = [T(es, "hb%d" % i, [128, D], BF16) for i in range(2)]
        hT = [T(es, "hT%d" % i, [128, 8, 128], BF16) for i in range(2)]
        st4 = T(es, "st4", [128, 8], F32)
        ropeAo = T(es, "ropeAo", [128, NOWN, 32], F32)
        fw.dma("sp", ropeAo[:], ropeAo_d, writes=["ropeAo"])
        rq = [T(es, "rq%d" % i, [128, 256], F32) for i in range(2)]
        kb = T(es, "kb", [128, 512], BF16)
        rotf = T(es, "rotf", [128, 8, 16], F32)
        tC = T(es, "tC", [128, 512], F32)
        tS = T(es, "tS", [128, 512], F32)
        krf = T(es, "krf", [128, 4, 128], F32)
        krb2 = [T(es, "krb%d" % i, [128, 4, 128], BF16) for i in range(2)]
        krb = krb2[0]
        sq = T(es, "sq", [128, 512], F32)
        ks8 = T(es, "ks8", [128, 8], F32)
        kmx = T(es, "kmx", [128, 8], F32)
        qmx = T(es, "qmx", [128, 8], F32)
        rk = [T(es, "rk%d" % i, [128, 256], F32) for i in range(2)]
        identf = T(es, "identf", [128, 128], F32)
        fw.copy("dve", identf[:], ident[:], ["ident"], ["identf"])
        es_a = ExitStack()
        stg = [T(es_a, "stg%d" % i, [128, 1810], F32) for i in range(2)]
        ropeA = T(es_a, "ropeA", [128, NB, 32], F32)
        rr = [T(es_a, "rr%d" % i, [128, 256], F32) for i in range(2)]
        kT = [T(es_a, "kT%d" % i, [128, 4, 128], BF16) for i in range(2)]
        vx = [T(es_a, "vx%d" % i, [128, 8, 65], BF16) for i in range(2)]
        ikf = T(es_a, "ikf", [128, 64], F32)
        ikb2 = T(es_a, "ikb2", [128, 2, 64], BF16)
        ikT = [T(es_a, "ikT%d" % i, [128, 128], BF16) for i in range(2)]
        vdb2 = [T(es_a, "vdb%d" % i, [128, 4, 256], BF16) for i in range(2)]
        Rst = T(es_a, "Rst", [128, 4, 256], F32)
        Rown = [T(es_a, "Rown%d" % i, [128, D], BF16) for i in range(2)]
        sbufs = [(stg[0][:, 0:905], ("stg", 0, "a")), (stg[0][:, 905:1810], ("stg", 0, "b")),
                 (stg[1][:, 0:905], ("stg", 1, "a")), (stg[1][:, 905:1810], ("stg", 1, "b")),
                 (xt[0][:, 0:905], ("xt", 0)), (xt[1][:, 0:905], ("xt", 1)), (tmpf[:, 0:905], "tmpf")]
        k = 0
        for kc in range(8):
            for part in range(8):
                sap, sres = sbufs[k % 7]
                c0 = part * 905
                fw.dma("sp" if k % 2 == 0 else "pool", sap, w_in[kc * 128:(kc + 1) * 128, c0:c0 + 905], writes=[sres])
                fw.copy(("act", "dve", "act", "dve", "pool")[k % 5], win[:, kc, c0:c0 + 905], sap, [sres], [("win", kc)])
                k += 1
        fw.dma("sp", ropeA[:], ropeA_d, writes=["ropeA"])
        fw.op("pool", lambda e: e.memset(Rst[:], 0.0), writes=["Rst"])
        fw.op("pool", lambda e: e.memset(kmx[:], 0.0), writes=["kmx"])
        fw.op("pool", lambda e: e.memset(qmx[:], 0.0), writes=["qmx"])
        for i in range(2):
            fw.op("pool", lambda e, i=i: e.memset(vx[i][:], 1.0), writes=[("vx", i)])

        def load_x(xsrc, s):
            fw.dma("sp", xt[s][:], xsrc, writes=[("xt", s)])

        def elem(s):
            fw.act(hb[s][:], xt[s][:], AF.Square, [("xt", s)], [("hb", s), "ss"], accum_out=st4[:, 0:1])
            fw.ts("dve", st4[:, 1:2], st4[:, 0:1], 1.0 / D, EPS, ALU.mult, ALU.add, ["ss"], ["vv"])
            fw.act(st4[:, 2:3], st4[:, 1:2], AF.Sqrt, ["vv"], ["sd"])
            fw.op("dve", lambda e: e.reciprocal(out=st4[:, 3:4], in_=st4[:, 2:3]), ["sd"], ["rstd"])
            fw.stt(tmpf[:], xt[s][:], st4[:, 3:4], A1[:], ALU.mult, ALU.mult, [("xt", s), "rstd", "A1"], ["tmpf"])
            fw.tt("pool", hb[s][:], tmpf[:], B1[:], ALU.add, ["tmpf", "B1"], [("hb", s)])

        def trans(s):
            tb_ = psb(0).rearrange("p (a b) -> p a b", b=128)
            for kc in range(8):
                fw.tr(tb_[:, kc, :], hb[s][:, kc * 128:(kc + 1) * 128], ident[:], [("hb", s), "ident"], [("ps", 0)], sig=(kc == 7))
            fw.copy("act", hT[s][:], tb_, [("ps", 0)], [("hT", s)])

        cbank = [1, 2, 3, 7]
        cstate = {"i": 0}

        def proj(s, c0, ncol):
            b = cbank[cstate["i"] % 4]
            cstate["i"] += 1
            for kc in range(8):
                fw.mm(PS[b][:, 0:ncol], hT[s][:, kc, :], win[:, kc, c0:c0 + ncol], kc == 0, kc == 7,
                      [("hT", s), ("win", kc)], [("ps", b)])
            return b

        def rope16(src_f, dst_b, tab, nh, nm, tres):
            CC = tab[:, 0:16].unsqueeze(1).to_broadcast([128, nh, 16])
            SSa = tab[:, 16:24].unsqueeze(1).to_broadcast([128, nh, 8])
            SSb = tab[:, 24:32].unsqueeze(1).to_broadcast([128, nh, 8])
            tCv = tC[:, 0:nh * 16].rearrange("p (h d) -> p h d", d=16)
            tSv = tS[:, 0:nh * 16].rearrange("p (h d) -> p h d", d=16)
            fw.tt("pool", tCv, src_f, CC, ALU.mult, [nm + "_f", tres], ["tC"])
            fw.tt("pool", tSv[:, :, 0:8], src_f[:, :, 8:16], SSa, ALU.mult, [nm + "_f", tres], ["tS"])
            fw.tt("pool", tSv[:, :, 8:16], src_f[:, :, 0:8], SSb, ALU.mult, [nm + "_f", tres], ["tS"])
            fw.tt("pool", dst_b, tCv, tSv, ALU.add, ["tC", "tS"], [nm + "_b"])

        def rope128(src_f, dst_b, tab, nm, tres, fres=None):
            fres = fres or nm
            CC = tab[:, 0:128].unsqueeze(1).to_broadcast([128, 4, 128])
            SSa = tab[:, 128:192].unsqueeze(1).to_broadcast([128, 4, 64])
            SSb = tab[:, 192:256].unsqueeze(1).to_broadcast([128, 4, 64])
            tCv = tC[:].rearrange("p (h d) -> p h d", d=128)
            tSv = tS[:].rearrange("p (h d) -> p h d", d=128)
            fw.tt("pool", tCv, src_f, CC, ALU.mult, [fres + "_f", tres], ["tC"])
            fw.tt("pool", tSv[:, :, 0:64], src_f[:, :, 64:128], SSa, ALU.mult, [fres + "_f", tres], ["tS"])
            fw.tt("pool", tSv[:, :, 64:128], src_f[:, :, 0:64], SSb, ALU.mult, [fres + "_f", tres], ["tS"])
            fw.tt("pool", dst_b, tCv, tSv, ALU.add, ["tC", "tS"], [nm + "_b"])

        def sumsq_max(srcb, acc, accname, nm):
            fw.tt("pool", sq[:], srcb, srcb, ALU.mult, [nm + "_b"], ["sq"])
            fw.op("dve", lambda e: e.tensor_reduce(out=ks8[:], in_=sq[:].rearrange("p (h d) -> p h d", d=64),
                                                   axis=AX.X, op=ALU.add), ["sq"], ["ks8"])
            fw.tt("dve", acc[:], acc[:], ks8[:], ALU.max, ["ks8", accname], [accname])

        load_x(xf[0:128, :], 0)
        if nb_lim > 1:
            load_x(xf[128:256, :], 1)
        fw.dma("pool", rr[0][:], ropeR_d[0], writes=[("rr", 0)])
        elem(0)
        trans(0)
        cvp = []
        for (wsrc, wdst) in ((w_g, WGd), (w_u, WUd)):
            for kc in range(8):
                for hf in range(2):
                    cvp.append((wsrc[kc * 128:(kc + 1) * 128, hf * 1408:(hf + 1) * 1408], wdst[:, kc, hf * 1408:(hf + 1) * 1408], 1408))
        for fc in range(22):
            cvp.append((w_d[fc * 128:(fc + 1) * 128, :], WDd[:, fc, :], D))

        def conv_in(k):
            if k < len(cvp):
                wsrc_, wdst_, n_ = cvp[k]
                fw.dma("sp", stg[0][:, 0:n_], wsrc_, writes=[("stg", 0), ("stg", 0, "a"), ("stg", 0, "b")])

        def conv_cast(k):
            if k < len(cvp):
                wsrc_, wdst_, n_ = cvp[k]
                fw.copy("dve", stg[1][:].bitcast(BF16)[:, 0:n_], stg[0][:, 0:n_], [("stg", 0)], [("stg", 1), ("stg", 1, "a"), ("stg", 1, "b")])

        def conv_out(k):
            if k < len(cvp):
                wsrc_, wdst_, n_ = cvp[k]
                fw.dma("sp", wdst_, stg[1][:].bitcast(BF16)[:, 0:n_], reads=[("stg", 1)])

        if nb_lim == NB:
            conv_in(0)
        upending = []
        for tb in range(nb_lim):
            s = tb % 2
            if tb + 1 < nb_lim:
                fw.dma("pool", rr[1 - s][:], ropeR_d[tb + 1], writes=[("rr", 1 - s)])
                elem(1 - s)
            b = proj(s, C_KA, 512)
            fw.copy("act", kb[:], PS[b][:], [("ps", b)], ["k_b"])
            fw.copy("act", rotf[:], PS[b][:].rearrange("p (h d) -> p h d", d=64)[:, :, 0:16], [("ps", b)], ["k_f"])
            rope16(rotf[:], kb[:].rearrange("p (h d) -> p h d", d=64)[:, :, 0:16], ropeA[:, tb, :], 8, "k", "ropeA")
            sumsq_max(kb[:], kmx, "kmx", "k")
            for fn_ in upending:
                fn_()
            upending = []
            b = proj(s, C_VA, 512)
            fw.copy("act", vx[s][:, :, 0:64], PS[b][:].rearrange("p (h d) -> p h d", d=64), [("ps", b)], [("vx", s)])
            fw.dma("sp", Vd[:, :, tb, :].rearrange("q p c -> p q c"), vx[s][:].rearrange("p (q a) c -> p q (a c)", a=2),
                   reads=[("vx", s)])
            b = proj(s, C_IK, 64)
            fw.copy("act", ikf[:], PS[b][:, 0:64], [("ps", b)], ["ik_f"])
            fw.copy("act", ikb2[:], PS[b][:, 0:64].unsqueeze(1).to_broadcast([128, 2, 64]), [("ps", b)], ["ik_b"])
            rope16(ikf[:, 0:16].unsqueeze(1).to_broadcast([128, 2, 16]), ikb2[:, :, 0:16], ropeA[:, tb, :], 2, "ik", "ropeA")
            b = proj(s, C_KR, 512)
            fw.copy("act", krf[:], PS[b][:].rearrange("p (h d) -> p h d", d=128), [("ps", b)], ["kr_f"])
            rope128(krf[:], krb2[s][:], rr[s], "kr%d" % s, ("rr", s), fres="kr")
            for half in range(2):
                b = proj(s, C_VR + half * 512, 512)
                for hh in range(2):
                    h = half * 2 + hh
                    fw.act(vdb2[s][:, h, :], PS[b][:, hh * 256:(hh + 1) * 256], AF.Copy, [("ps", b), "cst"], [("vdb", s, h)],
                           scale=kdec[:, h:h + 1])
            if tb + 2 < nb_lim:
                load_x(xf[(tb + 2) * 128:(tb + 3) * 128, :], s)
            t2 = psb(4).rearrange("p (a b) -> p a b", b=128)
            for pr in range(4):
                fw.tr(t2[:, pr, :], kb[:, pr * 128:(pr + 1) * 128], ident[:], ["k_b", "ident"], [("ps", 4)], sig=False)
            fw.tr(t2[:, 4, :], ikb2[:].rearrange("p a d -> p (a d)"), ident[:], ["ik_b", "ident"], [("ps", 4)], sig=True)
            if nb_lim == NB:
                conv_cast(tb)
            fw.copy("dve", kT[s][:], t2[:, 0:4, :], [("ps", 4)], [("kT", s)])
            fw.copy("dve", ikT[s][:], t2[:, 4, :], [("ps", 4)], [("ikT", s)])
            fw.dma("sp", KTd[:, :, tb * 128:(tb + 1) * 128].rearrange("q p t -> p q t"), kT[s][:], reads=[("kT", s)])
            fw.dma("sp", IKTd[:, tb * 128:(tb + 1) * 128], ikT[s][:], reads=[("ikT", s)])
            if nb_lim == NB:
                conv_out(tb)
                conv_in(tb + 1)
            def ustep(tb=tb, s=s):
                g, m = tb // 4, tb % 4
                rs = g % 2
                Rflat = Rst[:].rearrange("p h e -> p (h e)")
                if m == 0:
                    fw.ts("dve", Rown[rs][:], Rflat, oh[:, 0:1], None, ALU.mult, None, ["Rst", "cst"], [("Rown", rs)])
                else:
                    fw.stt(Rown[rs][:], Rflat, oh[:, m:m + 1], Rown[rs][:], ALU.mult, ALU.add, ["Rst", "cst", ("Rown", rs)], [("Rown", rs)])
                if m == 3:
                    fw.dma("sp", Rd[g], Rown[rs][:], reads=[("Rown", rs)])
                for h in range(4):
                    ub = 5 + h // 2
                    fw.mm(PS[ub][:, (h % 2) * 256:(h % 2) * 256 + 256], krb2[s][:, h, :], vdb2[s][:, h, :], True, True,
                          ["kr%d_b" % s, ("vdb", s, h)], [("ps", ub)])
                for h in range(4):
                    ub = 5 + h // 2
                    fw.stt(Rst[:, h, :], Rst[:, h, :], float(GAMMA[h] ** 128), PS[ub][:, (h % 2) * 256:(h % 2) * 256 + 256],
                           ALU.mult, ALU.add, ["Rst", ("ps", ub)], ["Rst"])
            upending.append(ustep)
            if tb + 1 < nb_lim:
                trans(1 - s)
        for fn_ in upending:
            fn_()
        fw.barrier()
        es_a.close()
        if stop_after <= 1:
            fw.finish()
            return nc

        with ExitStack() as es_b:
            gnbc = T(es_b, "gnbc", [128, D], F32)
            fw.dma("sp", gnbc[:], GNd, writes=["gnbc"])
            iqb = T(es_b, "iqb", [128, 512], BF16)
            rotf2 = T(es_b, "rotf2", [128, 8, 16], F32)
            qT = [T(es_b, "qT%d" % i, [128, 4, 128], BF16) for i in range(2)]
            iqT = [T(es_b, "iqT%d" % i, [128, 4, 128], BF16) for i in range(2)]
            iwf = [T(es_b, "iwf%d" % i, [128, 8], F32) for i in range(2)]
            qrf = T(es_b, "qrf", [128, 4, 128], F32)
            qrb = T(es_b, "qrb", [128, 4, 128], BF16)
            qrT = T(es_b, "qrT", [128, 4, 128], BF16)
            qxT = T(es_b, "qxT", [128, 4, 128], BF16)
            krT = T(es_b, "krT", [128, 4, 128], BF16)
            vrb = T(es_b, "vrb", [128, 4, 256], BF16)
            innT = T(es_b, "innT", [128, 4, 128], BF16)
            rb = [T(es_b, "rb%d" % i, [128, D], BF16) for i in range(2)]
            bst = T(es_b, "bst", [128, 4, 6], F32)
            mv = T(es_b, "mv", [128, 4, 2], F32)
            gv = T(es_b, "gv", [128, 12], F32)
            yn = T(es_b, "yn", [128, D], F32)
            sg = T(es_b, "sg", [128, D], BF16)
            zb = T(es_b, "zb", [128, D], BF16)
            zT = [T(es_b, "zT%d" % i, [128, 8, 128], BF16) for i in range(2)]
            sga = [T(es_b, "sga%d" % i, [128, D], BF16) for i in range(2)]
            sgb = [T(es_b, "sgb%d" % i, [128, D], BF16) for i in range(2)]

            def tr4(src_b, nm):
                t2 = psb(4).rearrange("p (a b) -> p a b", b=128)
                for pr in range(4):
                    fw.tr(t2[:, pr, :], src_b[:, pr * 128:(pr + 1) * 128], ident[:], [nm, "ident"], [("ps", 4)], sig=(pr == 3))
                return t2[:, 0:4, :]

            load_x(xo[0:128, :], 0)
            if nown_lim > 1:
                load_x(xo[128:256, :], 1)
            fw.dma("pool", rq[0][:], ropeRq_d[0], writes=[("rq", 0)])
            fw.dma("pool", rk[0][:], ropeRk_d[0], writes=[("rk", 0)])
            elem(0)
            trans(0)
            pending = []
            for i in range(nown_lim):
                s = i % 2
                if i + 1 < nown_lim:
                    fw.dma("pool", rq[1 - s][:], ropeRq_d[i + 1], writes=[("rq", 1 - s)])
                    fw.dma("pool", rk[1 - s][:], ropeRk_d[i + 1], writes=[("rk", 1 - s)])
                    elem(1 - s)
                fw.dma("sp", rb[s][:], Rd[i], writes=[("rb", s)])
                b = proj(s, C_QA, 512)
                fw.copy("act", kb[:], PS[b][:], [("ps", b)], ["k_b"])
                fw.copy("act", rotf[:], PS[b][:].rearrange("p (h d) -> p h d", d=64)[:, :, 0:16], [("ps", b)], ["k_f"])
                rope16(rotf[:], kb[:].rearrange("p (h d) -> p h d", d=64)[:, :, 0:16], ropeAo[:, i, :], 8, "k", "ropeAo")
                sumsq_max(kb[:], qmx, "qmx", "k")
                b = proj(s, C_IQ, 512)
                fw.copy("act", iqb[:], PS[b][:], [("ps", b)], ["iq_b"])
                fw.copy("act", rotf2[:], PS[b][:].rearrange("p (h d) -> p h d", d=64)[:, :, 0:16], [("ps", b)], ["iq_f"])
                rope16(rotf2[:], iqb[:].rearrange("p (h d) -> p h d", d=64)[:, :, 0:16], ropeAo[:, i, :], 8, "iq", "ropeAo")
                b = proj(s, C_IW, 8)
                fw.op("act", lambda e, b=b, s=s: e.mul(out=iwf[s][:], in_=PS[b][:, 0:8], mul=float(8 ** -0.5 * 64 ** -0.5)),
                      [("ps", b)], [("iwf", s)])
                fw.dma("sp", IWd[i], iwf[s][:], reads=[("iwf", s)])
                b = proj(s, C_QR, 512)
                fw.copy("act", qrf[:], PS[b][:].rearrange("p (h d) -> p h d", d=128), [("ps", b)], ["qr_f"])
                rope128(qrf[:], qrb[:], rq[s], "qr", ("rq", s))
                b = proj(s, C_KR, 512)
                fw.copy("act", krf[:], PS[b][:].rearrange("p (h d) -> p h d", d=128), [("ps", b)], ["kr_f"])
                rope128(krf[:], krb[:], rk[s], "kr", ("rk", s))
                for half in range(2):
                    b = proj(s, C_VR + half * 512, 512)
                    fw.copy("act", vrb[:, 2 * half:2 * half + 2, :], PS[b][:].rearrange("p (h e) -> p h e", e=256), [("ps", b)], [("vrb", half)])
                if i + 2 < nown_lim:
                    load_x(xo[(i + 2) * 128:(i + 3) * 128, :], s)
                for fn_ in pending:
                    fn_()
                pending = []
                tv = tr4(kb, "k_b")
                fw.copy("dve", qT[s][:], tv, [("ps", 4)], [("qT", s)])
                fw.dma("sp", QTd[i], qT[s][:].rearrange("p a b -> p (a b)"), reads=[("qT", s)])
                tv = tr4(iqb, "iq_b")
                fw.copy("dve", iqT[s][:], tv, [("ps", 4)], [("iqT", s)])
                fw.dma("sp", IQTd[i], iqT[s][:].rearrange("p a b -> p (a b)"), reads=[("iqT", s)])
                for half in range(2):
                    b = proj(s, C_GR + half * 512, 512)
                    fw.act(sg[:, half * 512:(half + 1) * 512], PS[b][:], AF.Silu, [("ps", b)], [("sg", half)])
                tv = tr4(qrb[:].rearrange("p h d -> p (h d)"), "qr_b")
                fw.copy("dve", qrT[:], tv, [("ps", 4)], ["qrT"])
                fw.tt("dve", qxT[:], tv, qdecT.rearrange("p (h t) -> p h t", t=128), ALU.mult, [("ps", 4), "cst"], ["qxT"])
                tv = tr4(krb[:].rearrange("p h d -> p (h d)"), "kr_b")
                fw.copy("dve", krT[:], tv, [("ps", 4)], ["krT"])
                for half in range(2):
                    b = proj(s, C_GA + half * 512, 512)
                    fw.act(sga[s][:, half * 512:(half + 1) * 512], PS[b][:], AF.Sigmoid, [("ps", b)], [("sga", s, half)])
                fw.dma("sp", SGAd[i], sga[s][:], reads=[("sga", s, 0), ("sga", s, 1)])
                b = cbank[cstate["i"] % 4]
                cstate["i"] += 1
                for h in range(4):
                    fw.mm(PS[b][:, h * 128:(h + 1) * 128], krT[:, h, :], qrT[:, h, :], True, True, ["krT", "qrT"], [("ps", b)], sig=(h == 3))
                fw.tt("dve", innT[:], PS[b][:].rearrange("p (h t) -> p h t", t=128), dmaskT.rearrange("p (h t) -> p h t", t=128),
                      ALU.mult, [("ps", b), "cst"], ["innT"])
                for half in range(2):
                    b = proj(s, C_GB + half * 512, 512)
                    fw.act(sgb[s][:, half * 512:(half + 1) * 512], PS[b][:], AF.Sigmoid, [("ps", b)], [("sgb", s, half)])
                fw.dma("sp", SGBd[i], sgb[s][:], reads=[("sgb", s, 0), ("sgb", s, 1)])
                for h in range(4):
                    ob = 5 + h // 2
                    osl = PS[ob][:, (h % 2) * 256:(h % 2) * 256 + 256]
                    fw.mm(osl, innT[:, h, :], vrb[:, h, :], True, False, ["innT", ("vrb", h // 2)], [("ps", ob)], sig=False)
                    fw.mm(osl, qxT[:, h, :], rb[s][:, h * 256:(h + 1) * 256], False, True, ["qxT", ("rb", s)], [("ps", ob)], sig=True)
                if i + 1 < nown_lim:
                    trans(1 - s)
                for h in range(4):
                    ob = 5 + h // 2
                    osl = PS[ob][:, (h % 2) * 256:(h % 2) * 256 + 256]
                    fw.op("dve", lambda e, h=h, osl=osl: e.bn_stats(out=bst[:, h, :], in_=osl), [("ps", ob)], [("bst", h)])
                    fw.op("dve", lambda e, h=h: e.bn_aggr(out=mv[:, h, :], in_=bst[:, h, :]), [("bst", h)], [("mv", h)])
                MV = [("mv", h) for h in range(4)]
                fw.ts("dve", gv[:, 0:4], mv[:, :, 1], EPS, None, ALU.add, None, MV, ["gv0"])
                fw.act(gv[:, 4:8], gv[:, 0:4], AF.Sqrt, ["gv0"], ["gv1"])
                fw.op("dve", lambda e: e.reciprocal(out=gv[:, 8:12], in_=gv[:, 4:8]), ["gv1"], ["gv2"])
                for h in range(4):
                    ob = 5 + h // 2
                    osl = PS[ob][:, (h % 2) * 256:(h % 2) * 256 + 256]
                    fw.ts("dve", yn[:, h * 256:(h + 1) * 256], osl, mv[:, h, 0:1], gv[:, 8 + h:9 + h], ALU.subtract, ALU.mult,
                          [("ps", ob), ("mv", h), "gv2"], [("yn", h)])
                YN = [("yn", h) for h in range(4)]
                fw.tt("dve", yn[:], yn[:], gnbc[:], ALU.mult, YN + ["gnbc"], YN)
                fw.tt("dve", zb[:], yn[:], sg[:], ALU.mult, YN + [("sg", 0), ("sg", 1)], ["zb"])

                def ztail(i=i, s=s):
                    t8 = psb(4).rearrange("p (a b) -> p a b", b=128)
                    for kc in range(8):
                        fw.tr(t8[:, kc, :], zb[:, kc * 128:(kc + 1) * 128], ident[:], ["zb", "ident"], [("ps", 4)], sig=(kc == 7))
                    fw.copy("dve", zT[s][:], t8, [("ps", 4)], [("zT", s)])
                    fw.dma("sp", ZTd[i], zT[s][:].rearrange("p a b -> p (a b)"), reads=[("zT", s)])
                pending.append(ztail)
            for fn_ in pending:
                fn_()

            m2 = T(es_b, "m2", [128, 2], F32)
            r2 = T(es_b, "r2", [1, 8], F32)
            onesr = T(es_b, "onesr", [1, 128], F32)
            nmt = T(es_b, "nmt", [128, 1], F32)
            fw.op("pool", lambda e: e.memset(onesr[:], 1.0), writes=["onesr"])
            fw.op("dve", lambda e: e.tensor_reduce(out=m2[:, 0:1], in_=qmx[:], axis=AX.X, op=ALU.max), ["qmx"], ["m2a"])
            fw.op("dve", lambda e: e.tensor_reduce(out=m2[:, 1:2], in_=kmx[:], axis=AX.X, op=ALU.max), ["kmx"], ["m2b"])
            fw.tr(PS[1][0:1, 0:128], m2[:, 0:1], identf[:], ["m2a", "identf"], [("ps", 1)], sig=False)
            fw.tr(PS[1][0:1, 128:256], m2[:, 1:2], identf[:], ["m2b", "identf"], [("ps", 1)], sig=True)
            fw.op("dve", lambda e: e.tensor_reduce(out=r2[:, 0:2], in_=PS[1][0:1, 0:256].rearrange("p (a t) -> p a t", t=128),
                                                   axis=AX.X, op=ALU.max), [("ps", 1)], ["r2a"])
            fw.tt("dve", r2[:, 2:3], r2[:, 0:1], r2[:, 1:2], ALU.mult, ["r2a"], ["r2b"])
            fw.act(r2[:, 3:4], r2[:, 2:3], AF.Sqrt, ["r2b"], ["r2c"])
            fw.ts("dve", r2[:, 4:5], r2[:, 3:4], -0.125, None, ALU.mult, None, ["r2c"], ["r2d"])
            fw.mm(PS[2][:, 0:1], onesr[0:1, :], r2[0:1, 4:5], True, True, ["onesr", "r2d"], [("ps", 2)])
            fw.copy("dve", nmt[:], PS[2][:, 0:1], [("ps", 2)], ["nmt"])
            fw.dma("sp", NMd, nmt[:], reads=["nmt"])
            fw.barrier()
    if stop_after <= 2:
        fw.finish()
        return nc

    with ExitStack() as es:
        scores = T(es, "scores", [128, S], F32)
        mb = [T(es, "mb%d" % i, [128, S], BF16) for i in range(2)]
        ikT2 = T(es, "ikT2", [128, S], BF16)
        KTc = [T(es, "KTc%d" % i, [128, 4, 1024], BF16) for i in range(2)]
        Vc = [T(es, "Vc%d" % i, [128, 4, 8 * 130], BF16) for i in range(2)]
        mbT = [T(es, "mbT%d" % i, [128, 1024], BF16) for i in range(2)]
        PT = [T(es, "PT%d" % i, [128, 512], BF16) for i in range(3)]
        rl = [T(es, "rl%d" % i, [128, 512], F32) for i in range(2)]
        iqTt = [T(es, "iqTt%d" % i, [128, 4, 128], BF16) for i in range(2)]
        qTt = [T(es, "qTt%d" % i, [128, 4, 128], BF16) for i in range(2)]
        iwt = [T(es, "iwt%d" % i, [128, 8], F32) for i in range(2)]
        maskt = [T(es, "maskt%d" % i, [128, 512], F32) for i in range(2)]
        bs = T(es, "bs", [128, 20], F32)
        steps = T(es, "steps", [128, NIT], F32)
        negm = T(es, "negm", [128, 1], F32)
        Wp = T(es, "Wp", [64, 8, D], BF16)
        Wr = T(es, "Wr", [128, 8, D], BF16)
        Wo = T(es, "Wo", [128, 8, D], BF16)
        OTs8 = T(es, "OTs8", [64, 8, 128], BF16)
        rinv8 = T(es, "rinv8", [128, 8], F32)
        one_t = T(es, "one_t", [128, 1], F32)
        yacc = T(es, "yacc", [128, D], F32)
        zTt = T(es, "zTt", [128, 8, 128], BF16)
        sgat = T(es, "sgat", [128, D], BF16)
        sgbt = T(es, "sgbt", [128, D], BF16)
        t1 = T(es, "t1", [128, D], F32)
        t2 = T(es, "t2", [128, D], F32)
        mg = T(es, "mg", [128, D], BF16)

        fw.dma("sp", negm[:], NMd, writes=["negm"])
        fw.dma("sp", yacc[:], MODd[2], writes=[("yacc", 0), ("yacc", 1)])
        fw.dma("pool", ikT2[:, 0:nb_lim * 128], IKTd[:, 0:nb_lim * 128], writes=["ikT2"])
        fw.op("pool", lambda e: e.memset(one_t[:], 1.0), writes=["one_t"])
        slots = [(t1, [("t", 0)]), (t2, [("t", 1)])] + [
            (scores[:, j * 1024:(j + 1) * 1024], [("sc", 2 * j), ("sc", 2 * j + 1)]) for j in range(8)]
        k = 0
        for h in range(8):
            sap, sres = slots[k % 10]
            fw.dma(("sp", "pool")[k % 2], sap[0:64, :], w_ap[h * 64:(h + 1) * 64, :], writes=sres)
            fw.copy(("act", "dve")[k % 2], Wp[:, h, :], sap[0:64, :], sres, [("Wp", h)])
            k += 1
        for kc in range(8):
            sap, sres = slots[k % 10]
            fw.dma(("sp", "pool")[k % 2], sap[:, :], w_rp[kc * 128:(kc + 1) * 128, :], writes=sres)
            fw.copy(("act", "dve")[k % 2], Wr[:, kc, :], sap[:, :], sres, [("Wr", kc)])
            k += 1
        for kc in range(8):
            sap, sres = slots[k % 10]
            fw.dma(("sp", "pool")[k % 2], sap[:, :], w_o[kc * 128:(kc + 1) * 128, :], writes=sres)
            fw.tt("dve", Wo[:, kc, :], sap[:, :], yacc[:], ALU.mult, sres + [("yacc", 0), ("yacc", 1)], [("Wo", kc)])
            k += 1
        pw = cst[:, 1032:1032 + NIT]
        cnt_ = {"rl": 0, "pt": 0, "kv": 0, "ib": 0, "lb": 0, "mt": 0, "ifl": None}

        def thread_b(i):
            s = i % 2
            nch = i + 1
            nk = 512 * nch
            mbi = mb[s]
            fw.dma("sp", iqTt[s][:].rearrange("p a b -> p (a b)"), IQTd[i], writes=[("iqTt", s)])
            fw.dma("sp", iwt[s][:], IWd[i], writes=[("iwt", s)])
            fw.dma("sp", maskt[s][:], maskd_d[i], writes=[("maskt", s)])
            items = [(ch, h) for ch in range(nch) for h in range(8)]
            ibk = (0, 5)

            def idx_mm(k):
                ch, h = items[k]
                pr, base = h // 2, 64 * (h % 2)
                bk = ibk[k % 2]
                cnt_["ifl"] = bk
                fw.mm(PS[bk][:], iqTt[s][base:base + 64, pr, :], ikT2[base:base + 64, ch * 512:(ch + 1) * 512], True, True,
                      [("iqTt", s), "ikT2"], [("ps", bk)])

            def fma(k, r):
                ch, h = items[k]
                sc_ = scores[:, ch * 512:(ch + 1) * 512]
                if h == 0:
                    fw.ts("dve", sc_, rl[r][:], iwt[s][:, 0:1], None, ALU.mult, None, [("rl", r), ("iwt", s)], [("sc", ch)])
                else:
                    fw.stt(sc_, rl[r][:], iwt[s][:, h:h + 1], sc_, ALU.mult, ALU.add, [("rl", r), ("iwt", s), ("sc", ch)], [("sc", ch)])

            idx_mm(0)
            yield
            prev_r = None
            for k, (ch, h) in enumerate(items):
                bk = ibk[k % 2]
                r = cnt_["rl"] % 2
                cnt_["rl"] += 1
                fw.act(rl[r][:], PS[bk][:], AF.Relu, [("ps", bk)], [("rl", r)])
                if prev_r is not None:
                    fma(k - 1, prev_r)
                prev_r = r
                cnt_["ifl"] = None
                if k + 1 < len(items):
                    idx_mm(k + 1)
                yield
            fma(len(items) - 1, prev_r)
            SC = [("sc", ch) for ch in range(nch)]
            fw.op("dve", lambda e: e.tensor_reduce(out=bs[:, 0:1], in_=scores[:, 0:nk], axis=AX.X, op=ALU.min), SC, ["mn"])
            fw.tt("dve", scores[:, nk - 512:nk], scores[:, nk - 512:nk], maskt[s][:], ALU.add, [("sc", nch - 1), ("maskt", s)], [("sc", nch - 1)])
            yield
            fw.op("dve", lambda e: e.tensor_reduce(out=bs[:, 1:2], in_=scores[:, 0:nk], axis=AX.X, op=ALU.max), SC, ["mx"])
            fw.tt("dve", bs[:, 2:3], bs[:, 1:2], bs[:, 0:1], ALU.subtract, ["mx", "mn"], ["w0"])
            fw.ts("dve", steps[:], pw, bs[:, 2:3], None, ALU.mult, None, ["w0", "cst"], ["steps"])
            fw.tt("dve", bs[:, 5:6], bs[:, 0:1], steps[:, 0:1], ALU.add, ["mn", "steps"], [("mid", 0)])
            yield
            nA = nk if nk <= 1024 else ((nk // 2 + 511) // 512) * 512
            pieces = []
            c0 = 0
            while c0 < nA:
                c1 = min(nA, c0 + 2048)
                pieces.append(("act", c0, c1))
                c0 = c1
            while c0 < nk:
                c1 = min(nk, c0 + 2048)
                pieces.append(("dve", c0, c1))
                c0 = c1
            pa = [p for p in pieces if p[0] == "act"]
            pd = [p for p in pieces if p[0] == "dve"]
            order = []
            for j in range(len(pa)):
                order.append(pa[j] + (8 + j,))
            for j in range(len(pd)):
                order.append(pd[j] + (12 + j,))
            cthr = float(2 * TOPK - nA) - 0.5
            for k in range(NIT):
                m0, m1 = k % 2, (k + 1) % 2
                for (eng, c0, c1, col) in order:
                    if eng == "act":
                        fw.act(mbi[:, c0:c1], scores[:, c0:c1], AF.Sign, SC + [("mid", m0)], [("mbp", s, c0), ("cntp", col)],
                               bias=bs[:, 5 + m0:6 + m0], scale=-1.0, accum_out=bs[:, col:col + 1])
                    else:
                        fw.op("dve", lambda e, c0=c0, c1=c1, col=col, m0=m0: e.tensor_scalar(
                            out=mbi[:, c0:c1], in0=scores[:, c0:c1], scalar1=bs[:, 5 + m0:6 + m0], scalar2=None,
                            op0=ALU.is_ge, op1=ALU.add, accum_out=bs[:, col:col + 1]),
                            SC + [("mid", m0)], [("mbp", s, c0), ("cntp", col)])
                    yield
                CA = [("cntp", 8 + j) for j in range(len(pa))]
                CD = [("cntp", 12 + j) for j in range(len(pd))]
                if len(pa) > 1:
                    fw.op("dve", lambda e: e.tensor_reduce(out=bs[:, 3:4], in_=bs[:, 8:8 + len(pa)], axis=AX.X, op=ALU.add), CA, ["sA"])
                    sA, sAr = bs[:, 3:4], ["sA"]
                else:
                    sA, sAr = bs[:, 8:9], CA
                if len(pd) == 0:
                    fw.ts("dve", bs[:, 7:8], sA, -1.0, None, ALU.mult, None, sAr, ["comb"])
                else:
                    if len(pd) > 1:
                        fw.op("dve", lambda e: e.tensor_reduce(out=bs[:, 6 + 10:7 + 10], in_=bs[:, 12:12 + len(pd)], axis=AX.X, op=ALU.add), CD, ["sD"])
                        sD, sDr = bs[:, 16:17], ["sD"]
                    else:
                        sD, sDr = bs[:, 12:13], CD
                    fw.stt(bs[:, 7:8], sD, 2.0, sA, ALU.mult, ALU.subtract, sDr + sAr, ["comb"])
                fw.stt(bs[:, 4:5], bs[:, 7:8], cthr, steps[:, k:k + 1], ALU.is_ge, ALU.mult, ["comb", "steps"], ["incr"])
                kn = min(k + 1, NIT - 1)
                fw.stt(bs[:, 5 + m1:6 + m1], bs[:, 4:5], steps[:, kn:kn + 1], bs[:, 5 + m0:6 + m0], ALU.subtract, ALU.add,
                       ["incr", "steps", ("mid", m0)], [("mid", m1)])
                yield
            mf = NIT % 2
            fw.ts("dve", mbi[:, 0:nk], scores[:, 0:nk], bs[:, 5 + mf:6 + mf], NEGM, ALU.is_lt, ALU.mult, SC + [("mid", mf)],
                  [("mb", s)] + [("mbp", s, p[1]) for p in pieces])
            yield

        def run_all(gen):
            for _ in gen:
                pass

        def thread_a(i, tick):
            s = i % 2
            nkb = 4 * (i + 1)
            mbi = mb[s]
            fw.dma("sp", qTt[s][:].rearrange("p a b -> p (a b)"), QTd[i], writes=[("qTt", s)])
            nc8 = (nkb + 7) // 8
            kvbuf = {}

            def nbk(c8):
                return min(8, nkb - c8 * 8)

            def prep_load(c8):
                kb0, nb_ = c8 * 8, nbk(c8)
                kv = c8 % 2
                fw.dma("sp", KTc[kv][:, :, 0:nb_ * 128], KTd[:, :, kb0 * 128:(kb0 + nb_) * 128].rearrange("q p t -> p q t"), writes=[("KTc", kv)])
                fw.dma("pool", Vc[kv][:, :, 0:nb_ * 130], Vd[:, :, kb0:kb0 + nb_, :].rearrange("q p b c -> p q (b c)"), writes=[("Vc", kv)])

            def prep_mask(c8):
                kb0, nb_ = c8 * 8, nbk(c8)
                mt = c8 % 2
                tbk = 0 if cnt_.get("ifl") == 5 else 5
                t8 = psb(tbk).rearrange("p (a b) -> p a b", b=128)
                for jj in range(nb_):
                    kbg = kb0 + jj
                    fw.tr(t8[:, jj, :], mbi[:, kbg * 128:(kbg + 1) * 128], ident[:], [("mb", s), "ident"], [("ps", tbk)], sig=(jj == nb_ - 1))
                fw.copy("dve", mbT[mt][:, 0:nb_ * 128], psb(tbk)[:, 0:nb_ * 128], [("ps", tbk)], [("mbT", mt)])

            units = [(c8, pr, g) for c8 in range(nc8) for pr in range(4) for g in range(nbk(c8) // 4)]
            LB = {}

            def qk(u):
                c8, pr, g = u
                kv = mt = c8 % 2
                lbs = []
                for a in range(2):
                    lb = 1 + cnt_["lb"] % 4
                    cnt_["lb"] += 1
                    lbs.append(lb)
                    fw.mm(PS[lb][:], ident[:], mbT[mt][:, g * 512:(g + 1) * 512], True, False, [("mbT", mt), "ident"], [("ps", lb)], sig=False)
                for jj in range(4):
                    kbl = g * 4 + jj
                    for a in range(2):
                        base = 64 * a
                        fw.mm(PS[lbs[a]][:, jj * 128:(jj + 1) * 128], KTc[kv][base:base + 64, pr, kbl * 128:(kbl + 1) * 128],
                              qTt[s][base:base + 64, pr, :], False, jj == 3, [("KTc", kv), ("qTt", s)], [("ps", lbs[a])], sig=(jj == 3))
                LB[u] = lbs

            def pv(u):
                c8, pr, g = u
                kv = c8 % 2
                nb_ = nbk(c8)
                lbs = LB.pop(u)
                ps_ = []
                for a in range(2):
                    p = cnt_["pt"] % 3
                    cnt_["pt"] += 1
                    ps_.append(p)
                    fw.act(PT[p][:], PS[lbs[a]][:], AF.Exp, [("ps", lbs[a]), "negm"], [("PT", p)], bias=negm[:, 0:1], scale=0.125)
                for a in range(2):
                    p = ps_[a]
                    for jj in range(4):
                        kbl = g * 4 + jj
                        vcol = kbl * 130 + a * 65
                        fw.mm(PS[6 + a][0:65, 0:128], Vc[kv][:, pr, vcol:vcol + 65], PT[p][:, jj * 128:(jj + 1) * 128],
                              kbl == 0, kbl == nb_ - 1, [("Vc", kv), ("PT", p)], [("ps", 6 + a)], sig=(jj == 3))
                tick()
                tick()
                if g == nb_ // 4 - 1:
                    for a in range(2):
                        h = 2 * pr + a
                        acc = t1[0:65, h * 128:(h + 1) * 128]
                        if c8 == 0:
                            fw.copy("dve", acc, PS[6 + a][0:65, 0:128], [("ps", 6 + a)], [("t", 0)])
                        else:
                            fw.tt("dve", acc, acc, PS[6 + a][0:65, 0:128], ALU.add, [("ps", 6 + a), ("t", 0)], [("t", 0)])

            prep_load(0)
            if nc8 > 1:
                prep_load(1)
            prep_mask(0)
            qk(units[0])
            for idx, u in enumerate(units):
                c8, pr, g = u
                if idx + 1 < len(units):
                    qk(units[idx + 1])
                pv(u)
                first = (pr == 0 and g == 0)
                last = (idx + 1 == len(units)) or units[idx + 1][0] != c8
                if first and c8 + 1 < nc8:
                    prep_mask(c8 + 1)
                if last and c8 + 2 < nc8:
                    prep_load(c8 + 2)
            fw.copy("act", OTs8[:, 0:4, :], t1[0:64, 0:512].rearrange("p (h t) -> p h t", t=128), [("t", 0)], [("OTs8", 0)])
            fw.copy("pool", OTs8[:, 4:8, :], t1[0:64, 512:1024].rearrange("p (h t) -> p h t", t=128), [("t", 0)], [("OTs8", 1)])
            for h in range(8):
                fw.mm(PS[6][:, h:h + 1], t1[64:65, h * 128:(h + 1) * 128], one_t[64:65, 0:1], True, True, [("t", 0), "one_t"], [("ps", 6)], sig=(h == 7))
            fw.op("dve", lambda e: e.reciprocal(out=rinv8[:], in_=PS[6][:, 0:8]), [("ps", 6)], ["rinv8"])
            tick()
            for h in range(8):
                for nn in range(2):
                    yb = 1 + 2 * (h % 2) + nn
                    fw.mm(PS[yb][:], OTs8[:, h, :], Wp[:, h, nn * 512:(nn + 1) * 512], True, True, [("OTs8", h // 4), ("Wp", h)], [("ps", yb)])
                    ya = yacc[:, nn * 512:(nn + 1) * 512]
                    if h == 0:
                        fw.ts("dve", ya, PS[yb][:], rinv8[:, h:h + 1], None, ALU.mult, None, [("ps", yb), "rinv8"], [("yacc", nn)])
                    else:
                        fw.stt(ya, PS[yb][:], rinv8[:, h:h + 1], ya, ALU.mult, ALU.add, [("ps", yb), "rinv8", ("yacc", nn)], [("yacc", nn)])
                tick()
            fw.dma("sp", zTt[:].rearrange("p a b -> p (a b)"), ZTd[i], writes=["zTt"])
            fw.dma("sp", sgat[:], SGAd[i], writes=["sgat"])
            fw.dma("sp", sgbt[:], SGBd[i], writes=["sgbt"])
            for nn in range(2):
                for kc in range(8):
                    fw.mm(PS[1 + nn][:], zTt[:, kc, :], Wr[:, kc, nn * 512:(nn + 1) * 512], kc == 0, kc == 7, ["zTt", ("Wr", kc)], [("ps", 1 + nn)])
                hs = slice(nn * 512, (nn + 1) * 512)
                fw.tt("dve", t1[:, hs], PS[1 + nn][:], sgbt[:, hs], ALU.mult, [("ps", 1 + nn), "sgbt"], [("t", 0)])
                fw.tt("pool", t2[:, hs], yacc[:, hs], sgat[:, hs], ALU.mult, [("yacc", nn), "sgat"], [("t", 1)])
                fw.tt("pool", mg[:, hs], t1[:, hs], t2[:, hs], ALU.add, [("t", 0), ("t", 1)], [("mg", nn)])
                tick()
            tbk = 0 if cnt_.get("ifl") == 5 else 5
            t8 = psb(tbk).rearrange("p (a b) -> p a b", b=128)
            for kc in range(8):
                fw.tr(t8[:, kc, :], mg[:, kc * 128:(kc + 1) * 128], ident[:], [("mg", kc // 4), "ident"], [("ps", tbk)], sig=(kc == 7))
            fw.copy("act", zTt[:], t8, [("ps", tbk)], ["zTt"])
            fw.dma("sp", yacc[:], xo[i * 128:(i + 1) * 128, :], writes=[("yacc", 0), ("yacc", 1)])
            for nn in range(2):
                for kc in range(8):
                    fw.mm(PS[1 + nn][:], zTt[:, kc, :], Wo[:, kc, nn * 512:(nn + 1) * 512], kc == 0, kc == 7, ["zTt", ("Wo", kc)], [("ps", 1 + nn)])
                hs = slice(nn * 512, (nn + 1) * 512)
                fw.tt("dve", t2[:, hs], PS[1 + nn][:], yacc[:, hs], ALU.add, [("ps", 1 + nn), ("yacc", nn)], [("t", 1)])
                tick()
            fw.dma("sp", X1d[i], t2[:], reads=[("t", 1)])

        run_all(thread_b(0))
        for i in range(nown_lim):
            if i + 1 < nown_lim:
                gb = thread_b(i + 1)
                n_b = 8 * (i + 2) + NIT * ((512 * (i + 2) + 2047) // 2048 + 2) + 4
                n_a = 8 * (i + 1) + 8 + 4
                state = {"acc": 0.0, "done": False}

                def tick(gb=gb, state=state, ratio=n_b / n_a):
                    if state["done"]:
                        return
                    state["acc"] += ratio
                    while state["acc"] >= 1.0:
                        state["acc"] -= 1.0
                        try:
                            next(gb)
                        except StopIteration:
                            state["done"] = True
                            return
                thread_a(i, tick)
                run_all(gb)
            else:
                thread_a(i, lambda: None)
        fw.barrier()
    if stop_after <= 3:
        fw.finish()
        return nc

    with ExitStack() as es:
        Wg = T(es, "Wg", [128, 8, DFF], BF16)
        Wu = T(es, "Wu", [128, 8, DFF], BF16)
        Wd = T(es, "Wd", [128, 22, D], BF16)
        A2 = T(es, "A2", [128, D], F32)
        B2 = T(es, "B2", [128, D], F32)
        G2 = T(es, "G2", [128, D], F32)
        FN = T(es, "FN", [128, D], F32)
        fw.dma("pool", A2[:], MODd[3], writes=["A2"])
        fw.dma("pool", B2[:], MODd[4], writes=["B2"])
        fw.dma("pool", G2[:], MODd[5], writes=["G2"])
        fw.dma("pool", FN[:], MODd[6], writes=["FN"])
        if nb_lim == NB:
            for kc in range(8):
                fw.dma(("sp", "pool")[kc % 2], Wg[:, kc, :], WGd[:, kc, :], writes=[("Wg", kc)])
            for kc in range(8):
                fw.dma(("sp", "pool")[kc % 2], Wu[:, kc, :], WUd[:, kc, :], writes=[("Wu", kc)])
            for q4 in range(2):
                fw.dma(("sp", "pool")[q4 % 2], Wd[:, q4 * 11:(q4 + 1) * 11, :], WDd[:, q4 * 11:(q4 + 1) * 11, :], writes=[("Wd", fc) for fc in range(q4 * 11, (q4 + 1) * 11)])
        else:
            with ExitStack() as es_w:
                stg3 = [T(es_w, "stg3_%d" % i, [128, DFF], F32) for i in range(2)]
                k = 0
                for (wsrc, wdst, nm) in ((w_g, Wg, "Wg"), (w_u, Wu, "Wu")):
                    for kc in range(8):
                        s = k % 2
                        fw.dma("sp", stg3[s][:], wsrc[kc * 128:(kc + 1) * 128, :], writes=[("stg3", s)])
                        fw.copy(("act", "dve")[k % 2], wdst[:, kc, :], stg3[s][:], [("stg3", s)], [(nm, kc)])
                        k += 1
                for fc in range(22):
                    s = k % 2
                    fw.dma("sp", stg3[s][:, 0:D], w_d[fc * 128:(fc + 1) * 128, :], writes=[("stg3", s)])
                    fw.copy(("act", "dve")[k % 2], Wd[:, fc, :], stg3[s][:, 0:D], [("stg3", s)], [("Wd", fc)])
                    k += 1
                fw.barrier()
        x1t = [T(es, "x1t%d" % i, [128, D], F32) for i in range(2)]
        h2T = T(es, "h2T", [128, 8, 512], BF16)
        aT = T(es, "aT", [128, 22, 512], BF16)
        tmp3 = T(es, "tmp3", [128, D], F32)
        hb3 = T(es, "hb3", [128, D], BF16)
        t3 = T(es, "t3", [128, D], F32)
        sl = [T(es, "sl%d" % i, [128, 512], F32) for i in range(2)]
        st3 = T(es, "st3", [128, 8], F32)
        ngrp = (nown_lim + 3) // 4

        def norm_group(g):
            nbg = min(4, nown_lim - 4 * g)
            for bi in range(nbg):
                    i = 4 * g + bi
                    s = i % 2
                    fw.dma("sp", x1t[s][:], X1d[i], writes=[("x1t", s)])
                    fw.act(hb3[:], x1t[s][:], AF.Square, [("x1t", s)], ["hb3", "ss"], accum_out=st3[:, 0:1])
                    fw.ts("dve", st3[:, 1:2], st3[:, 0:1], 1.0 / D, EPS, ALU.mult, ALU.add, ["ss"], ["vv"])
                    fw.act(st3[:, 2:3], st3[:, 1:2], AF.Sqrt, ["vv"], ["sd"])
                    fw.op("dve", lambda e: e.reciprocal(out=st3[:, 3:4], in_=st3[:, 2:3]), ["sd"], ["rstd"])
                    fw.stt(tmp3[:], x1t[s][:], st3[:, 3:4], A2[:], ALU.mult, ALU.mult, [("x1t", s), "rstd", "A2"], ["tmp3"])
                    fw.tt("pool", hb3[:], tmp3[:], B2[:], ALU.add, ["tmp3", "B2"], ["hb3"])
                    tb_ = psb(7).rearrange("p (a b) -> p a b", b=128)
                    for kc in range(8):
                        fw.tr(tb_[:, kc, :], hb3[:, kc * 128:(kc + 1) * 128], ident[:], ["hb3", "ident"], [("ps", 7)], sig=(kc == 7))
                    fw.copy("act", h2T[:, :, bi * 128:(bi + 1) * 128], tb_, [("ps", 7)], [("h2T", bi)])

        norm_group(0)
        for g in range(ngrp):
            nbg = min(4, nown_lim - 4 * g)
            NT = nbg * 128
            H2 = [("h2T", bi) for bi in range(nbg)]
            for fc in range(22):
                gb_, ub_ = fc % 2, 2 + fc % 2
                for kc in range(8):
                    fw.mm(PS[gb_][:, 0:NT], Wg[:, kc, fc * 128:(fc + 1) * 128], h2T[:, kc, 0:NT], kc == 0, kc == 7, H2 + [("Wg", kc)], [("ps", gb_)])
                for kc in range(8):
                    fw.mm(PS[ub_][:, 0:NT], Wu[:, kc, fc * 128:(fc + 1) * 128], h2T[:, kc, 0:NT], kc == 0, kc == 7, H2 + [("Wu", kc)], [("ps", ub_)])
                fw.act(sl[fc % 2][:, 0:NT], PS[gb_][:, 0:NT], AF.Silu, [("ps", gb_)], [("sl", fc % 2)])
                fw.tt("dve", aT[:, fc, 0:NT], sl[fc % 2][:, 0:NT], PS[ub_][:, 0:NT], ALU.mult, [("sl", fc % 2), ("ps", ub_)], [("aT", fc)])
            AT = [("aT", fc) for fc in range(22)]
            if g + 1 < ngrp:
                norm_group(g + 1)
            for bi in range(nbg):
                i = 4 * g + bi
                s = i % 2
                fw.dma("sp", x1t[s][:], X1d[i], writes=[("x1t", s)])
                for nn in range(2):
                    db = 4 + nn
                    for fc in range(22):
                        fw.mm(PS[db][:], aT[:, fc, bi * 128:(bi + 1) * 128], Wd[:, fc, nn * 512:(nn + 1) * 512], fc == 0, fc == 21,
                              [("aT", fc), ("Wd", fc)], [("ps", db)])
                    hs = slice(nn * 512, (nn + 1) * 512)
                    fw.tt("dve", t3[:, hs], PS[db][:], G2[:, hs], ALU.mult, [("ps", db), "G2"], [("t3", nn)])
                    fw.tt("pool", t3[:, hs], t3[:, hs], x1t[s][:, hs], ALU.add, [("t3", nn), ("x1t", s)], [("t3", nn)])
                T3 = [("t3", 0), ("t3", 1)]
                fw.act(hb3[:], t3[:], AF.Square, T3, ["hb3", "ss2"], accum_out=st3[:, 4:5])
                fw.ts("dve", st3[:, 5:6], st3[:, 4:5], 1.0 / D, EPS, ALU.mult, ALU.add, ["ss2"], ["vv2"])
                fw.act(st3[:, 6:7], st3[:, 5:6], AF.Sqrt, ["vv2"], ["sd2"])
                fw.op("dve", lambda e: e.reciprocal(out=st3[:, 7:8], in_=st3[:, 6:7]), ["sd2"], ["rstd2"])
                fw.stt(tmp3[:], t3[:], st3[:, 7:8], FN[:], ALU.mult, ALU.mult, T3 + ["rstd2", "FN"], ["tmp3"])
                fw.dma("sp", out_d[i * 128:(i + 1) * 128, :], tmp3[:], reads=["tmp3"])
        fw.barrier()

    fw.finish()
    return nc


def _rope_tab(pos, dim, theta, scale=1.0):
    inv = 1.0 / (theta ** (np.arange(0, dim, 2, dtype=np.float64) / dim))
    ang = pos.astype(np.float64)[:, None] * inv[None, :]
    cs, sn = np.cos(ang) * scale, np.sin(ang) * scale
    return np.concatenate([cs, cs, -sn, sn], axis=1).astype(np.float32)


def make_consts(j):
    cst = {}
    cst["ident"] = np.eye(128, dtype=np.float32).astype(ml_dtypes.bfloat16)
    pos = np.arange(S)
    tA = _rope_tab(pos, 16, 500000.0)
    cst["ropeA"] = np.ascontiguousarray(tA.reshape(NB, 128, 32).transpose(1, 0, 2))
    tRk = _rope_tab(pos, 128, 10000.0, scale=128 ** -0.5)
    tRq = _rope_tab(pos, 128, 10000.0)
    cst["ropeR"] = np.ascontiguousarray(tRk.reshape(NB, 128, 256))
    own = np.array([4 * i + j for i in range(NOWN)])
    cst["ropeAo"] = np.ascontiguousarray(tA.reshape(NB, 128, 32)[own].transpose(1, 0, 2))
    cst["ropeRq"] = np.ascontiguousarray(tRq.reshape(NB, 128, 256)[own])
    cst["ropeRk"] = np.ascontiguousarray(tRk.reshape(NB, 128, 256)[own])
    c = np.zeros((128, 8 + 1024 + 16), np.float32)
    c[:, 1032:1048] = (0.5 ** np.arange(1, 17))[None, :]
    c[:, j] = 1.0
    g = np.array(GAMMA, np.float64)
    p = np.arange(128, dtype=np.float64)
    c[:, 4:8] = (g[None, :] ** (127.0 - p)[:, None])
    qd = g[:, None] ** (p[None, :] + 1.0)
    c[:, 8:520] = qd.reshape(1, 512)
    diff = p[None, :] - p[:, None]
    dm = np.where(diff[None] >= 0, g[:, None, None] ** np.maximum(diff, 0.0)[None], 0.0)
    c[:, 520:1032] = dm.transpose(1, 0, 2).reshape(128, 512)
    cst["cst"] = c
    md = np.zeros((NOWN, 128, 512), np.float32)
    for i in range(NOWN):
        qpos = 128 * (4 * i + j) + np.arange(128)
        kpos = 512 * i + np.arange(512)
        md[i] = np.where(kpos[None, :] <= qpos[:, None], 0.0, -1e30)
    cst["maskd"] = md
    return cst


def make_in_maps(inputs):
    x = np.asarray(inputs["x"], np.float32)
    c = np.asarray(inputs["c"], np.float32)
    f = lambda k: np.ascontiguousarray(np.asarray(inputs[k], np.float32))
    shared = {
        "w_ada": f("w_ada"), "b_ada": f("b_ada").reshape(1, -1),
        "nws": np.stack([f("norm1_w"), f("norm2_w"), f("final_norm_w"), f("gn_w")], 0).reshape(1, 4, D),
        "w_in": f("w_in"), "w_attn_proj": f("w_attn_proj"), "w_ret_proj": f("w_ret_proj"), "w_out": f("w_out"),
        "w_ffn_gate": f("w_ffn_gate"), "w_ffn_up": f("w_ffn_up"), "w_ffn_down": f("w_ffn_down"),
    }
    maps = []
    for core in range(8):
        b, j = core // 4, core % 4
        m = dict(shared)
        m["xf"] = np.ascontiguousarray(x[b])
        m["xo"] = np.ascontiguousarray(x[b].reshape(NOWN, 4, 128, D)[:, j].reshape(NOWN * 128, D))
        m["c_l"] = np.ascontiguousarray(c[b].reshape(8, 128).T)
        m.update(make_consts(j))
        maps.append(m)
    return maps


def kernel(**inputs):
    nc = build_program()
    maps = make_in_maps(inputs)
    res = run_bass_kernel_spmd(nc, maps, core_ids=list(range(8)))
    out = np.zeros((2, S, D), np.float32)
    for core in range(8):
        b, j = core // 4, core % 4
        o = np.asarray(res.results[core]["out"]).reshape(NOWN, 128, D)
        out[b].reshape(NOWN, 4, 128, D)[:, j] = o
    return out
```

```python
import bisect
from contextlib import ExitStack

import numpy as np
import ml_dtypes

import concourse.bass as bass
import concourse.mybir as mybir
from concourse.bass_utils import run_bass_kernel_spmd

F32 = mybir.dt.float32
BF16 = mybir.dt.bfloat16
ALU = mybir.AluOpType
AF = mybir.ActivationFunctionType
AX = mybir.AxisListType

D = 1024
S = 8192
NB = 64
NOWN = 16
DIN = 7240
DFF = 2816
EPS = 1e-6
TOPK = 256
NIT = 13
NEGM = -30000.0
C_QA, C_KA, C_VA, C_IQ, C_IK, C_IW, C_QR, C_KR, C_VR, C_GR, C_GA, C_GB = (
    0, 512, 1024, 1536, 2048, 2112, 2120, 2632, 3144, 4168, 5192, 6216)
GAMMA = [1.0 - 2.0 ** (-5.0 - h) for h in range(4)]


class FW:
    def __init__(self, nc, ndma=8):
        self.nc = nc
        self.E = {"pe": nc.tensor, "act": nc.scalar, "dve": nc.vector, "pool": nc.gpsimd, "sp": nc.sync}
        self.sem, self.cnt, self.seq, self.sigs = {}, {}, {}, {}
        for e in ("pe", "act", "dve", "pool"):
            self.sem[e] = nc.alloc_semaphore("s_" + e)
            self.cnt[e] = 0
            self.seq[e] = 0
            self.sigs[e] = []
        self.dsem, self.duse, self.dnext = {}, {}, {}
        for q in ("sp", "pool", "act"):
            self.dsem[q] = [nc.alloc_semaphore("d_%s%d" % (q, i)) for i in range(ndma)]
            self.duse[q] = [0] * ndma
            self.dnext[q] = 0
        self.seen = {e: {} for e in self.E}
        self.lastw = {}
        self.readers = {}
        self.alldma = []
        self.n = 0

    def _resolve(self, tok):
        if tok[0] == "d":
            return tok[1], tok[2]
        _, eng, seq = tok
        arr = self.sigs[eng]
        i = bisect.bisect_left(arr, (seq, -1))
        if i >= len(arr):
            raise RuntimeError("dependency on unsignaled %s op seq %d" % (eng, seq))
        return self.sem[eng], arr[i][1]

    def _wait(self, eng, tok, kind, dma=False):
        if (not dma) and tok[0] == "c" and tok[1] == eng and eng == "pe":
            return
        sem, val = self._resolve(tok)
        key = sem.num
        if self.seen[eng].get(key, 0) >= val:
            return
        self.E[eng].wait_ge(sem, val)
        self.seen[eng][key] = val

    def op(self, eng, fn, reads=(), writes=(), sig=None, dma=False):
        self.n += 1
        for r in reads:
            w = self.lastw.get(r)
            if w is not None:
                self._wait(eng, w, "raw", dma)
        for r in writes:
            w = self.lastw.get(r)
            if w is not None:
                self._wait(eng, w, "waw", dma)
            for rd in self.readers.get(r, ()):
                self._wait(eng, rd, "war", dma)
        if dma:
            q = eng
            i = self.dnext[q]
            self.dnext[q] = (i + 1) % len(self.dsem[q])
            sem = self.dsem[q][i]
            if self.duse[q][i] > 0:
                self._wait(eng, ("d", sem, 16 * self.duse[q][i]), "raw")
            ins = fn(self.E[eng])
            self.duse[q][i] += 1
            ins.then_inc(sem, 16)
            tok = ("d", sem, 16 * self.duse[q][i])
            self.alldma.append(tok)
        else:
            ins = fn(self.E[eng])
            self.seq[eng] += 1
            if sig is None:
                sig = eng != "pe"
            if sig:
                self.cnt[eng] += 1
                ins.then_inc(self.sem[eng], 1)
                self.sigs[eng].append((self.seq[eng], self.cnt[eng]))
            tok = ("c", eng, self.seq[eng])
        for r in writes:
            self.lastw[r] = tok
            self.readers[r] = []
        for r in reads:
            if r in writes:
                continue
            self.readers.setdefault(r, []).append(tok)
        return ins

    def dma(self, q, out, in_, reads=(), writes=()):
        return self.op(q, lambda e: e.dma_start(out=out, in_=in_), reads, writes, dma=True)

    def barrier(self):
        toks = []
        for e in ("pe", "act", "dve", "pool"):
            if self.seq[e] > 0:
                if not self.sigs[e] or self.sigs[e][-1][0] != self.seq[e]:
                    raise RuntimeError("barrier: last %s op not signaled" % e)
                toks.append(("c", e, self.seq[e]))
        toks += self.alldma
        self.alldma = []
        for eng in self.E:
            for t in toks:
                if t[0] == "c" and t[1] == eng:
                    continue
                self._wait(eng, t, "raw")
        self.lastw = {}
        self.readers = {}

    def finish(self):
        for t in self.alldma:
            self._wait("sp", t, "raw")
        self.alldma = []

    def mm(self, out, lhsT, rhs, start, stop, r, w, sig=None):
        if sig is None:
            sig = stop
        return self.op("pe", lambda e: e.matmul(out, lhsT=lhsT, rhs=rhs, start=start, stop=stop), r, w, sig=sig)

    def tr(self, out, in_, ident, r, w, sig):
        return self.op("pe", lambda e: e.transpose(out=out, in_=in_, identity=ident), r, w, sig=sig)

    def act(self, out, in_, func, r, w, **kw):
        return self.op("act", lambda e: e.activation(out=out, in_=in_, func=func, **kw), r, w)

    def copy(self, eng, out, in_, r, w):
        if eng == "act":
            return self.op("act", lambda e: e.copy(out=out, in_=in_), r, w)
        return self.op(eng, lambda e: e.tensor_copy(out=out, in_=in_), r, w)

    def tt(self, eng, out, in0, in1, op, r, w):
        return self.op(eng, lambda e: e.tensor_tensor(out=out, in0=in0, in1=in1, op=op), r, w)

    def ts(self, eng, out, in0, s1, s2, op0, op1, r, w, **kw):
        if op1 is None:
            return self.op(eng, lambda e: e.tensor_scalar(out=out, in0=in0, scalar1=s1, scalar2=None, op0=op0, **kw), r, w)
        return self.op(eng, lambda e: e.tensor_scalar(out=out, in0=in0, scalar1=s1, scalar2=s2, op0=op0, op1=op1, **kw), r, w)

    def stt(self, out, in0, scalar, in1, op0, op1, r, w):
        return self.op("dve", lambda e: e.scalar_tensor_tensor(out=out, in0=in0, scalar=scalar, in1=in1, op0=op0, op1=op1), r, w)


def build_program(stop_after=99, dbg=False, nb_lim=NB, nown_lim=NOWN):
    nc = bass.Bass("TRN2", target_bir_lowering=False)
    kind_s = "ExternalOutput" if dbg else "Internal"

    def din(name, shape, dt=F32):
        return nc.dram_tensor(name, list(shape), dt, kind="ExternalInput").ap()

    def dscr(name, shape, dt):
        return nc.dram_tensor(name, list(shape), dt, kind=kind_s).ap()

    xf = din("xf", [S, D])
    xo = din("xo", [NOWN * 128, D])
    c_l = din("c_l", [128, 8])
    w_ada = din("w_ada", [D, 6 * D])
    b_ada = din("b_ada", [1, 6 * D])
    nws_d = din("nws", [1, 4, D])
    w_in = din("w_in", [D, DIN])
    w_ap = din("w_attn_proj", [512, D])
    w_rp = din("w_ret_proj", [D, D])
    w_o = din("w_out", [D, D])
    w_g = din("w_ffn_gate", [D, DFF])
    w_u = din("w_ffn_up", [D, DFF])
    w_d = din("w_ffn_down", [DFF, D])
    ident_d = din("ident", [128, 128], BF16)
    ropeA_d = din("ropeA", [128, NB, 32])
    ropeR_d = din("ropeR", [NB, 128, 256])
    ropeAo_d = din("ropeAo", [128, NOWN, 32])
    ropeRq_d = din("ropeRq", [NOWN, 128, 256])
    ropeRk_d = din("ropeRk", [NOWN, 128, 256])
    cst_d = din("cst", [128, 8 + 512 + 512 + 16])
    maskd_d = din("maskd", [NOWN, 128, 512])
    out_d = nc.dram_tensor("out", [NOWN * 128, D], F32, kind="ExternalOutput").ap()

    MODd = dscr("MODd", [7, 128, D], F32)
    KTd = dscr("KTd", [4, 128, S], BF16)
    Vd = dscr("Vd", [4, 128, NB, 130], BF16)
    IKTd = dscr("IKTd", [128, S], BF16)
    Rd = dscr("Rd", [NOWN, 128, D], BF16)
    QTd = dscr("QTd", [NOWN, 128, 512], BF16)
    IQTd = dscr("IQTd", [NOWN, 128, 512], BF16)
    IWd = dscr("IWd", [NOWN, 128, 8], F32)
    ZTd = dscr("ZTd", [NOWN, 128, D], BF16)
    SGAd = dscr("SGAd", [NOWN, 128, D], BF16)
    SGBd = dscr("SGBd", [NOWN, 128, D], BF16)
    NMd = dscr("NMd", [128, 1], F32)
    X1d = dscr("X1d", [NOWN, 128, D], F32)
    WGd = dscr("WGd", [128, 8, DFF], BF16)
    WUd = dscr("WUd", [128, 8, DFF], BF16)
    WDd = dscr("WDd", [128, 22, D], BF16)

    fw = FW(nc)
    top = ExitStack()

    def T(es, name, shape, dt):
        if dbg:
            print("alloc", name, shape, nc.sbuf_bytes_remaining)
        return es.enter_context(nc.sbuf_tensor("sb_" + name, list(shape), dt))

    PS = [top.enter_context(nc.psum_tensor("ps%d" % i, [128, 512], F32)) for i in range(8)]

    def psb(i):
        return PS[i][:].bitcast(BF16)

    ident = T(top, "ident", [128, 128], BF16)
    fw.dma("sp", ident[:], ident_d, writes=["ident"])
    cst = T(top, "cst", [128, 8 + 1024 + 16], F32)
    fw.dma("sp", cst[:], cst_d, writes=["cst"])
    oh = cst[:, 0:4]
    kdec = cst[:, 4:8]
    qdecT = cst[:, 8:520]
    dmaskT = cst[:, 520:1032]

    with ExitStack() as es:
        cl = T(es, "cl", [128, 8], F32)
        sc = T(es, "sc", [128, 8], F32)
        scb = T(es, "scb", [128, 8, 128], F32)
        ones1 = T(es, "ones1", [1, 128], F32)
        bada = T(es, "bada", [1, 6 * D], F32)
        nws = T(es, "nws", [1, 4, D], F32)
        modbc = T(es, "modbc", [128, 6, D], F32)
        nwbc = T(es, "nwbc", [128, 4, D], F32)
        mo = T(es, "mo", [128, 2, D], F32)
        wa = [T(es, "wa%d" % i, [128, 8, 512], F32) for i in range(2)]
        fw.dma("sp", cl[:], c_l, writes=["cl"])
        fw.dma("sp", bada[:], b_ada, writes=["bada"])
        fw.dma("sp", nws[:], nws_d, writes=["nws"])
        fw.op("pool", lambda e: e.memset(ones1[:], 1.0), writes=["ones1"])
        fw.act(sc[:], cl[:], AF.Silu, ["cl"], ["sc"])
        for kc in range(8):
            fw.copy("dve", scb[:, kc, :], sc[:, kc:kc + 1].to_broadcast([128, 128]), ["sc"], [("scb", kc)])
        for ncx in range(12):
            s = ncx % 2
            n0 = ncx * 512
            fw.dma("sp" if s == 0 else "pool", wa[s][:],
                   w_ada[:, n0:n0 + 512].rearrange("(kc p) n -> p kc n", p=128), writes=[("wa", s)])
            pb = PS[s]
            for kc in range(8):
                fw.mm(pb[:], scb[:, kc, :], wa[s][:, kc, :], kc == 0, False, [("scb", kc), ("wa", s)], [("ps", s)])
            fw.mm(pb[:], ones1[0:1, :], bada[0:1, n0:n0 + 512], False, True, ["ones1", "bada"], [("ps", s)])
            fw.copy("act", modbc[:, ncx // 2, (ncx % 2) * 512:(ncx % 2) * 512 + 512], pb[:], [("ps", s)], [("modbc", ncx // 2)])
        for v in range(4):
            for hf in range(2):
                s = hf
                fw.mm(PS[s][:], ones1[0:1, :], nws[0:1, v, hf * 512:hf * 512 + 512], True, True, ["ones1", "nws"], [("ps", s)])
                fw.copy("act", nwbc[:, v, hf * 512:hf * 512 + 512], PS[s][:], [("ps", s)], [("nwbc", v)])
        fw.stt(mo[:, 0, :], modbc[:, 1, :], 1.0, nwbc[:, 0, :], ALU.add, ALU.mult, [("modbc", 1), ("nwbc", 0)], [("mo", 0)])
        fw.stt(mo[:, 1, :], modbc[:, 4, :], 1.0, nwbc[:, 1, :], ALU.add, ALU.mult, [("modbc", 4), ("nwbc", 1)], [("mo", 1)])
        fw.dma("sp", MODd[0], mo[:, 0, :], reads=[("mo", 0)])
        fw.dma("sp", MODd[1], modbc[:, 0, :], reads=[("modbc", 0)])
        fw.dma("sp", MODd[2], modbc[:, 2, :], reads=[("modbc", 2)])
        fw.dma("sp", MODd[3], mo[:, 1, :], reads=[("mo", 1)])
        fw.dma("sp", MODd[4], modbc[:, 3, :], reads=[("modbc", 3)])
        fw.dma("sp", MODd[5], modbc[:, 5, :], reads=[("modbc", 5)])
        fw.dma("sp", MODd[6], nwbc[:, 2, :], reads=[("nwbc", 2)])
        GNd = dscr("GNd", [128, D], F32)
        fw.dma("sp", GNd, nwbc[:, 3, :], reads=[("nwbc", 3)])
        fw.barrier()
    if stop_after <= 0:
        fw.finish()
        return nc

    with ExitStack() as es:
        win = T(es, "win", [128, 8, DIN], BF16)
        A1 = T(es, "A1", [128, D], F32)
        B1 = T(es, "B1", [128, D], F32)
        fw.dma("sp", A1[:], MODd[0], writes=["A1"])
        fw.dma("sp", B1[:], MODd[1], writes=["B1"])
        WIN_R = [("win", kc) for kc in range(8)]

        xt = [T(es, "xt%d" % i, [128, D], F32) for i in range(2)]
        tmpf = T(es, "tmpf", [128, D], F32)
        hb = [T(es, "hb%d" % i, [128, D], BF16) for i in range(2)]
        hT = [T(es, "hT%d" % i, [128, 8, 128], BF16) for i in range(2)]
        st4 = T(es, "st4", [128, 8], F32)
        ropeAo = T(es, "ropeAo", [128, NOWN, 32], F32)
        fw.dma("sp", ropeAo[:], ropeAo_d, writes=["ropeAo"])
        rq = [T(es, "rq%d" % i, [128, 256], F32) for i in range(2)]
        kb = T(es, "kb", [128, 512], BF16)
        rotf = T(es, "rotf", [128, 8, 16], F32)
        tC = T(es, "tC", [128, 512], F32)
        tS = T(es, "tS", [128, 512], F32)
        krf = T(es, "krf", [128, 4, 128], F32)
        krb2 = [T(es, "krb%d" % i, [128, 4, 128], BF16) for i in range(2)]
        krb = krb2[0]
        sq = T(es, "sq", [128, 512], F32)
        ks8 = T(es, "ks8", [128, 8], F32)
        kmx = T(es, "kmx", [128, 8], F32)
        qmx = T(es, "qmx", [128, 8], F32)
        rk = [T(es, "rk%d" % i, [128, 256], F32) for i in range(2)]
        identf = T(es, "identf", [128, 128], F32)
        fw.copy("dve", identf[:], ident[:], ["ident"], ["identf"])
        es_a = ExitStack()
        stg = [T(es_a, "stg%d" % i, [128, 1810], F32) for i in range(2)]
        ropeA = T(es_a, "ropeA", [128, NB, 32], F32)
        rr = [T(es_a, "rr%d" % i, [128, 256], F32) for i in range(2)]
        kT = [T(es_a, "kT%d" % i, [128, 4, 128], BF16) for i in range(2)]
        vx = [T(es_a, "vx%d" % i, [128, 8, 65], BF16) for i in range(2)]
        ikf = T(es_a, "ikf", [128, 64], F32)
        ikb2 = T(es_a, "ikb2", [128, 2, 64], BF16)
        ikT = [T(es_a, "ikT%d" % i, [128, 128], BF16) for i in range(2)]
        vdb2 = [T(es_a, "vdb%d" % i, [128, 4, 256], BF16) for i in range(2)]
        Rst = T(es_a, "Rst", [128, 4, 256], F32)
        Rown = [T(es_a, "Rown%d" % i, [128, D], BF16) for i in range(2)]
        sbufs = [(stg[0][:, 0:905], ("stg", 0, "a")), (stg[0][:, 905:1810], ("stg", 0, "b")),
                 (stg[1][:, 0:905], ("stg", 1, "a")), (stg[1][:, 905:1810], ("stg", 1, "b")),
                 (xt[0][:, 0:905], ("xt", 0)), (xt[1][:, 0:905], ("xt", 1)), (tmpf[:, 0:905], "tmpf")]
        k = 0
        for kc in range(8):
            for part in range(8):
                sap, sres = sbufs[k % 7]
                c0 = part * 905
                fw.dma("sp" if k % 2 == 0 else "pool", sap, w_in[kc * 128:(kc + 1) * 128, c0:c0 + 905], writes=[sres])
                fw.copy(("act", "dve", "act", "dve", "pool")[k % 5], win[:, kc, c0:c0 + 905], sap, [sres], [("win", kc)])
                k += 1
        fw.dma("sp", ropeA[:], ropeA_d, writes=["ropeA"])
        fw.op("pool", lambda e: e.memset(Rst[:], 0.0), writes=["Rst"])
        fw.op("pool", lambda e: e.memset(kmx[:], 0.0), writes=["kmx"])
        fw.op("pool", lambda e: e.memset(qmx[:], 0.0), writes=["qmx"])
        for i in range(2):
            fw.op("pool", lambda e, i=i: e.memset(vx[i][:], 1.0), writes=[("vx", i)])

        def load_x(xsrc, s):
            fw.dma("sp", xt[s][:], xsrc, writes=[("xt", s)])

        def elem(s):
            fw.act(hb[s][:], xt[s][:], AF.Square, [("xt", s)], [("hb", s), "ss"], accum_out=st4[:, 0:1])
            fw.ts("dve", st4[:, 1:2], st4[:, 0:1], 1.0 / D, EPS, ALU.mult, ALU.add, ["ss"], ["vv"])
            fw.act(st4[:, 2:3], st4[:, 1:2], AF.Sqrt, ["vv"], ["sd"])
            fw.op("dve", lambda e: e.reciprocal(out=st4[:, 3:4], in_=st4[:, 2:3]), ["sd"], ["rstd"])
            fw.stt(tmpf[:], xt[s][:], st4[:, 3:4], A1[:], ALU.mult, ALU.mult, [("xt", s), "rstd", "A1"], ["tmpf"])
            fw.tt("pool", hb[s][:], tmpf[:], B1[:], ALU.add, ["tmpf", "B1"], [("hb", s)])

        def trans(s):
            tb_ = psb(0).rearrange("p (a b) -> p a b", b=128)
            for kc in range(8):
                fw.tr(tb_[:, kc, :], hb[s][:, kc * 128:(kc + 1) * 128], ident[:], [("hb", s), "ident"], [("ps", 0)], sig=(kc == 7))
            fw.copy("act", hT[s][:], tb_, [("ps", 0)], [("hT", s)])

        cbank = [1, 2, 3, 7]
        cstate = {"i": 0}

        def proj(s, c0, ncol):
            b = cbank[cstate["i"] % 4]
            cstate["i"] += 1
            for kc in range(8):
                fw.mm(PS[b][:, 0:ncol], hT[s][:, kc, :], win[:, kc, c0:c0 + ncol], kc == 0, kc == 7,
                      [("hT", s), ("win", kc)], [("ps", b)])
            return b

        def rope16(src_f, dst_b, tab, nh, nm, tres):
            CC = tab[:, 0:16].unsqueeze(1).to_broadcast([128, nh, 16])
            SSa = tab[:, 16:24].unsqueeze(1).to_broadcast([128, nh, 8])
            SSb = tab[:, 24:32].unsqueeze(1).to_broadcast([128, nh, 8])
            tCv = tC[:, 0:nh * 16].rearrange("p (h d) -> p h d", d=16)
            tSv = tS[:, 0:nh * 16].rearrange("p (h d) -> p h d", d=16)
            fw.tt("pool", tCv, src_f, CC, ALU.mult, [nm + "_f", tres], ["tC"])
            fw.tt("pool", tSv[:, :, 0:8], src_f[:, :, 8:16], SSa, ALU.mult, [nm + "_f", tres], ["tS"])
            fw.tt("pool", tSv[:, :, 8:16], src_f[:, :, 0:8], SSb, ALU.mult, [nm + "_f", tres], ["tS"])
            fw.tt("pool", dst_b, tCv, tSv, ALU.add, ["tC", "tS"], [nm + "_b"])

        def rope128(src_f, dst_b, tab, nm, tres, fres=None):
            fres = fres or nm
            CC = tab[:, 0:128].unsqueeze(1).to_broadcast([128, 4, 128])
            SSa = tab[:, 128:192].unsqueeze(1).to_broadcast([128, 4, 64])
            SSb = tab[:, 192:256].unsqueeze(1).to_broadcast([128, 4, 64])
            tCv = tC[:].rearrange("p (h d) -> p h d", d=128)
            tSv = tS[:].rearrange("p (h d) -> p h d", d=128)
            fw.tt("pool", tCv, src_f, CC, ALU.mult, [fres + "_f", tres], ["tC"])
            fw.tt("pool", tSv[:, :, 0:64], src_f[:, :, 64:128], SSa, ALU.mult, [fres + "_f", tres], ["tS"])
            fw.tt("pool", tSv[:, :, 64:128], src_f[:, :, 0:64], SSb, ALU.mult, [fres + "_f", tres], ["tS"])
            fw.tt("pool", dst_b, tCv, tSv, ALU.add, ["tC", "tS"], [nm + "_b"])

        def sumsq_max(srcb, acc, accname, nm):
            fw.tt("pool", sq[:], srcb, srcb, ALU.mult, [nm + "_b"], ["sq"])
            fw.op("dve", lambda e: e.tensor_reduce(out=ks8[:], in_=sq[:].rearrange("p (h d) -> p h d", d=64),
                                                   axis=AX.X, op=ALU.add), ["sq"], ["ks8"])
            fw.tt("dve", acc[:], acc[:], ks8[:], ALU.max, ["ks8", accname], [accname])

        load_x(xf[0:128, :], 0)
        if nb_lim > 1:
            load_x(xf[128:256, :], 1)
        fw.dma("pool", rr[0][:], ropeR_d[0], writes=[("rr", 0)])
        elem(0)
        trans(0)
        cvp = []
        for (wsrc, wdst) in ((w_g, WGd), (w_u, WUd)):
            for kc in range(8):
                for hf in range(2):
                    cvp.append((wsrc[kc * 128:(kc + 1) * 128, hf * 1408:(hf + 1) * 1408], wdst[:, kc, hf * 1408:(hf + 1) * 1408], 1408))
        for fc in range(22):
            cvp.append((w_d[fc * 128:(fc + 1) * 128, :], WDd[:, fc, :], D))

        def conv_in(k):
            if k < len(cvp):
                wsrc_, wdst_, n_ = cvp[k]
                fw.dma("sp", stg[0][:, 0:n_], wsrc_, writes=[("stg", 0), ("stg", 0, "a"), ("stg", 0, "b")])

        def conv_cast(k):
            if k < len(cvp):
                wsrc_, wdst_, n_ = cvp[k]
                fw.copy("dve", stg[1][:].bitcast(BF16)[:, 0:n_], stg[0][:, 0:n_], [("stg", 0)], [("stg", 1), ("stg", 1, "a"), ("stg", 1, "b")])

        def conv_out(k):
            if k < len(cvp):
                wsrc_, wdst_, n_ = cvp[k]
                fw.dma("sp", wdst_, stg[1][:].bitcast(BF16)[:, 0:n_], reads=[("stg", 1)])

        if nb_lim == NB:
            conv_in(0)
        upending = []
        for tb in range(nb_lim):
            s = tb % 2
            if tb + 1 < nb_lim:
                fw.dma("pool", rr[1 - s][:], ropeR_d[tb + 1], writes=[("rr", 1 - s)])
                elem(1 - s)
            b = proj(s, C_KA, 512)
            fw.copy("act", kb[:], PS[b][:], [("ps", b)], ["k_b"])
            fw.copy("act", rotf[:], PS[b][:].rearrange("p (h d) -> p h d", d=64)[:, :, 0:16], [("ps", b)], ["k_f"])
            rope16(rotf[:], kb[:].rearrange("p (h d) -> p h d", d=64)[:, :, 0:16], ropeA[:, tb, :], 8, "k", "ropeA")
            sumsq_max(kb[:], kmx, "kmx", "k")
            for fn_ in upending:
                fn_()
            upending = []
            b = proj(s, C_VA, 512)
            fw.copy("act", vx[s][:, :, 0:64], PS[b][:].rearrange("p (h d) -> p h d", d=64), [("ps", b)], [("vx", s)])
            fw.dma("sp", Vd[:, :, tb, :].rearrange("q p c -> p q c"), vx[s][:].rearrange("p (q a) c -> p q (a c)", a=2),
                   reads=[("vx", s)])
            b = proj(s, C_IK, 64)
            fw.copy("act", ikf[:], PS[b][:, 0:64], [("ps", b)], ["ik_f"])
            fw.copy("act", ikb2[:], PS[b][:, 0:64].unsqueeze(1).to_broadcast([128, 2, 64]), [("ps", b)], ["ik_b"])
            rope16(ikf[:, 0:16].unsqueeze(1).to_broadcast([128, 2, 16]), ikb2[:, :, 0:16], ropeA[:, tb, :], 2, "ik", "ropeA")
            b = proj(s, C_KR, 512)
            fw.copy("act", krf[:], PS[b][:].rearrange("p (h d) -> p h d", d=128), [("ps", b)], ["kr_f"])
            rope128(krf[:], krb2[s][:], rr[s], "kr%d" % s, ("rr", s), fres="kr")
            for half in range(2):
                b = proj(s, C_VR + half * 512, 512)
                for hh in range(2):
                    h = half * 2 + hh
                    fw.act(vdb2[s][:, h, :], PS[b][:, hh * 256:(hh + 1) * 256], AF.Copy, [("ps", b), "cst"], [("vdb", s, h)],
                           scale=kdec[:, h:h + 1])
            if tb + 2 < nb_lim:
                load_x(xf[(tb + 2) * 128:(tb + 3) * 128, :], s)
            t2 = psb(4).rearrange("p (a b) -> p a b", b=128)
            for pr in range(4):
                fw.tr(t2[:, pr, :], kb[:, pr * 128:(pr + 1) * 128], ident[:], ["k_b", "ident"], [("ps", 4)], sig=False)
            fw.tr(t2[:, 4, :], ikb2[:].rearrange("p a d -> p (a d)"), ident[:], ["ik_b", "ident"], [("ps", 4)], sig=True)
            if nb_lim == NB:
                conv_cast(tb)
            fw.copy("dve", kT[s][:], t2[:, 0:4, :], [("ps", 4)], [("kT", s)])
            fw.copy("dve", ikT[s][:], t2[:, 4, :], [("ps", 4)], [("ikT", s)])
            fw.dma("sp", KTd[:, :, tb * 128:(tb + 1) * 128].rearrange("q p t -> p q t"), kT[s][:], reads=[("kT", s)])
            fw.dma("sp", IKTd[:, tb * 128:(tb + 1) * 128], ikT[s][:], reads=[("ikT", s)])
            if nb_lim == NB:
                conv_out(tb)
                conv_in(tb + 1)
            def ustep(tb=tb, s=s):
                g, m = tb // 4, tb % 4
                rs = g % 2
                Rflat = Rst[:].rearrange("p h e -> p (h e)")
                if m == 0:
                    fw.ts("dve", Rown[rs][:], Rflat, oh[:, 0:1], None, ALU.mult, None, ["Rst", "cst"], [("Rown", rs)])
                else:
                    fw.stt(Rown[rs][:], Rflat, oh[:, m:m + 1], Rown[rs][:], ALU.mult, ALU.add, ["Rst", "cst", ("Rown", rs)], [("Rown", rs)])
                if m == 3:
                    fw.dma("sp", Rd[g], Rown[rs][:], reads=[("Rown", rs)])
                for h in range(4):
                    ub = 5 + h // 2
                    fw.mm(PS[ub][:, (h % 2) * 256:(h % 2) * 256 + 256], krb2[s][:, h, :], vdb2[s][:, h, :], True, True,
                          ["kr%d_b" % s, ("vdb", s, h)], [("ps", ub)])
                for h in range(4):
                    ub = 5 + h // 2
                    fw.stt(Rst[:, h, :], Rst[:, h, :], float(GAMMA[h] ** 128), PS[ub][:, (h % 2) * 256:(h % 2) * 256 + 256],
                           ALU.mult, ALU.add, ["Rst", ("ps", ub)], ["Rst"])
            upending.append(ustep)
            if tb + 1 < nb_lim:
                trans(1 - s)
        for fn_ in upending:
            fn_()
        fw.barrier()
        es_a.close()
        if stop_after <= 1:
            fw.finish()
            return nc

        with ExitStack() as es_b:
            gnbc = T(es_b, "gnbc", [128, D], F32)
            fw.dma("sp", gnbc[:], GNd, writes=["gnbc"])
            iqb = T(es_b, "iqb", [128, 512], BF16)
            rotf2 = T(es_b, "rotf2", [128, 8, 16], F32)
            qT = [T(es_b, "qT%d" % i, [128, 4, 128], BF16) for i in range(2)]
            iqT = [T(es_b, "iqT%d" % i, [128, 4, 128], BF16) for i in range(2)]
            iwf = [T(es_b, "iwf%d" % i, [128, 8], F32) for i in range(2)]
            qrf = T(es_b, "qrf", [128, 4, 128], F32)
            qrb = T(es_b, "qrb", [128, 4, 128], BF16)
            qrT = T(es_b, "qrT", [128, 4, 128], BF16)
            qxT = T(es_b, "qxT", [128, 4, 128], BF16)
            krT = T(es_b, "krT", [128, 4, 128], BF16)
            vrb = T(es_b, "vrb", [128, 4, 256], BF16)
            innT = T(es_b, "innT", [128, 4, 128], BF16)
            rb = [T(es_b, "rb%d" % i, [128, D], BF16) for i in range(2)]
            bst = T(es_b, "bst", [128, 4, 6], F32)
            mv = T(es_b, "mv", [128, 4, 2], F32)
            gv = T(es_b, "gv", [128, 12], F32)
            yn = T(es_b, "yn", [128, D], F32)
            sg = T(es_b, "sg", [128, D], BF16)
            zb = T(es_b, "zb", [128, D], BF16)
            zT = [T(es_b, "zT%d" % i, [128, 8, 128], BF16) for i in range(2)]
            sga = [T(es_b, "sga%d" % i, [128, D], BF16) for i in range(2)]
            sgb = [T(es_b, "sgb%d" % i, [128, D], BF16) for i in range(2)]

            def tr4(src_b, nm):
                t2 = psb(4).rearrange("p (a b) -> p a b", b=128)
                for pr in range(4):
                    fw.tr(t2[:, pr, :], src_b[:, pr * 128:(pr + 1) * 128], ident[:], [nm, "ident"], [("ps", 4)], sig=(pr == 3))
                return t2[:, 0:4, :]

            load_x(xo[0:128, :], 0)
            if nown_lim > 1:
                load_x(xo[128:256, :], 1)
            fw.dma("pool", rq[0][:], ropeRq_d[0], writes=[("rq", 0)])
            fw.dma("pool", rk[0][:], ropeRk_d[0], writes=[("rk", 0)])
            elem(0)
            trans(0)
            pending = []
            for i in range(nown_lim):
                s = i % 2
                if i + 1 < nown_lim:
                    fw.dma("pool", rq[1 - s][:], ropeRq_d[i + 1], writes=[("rq", 1 - s)])
                    fw.dma("pool", rk[1 - s][:], ropeRk_d[i + 1], writes=[("rk", 1 - s)])
                    elem(1 - s)
                fw.dma("sp", rb[s][:], Rd[i], writes=[("rb", s)])
                b = proj(s, C_QA, 512)
                fw.copy("act", kb[:], PS[b][:], [("ps", b)], ["k_b"])
                fw.copy("act", rotf[:], PS[b][:].rearrange("p (h d) -> p h d", d=64)[:, :, 0:16], [("ps", b)], ["k_f"])
                rope16(rotf[:], kb[:].rearrange("p (h d) -> p h d", d=64)[:, :, 0:16], ropeAo[:, i, :], 8, "k", "ropeAo")
                sumsq_max(kb[:], qmx, "qmx", "k")
                b = proj(s, C_IQ, 512)
                fw.copy("act", iqb[:], PS[b][:], [("ps", b)], ["iq_b"])
                fw.copy("act", rotf2[:], PS[b][:].rearrange("p (h d) -> p h d", d=64)[:, :, 0:16], [("ps", b)], ["iq_f"])
                rope16(rotf2[:], iqb[:].rearrange("p (h d) -> p h d", d=64)[:, :, 0:16], ropeAo[:, i, :], 8, "iq", "ropeAo")
                b = proj(s, C_IW, 8)
                fw.op("act", lambda e, b=b, s=s: e.mul(out=iwf[s][:], in_=PS[b][:, 0:8], mul=float(8 ** -0.5 * 64 ** -0.5)),
                      [("ps", b)], [("iwf", s)])
                fw.dma("sp", IWd[i], iwf[s][:], reads=[("iwf", s)])
                b = proj(s, C_QR, 512)
                fw.copy("act", qrf[:], PS[b][:].rearrange("p (h d) -> p h d", d=128), [("ps", b)], ["qr_f"])
                rope128(qrf[:], qrb[:], rq[s], "qr", ("rq", s))
                b = proj(s, C_KR, 512)
                fw.copy("act", krf[:], PS[b][:].rearrange("p (h d) -> p h d", d=128), [("ps", b)], ["kr_f"])
                rope128(krf[:], krb[:], rk[s], "kr", ("rk", s))
                for half in range(2):
                    b = proj(s, C_VR + half * 512, 512)
                    fw.copy("act", vrb[:, 2 * half:2 * half + 2, :], PS[b][:].rearrange("p (h e) -> p h e", e=256), [("ps", b)], [("vrb", half)])
                if i + 2 < nown_lim:
                    load_x(xo[(i + 2) * 128:(i + 3) * 128, :], s)
                for fn_ in pending:
                    fn_()
                pending = []
                tv = tr4(kb, "k_b")
                fw.copy("dve", qT[s][:], tv, [("ps", 4)], [("qT", s)])
                fw.dma("sp", QTd[i], qT[s][:].rearrange("p a b -> p (a b)"), reads=[("qT", s)])
                tv = tr4(iqb, "iq_b")
                fw.copy("dve", iqT[s][:], tv, [("ps", 4)], [("iqT", s)])
                fw.dma("sp", IQTd[i], iqT[s][:].rearrange("p a b -> p (a b)"), reads=[("iqT", s)])
                for half in range(2):
                    b = proj(s, C_GR + half * 512, 512)
                    fw.act(sg[:, half * 512:(half + 1) * 512], PS[b][:], AF.Silu, [("ps", b)], [("sg", half)])
                tv = tr4(qrb[:].rearrange("p h d -> p (h d)"), "qr_b")
                fw.copy("dve", qrT[:], tv, [("ps", 4)], ["qrT"])
                fw.tt("dve", qxT[:], tv, qdecT.rearrange("p (h t) -> p h t", t=128), ALU.mult, [("ps", 4), "cst"], ["qxT"])
                tv = tr4(krb[:].rearrange("p h d -> p (h d)"), "kr_b")
                fw.copy("dve", krT[:], tv, [("ps", 4)], ["krT"])
                for half in range(2):
                    b = proj(s, C_GA + half * 512, 512)
                    fw.act(sga[s][:, half * 512:(half + 1) * 512], PS[b][:], AF.Sigmoid, [("ps", b)], [("sga", s, half)])
                fw.dma("sp", SGAd[i], sga[s][:], reads=[("sga", s, 0), ("sga", s, 1)])
                b = cbank[cstate["i"] % 4]
                cstate["i"] += 1
                for h in range(4):
                    fw.mm(PS[b][:, h * 128:(h + 1) * 128], krT[:, h, :], qrT[:, h, :], True, True, ["krT", "qrT"], [("ps", b)], sig=(h == 3))
                fw.tt("dve", innT[:], PS[b][:].rearrange("p (h t) -> p h t", t=128), dmaskT.rearrange("p (h t) -> p h t", t=128),
                      ALU.mult, [("ps", b), "cst"], ["innT"])
                for half in range(2):
                    b = proj(s, C_GB + half * 512, 512)
                    fw.act(sgb[s][:, half * 512:(half + 1) * 512], PS[b][:], AF.Sigmoid, [("ps", b)], [("sgb", s, half)])
                fw.dma("sp", SGBd[i], sgb[s][:], reads=[("sgb", s, 0), ("sgb", s, 1)])
                for h in range(4):
                    ob = 5 + h // 2
                    osl = PS[ob][:, (h % 2) * 256:(h % 2) * 256 + 256]
                    fw.mm(osl, innT[:, h, :], vrb[:, h, :], True, False, ["innT", ("vrb", h // 2)], [("ps", ob)], sig=False)
                    fw.mm(osl, qxT[:, h, :], rb[s][:, h * 256:(h + 1) * 256], False, True, ["qxT", ("rb", s)], [("ps", ob)], sig=True)
                if i + 1 < nown_lim:
                    trans(1 - s)
                for h in range(4):
                    ob = 5 + h // 2
                    osl = PS[ob][:, (h % 2) * 256:(h % 2) * 256 + 256]
                    fw.op("dve", lambda e, h=h, osl=osl: e.bn_stats(out=bst[:, h, :], in_=osl), [("ps", ob)], [("bst", h)])
                    fw.op("dve", lambda e, h=h: e.bn_aggr(out=mv[:, h, :], in_=bst[:, h, :]), [("bst", h)], [("mv", h)])
                MV = [("mv", h) for h in range(4)]
                fw.ts("dve", gv[:, 0:4], mv[:, :, 1], EPS, None, ALU.add, None, MV, ["gv0"])
                fw.act(gv[:, 4:8], gv[:, 0:4], AF.Sqrt, ["gv0"], ["gv1"])
                fw.op("dve", lambda e: e.reciprocal(out=gv[:, 8:12], in_=gv[:, 4:8]), ["gv1"], ["gv2"])
                for h in range(4):
                    ob = 5 + h // 2
                    osl = PS[ob][:, (h % 2) * 256:(h % 2) * 256 + 256]
                    fw.ts("dve", yn[:, h * 256:(h + 1) * 256], osl, mv[:, h, 0:1], gv[:, 8 + h:9 + h], ALU.subtract, ALU.mult,
                          [("ps", ob), ("mv", h), "gv2"], [("yn", h)])
                YN = [("yn", h) for h in range(4)]
                fw.tt("dve", yn[:], yn[:], gnbc[:], ALU.mult, YN + ["gnbc"], YN)
                fw.tt("dve", zb[:], yn[:], sg[:], ALU.mult, YN + [("sg", 0), ("sg", 1)], ["zb"])

                def ztail(i=i, s=s):
                    t8 = psb(4).rearrange("p (a b) -> p a b", b=128)
                    for kc in range(8):
                        fw.tr(t8[:, kc, :], zb[:, kc * 128:(kc + 1) * 128], ident[:], ["zb", "ident"], [("ps", 4)], sig=(kc == 7))
                    fw.copy("dve", zT[s][:], t8, [("ps", 4)], [("zT", s)])
                    fw.dma("sp", ZTd[i], zT[s][:].rearrange("p a b -> p (a b)"), reads=[("zT", s)])
                pending.append(ztail)
            for fn_ in pending:
                fn_()

            m2 = T(es_b, "m2", [128, 2], F32)
            r2 = T(es_b, "r2", [1, 8], F32)
            onesr = T(es_b, "onesr", [1, 128], F32)
            nmt = T(es_b, "nmt", [128, 1], F32)
            fw.op("pool", lambda e: e.memset(onesr[:], 1.0), writes=["onesr"])
            fw.op("dve", lambda e: e.tensor_reduce(out=m2[:, 0:1], in_=qmx[:], axis=AX.X, op=ALU.max), ["qmx"], ["m2a"])
            fw.op("dve", lambda e: e.tensor_reduce(out=m2[:, 1:2], in_=kmx[:], axis=AX.X, op=ALU.max), ["kmx"], ["m2b"])
            fw.tr(PS[1][0:1, 0:128], m2[:, 0:1], identf[:], ["m2a", "identf"], [("ps", 1)], sig=False)
            fw.tr(PS[1][0:1, 128:256], m2[:, 1:2], identf[:], ["m2b", "identf"], [("ps", 1)], sig=True)
            fw.op("dve", lambda e: e.tensor_reduce(out=r2[:, 0:2], in_=PS[1][0:1, 0:256].rearrange("p (a t) -> p a t", t=128),
                                                   axis=AX.X, op=ALU.max), [("ps", 1)], ["r2a"])
            fw.tt("dve", r2[:, 2:3], r2[:, 0:1], r2[:, 1:2], ALU.mult, ["r2a"], ["r2b"])
            fw.act(r2[:, 3:4], r2[:, 2:3], AF.Sqrt, ["r2b"], ["r2c"])
            fw.ts("dve", r2[:, 4:5], r2[:, 3:4], -0.125, None, ALU.mult, None, ["r2c"], ["r2d"])
            fw.mm(PS[2][:, 0:1], onesr[0:1, :], r2[0:1, 4:5], True, True, ["onesr", "r2d"], [("ps", 2)])
            fw.copy("dve", nmt[:], PS[2][:, 0:1], [("ps", 2)], ["nmt"])
            fw.dma("sp", NMd, nmt[:], reads=["nmt"])
            fw.barrier()
    if stop_after <= 2:
        fw.finish()
        return nc

    with ExitStack() as es:
        scores = T(es, "scores", [128, S], F32)
        mb = [T(es, "mb%d" % i, [128, S], BF16) for i in range(2)]
        ikT2 = T(es, "ikT2", [128, S], BF16)
        KTc = [T(es, "KTc%d" % i, [128, 4, 1024], BF16) for i in range(2)]
        Vc = [T(es, "Vc%d" % i, [128, 4, 8 * 130], BF16) for i in range(2)]
        mbT = [T(es, "mbT%d" % i, [128, 1024], BF16) for i in range(2)]
        PT = [T(es, "PT%d" % i, [128, 512], BF16) for i in range(3)]
        rl = [T(es, "rl%d" % i, [128, 512], F32) for i in range(2)]
        iqTt = [T(es, "iqTt%d" % i, [128, 4, 128], BF16) for i in range(2)]
        qTt = [T(es, "qTt%d" % i, [128, 4, 128], BF16) for i in range(2)]
        iwt = [T(es, "iwt%d" % i, [128, 8], F32) for i in range(2)]
        maskt = [T(es, "maskt%d" % i, [128, 512], F32) for i in range(2)]
        bs = T(es, "bs", [128, 20], F32)
        steps = T(es, "steps", [128, NIT], F32)
        negm = T(es, "negm", [128, 1], F32)
        Wp = T(es, "Wp", [64, 8, D], BF16)
        Wr = T(es, "Wr", [128, 8, D], BF16)
        Wo = T(es, "Wo", [128, 8, D], BF16)
        OTs8 = T(es, "OTs8", [64, 8, 128], BF16)
        rinv8 = T(es, "rinv8", [128, 8], F32)
        one_t = T(es, "one_t", [128, 1], F32)
        yacc = T(es, "yacc", [128, D], F32)
        zTt = T(es, "zTt", [128, 8, 128], BF16)
        sgat = T(es, "sgat", [128, D], BF16)
        sgbt = T(es, "sgbt", [128, D], BF16)
        t1 = T(es, "t1", [128, D], F32)
        t2 = T(es, "t2", [128, D], F32)
        mg = T(es, "mg", [128, D], BF16)

        fw.dma("sp", negm[:], NMd, writes=["negm"])
        fw.dma("sp", yacc[:], MODd[2], writes=[("yacc", 0), ("yacc", 1)])
        fw.dma("pool", ikT2[:, 0:nb_lim * 128], IKTd[:, 0:nb_lim * 128], writes=["ikT2"])
        fw.op("pool", lambda e: e.memset(one_t[:], 1.0), writes=["one_t"])
        slots = [(t1, [("t", 0)]), (t2, [("t", 1)])] + [
            (scores[:, j * 1024:(j + 1) * 1024], [("sc", 2 * j), ("sc", 2 * j + 1)]) for j in range(8)]
        k = 0
        for h in range(8):
            sap, sres = slots[k % 10]
            fw.dma(("sp", "pool")[k % 2], sap[0:64, :], w_ap[h * 64:(h + 1) * 64, :], writes=sres)
            fw.copy(("act", "dve")[k % 2], Wp[:, h, :], sap[0:64, :], sres, [("Wp", h)])
            k += 1
        for kc in range(8):
            sap, sres = slots[k % 10]
            fw.dma(("sp", "pool")[k % 2], sap[:, :], w_rp[kc * 128:(kc + 1) * 128, :], writes=sres)
            fw.copy(("act", "dve")[k % 2], Wr[:, kc, :], sap[:, :], sres, [("Wr", kc)])
            k += 1
        for kc in range(8):
            sap, sres = slots[k % 10]
            fw.dma(("sp", "pool")[k % 2], sap[:, :], w_o[kc * 128:(kc + 1) * 128, :], writes=sres)
            fw.tt("dve", Wo[:, kc, :], sap[:, :], yacc[:], ALU.mult, sres + [("yacc", 0), ("yacc", 1)], [("Wo", kc)])
            k += 1
        pw = cst[:, 1032:1032 + NIT]
        cnt_ = {"rl": 0, "pt": 0, "kv": 0, "ib": 0, "lb": 0, "mt": 0, "ifl": None}

        def thread_b(i):
            s = i % 2
            nch = i + 1
            nk = 512 * nch
            mbi = mb[s]
            fw.dma("sp", iqTt[s][:].rearrange("p a b -> p (a b)"), IQTd[i], writes=[("iqTt", s)])
            fw.dma("sp", iwt[s][:], IWd[i], writes=[("iwt", s)])
            fw.dma("sp", maskt[s][:], maskd_d[i], writes=[("maskt", s)])
            items = [(ch, h) for ch in range(nch) for h in range(8)]
            ibk = (0, 5)

            def idx_mm(k):
                ch, h = items[k]
                pr, base = h // 2, 64 * (h % 2)
                bk = ibk[k % 2]
                cnt_["ifl"] = bk
                fw.mm(PS[bk][:], iqTt[s][base:base + 64, pr, :], ikT2[base:base + 64, ch * 512:(ch + 1) * 512], True, True,
                      [("iqTt", s), "ikT2"], [("ps", bk)])

            def fma(k, r):
                ch, h = items[k]
                sc_ = scores[:, ch * 512:(ch + 1) * 512]
                if h == 0:
                    fw.ts("dve", sc_, rl[r][:], iwt[s][:, 0:1], None, ALU.mult, None, [("rl", r), ("iwt", s)], [("sc", ch)])
                else:
                    fw.stt(sc_, rl[r][:], iwt[s][:, h:h + 1], sc_, ALU.mult, ALU.add, [("rl", r), ("iwt", s), ("sc", ch)], [("sc", ch)])

            idx_mm(0)
            yield
            prev_r = None
            for k, (ch, h) in enumerate(items):
                bk = ibk[k % 2]
                r = cnt_["rl"] % 2
                cnt_["rl"] += 1
                fw.act(rl[r][:], PS[bk][:], AF.Relu, [("ps", bk)], [("rl", r)])
                if prev_r is not None:
                    fma(k - 1, prev_r)
                prev_r = r
                cnt_["ifl"] = None
                if k + 1 < len(items):
                    idx_mm(k + 1)
                yield
            fma(len(items) - 1, prev_r)
            SC = [("sc", ch) for ch in range(nch)]
            fw.op("dve", lambda e: e.tensor_reduce(out=bs[:, 0:1], in_=scores[:, 0:nk], axis=AX.X, op=ALU.min), SC, ["mn"])
            fw.tt("dve", scores[:, nk - 512:nk], scores[:, nk - 512:nk], maskt[s][:], ALU.add, [("sc", nch - 1), ("maskt", s)], [("sc", nch - 1)])
            yield
            fw.op("dve", lambda e: e.tensor_reduce(out=bs[:, 1:2], in_=scores[:, 0:nk], axis=AX.X, op=ALU.max), SC, ["mx"])
            fw.tt("dve", bs[:, 2:3], bs[:, 1:2], bs[:, 0:1], ALU.subtract, ["mx", "mn"], ["w0"])
            fw.ts("dve", steps[:], pw, bs[:, 2:3], None, ALU.mult, None, ["w0", "cst"], ["steps"])
            fw.tt("dve", bs[:, 5:6], bs[:, 0:1], steps[:, 0:1], ALU.add, ["mn", "steps"], [("mid", 0)])
            yield
            nA = nk if nk <= 1024 else ((nk // 2 + 511) // 512) * 512
            pieces = []
            c0 = 0
            while c0 < nA:
                c1 = min(nA, c0 + 2048)
                pieces.append(("act", c0, c1))
                c0 = c1
            while c0 < nk:
                c1 = min(nk, c0 + 2048)
                pieces.append(("dve", c0, c1))
                c0 = c1
            pa = [p for p in pieces if p[0] == "act"]
            pd = [p for p in pieces if p[0] == "dve"]
            order = []
            for j in range(max(len(pa), len(pd))):
                if j < len(pa):
                    order.append(pa[j] + (8 + j,))
                if j < len(pd):
                    order.append(pd[j] + (12 + j,))
            cthr = float(2 * TOPK - nA) - 0.5
            for k in range(NIT):
                m0, m1 = k % 2, (k + 1) % 2
                for (eng, c0, c1, col) in order:
                    if eng == "act":
                        fw.act(mbi[:, c0:c1], scores[:, c0:c1], AF.Sign, SC + [("mid", m0)], [("mbp", s, c0), ("cntp", col)],
                               bias=bs[:, 5 + m0:6 + m0], scale=-1.0, accum_out=bs[:, col:col + 1])
                    else:
                        fw.op("dve", lambda e, c0=c0, c1=c1, col=col, m0=m0: e.tensor_scalar(
                            out=mbi[:, c0:c1], in0=scores[:, c0:c1], scalar1=bs[:, 5 + m0:6 + m0], scalar2=None,
                            op0=ALU.is_ge, op1=ALU.add, accum_out=bs[:, col:col + 1]),
                            SC + [("mid", m0)], [("mbp", s, c0), ("cntp", col)])
                    yield
                CA = [("cntp", 8 + j) for j in range(len(pa))]
                CD = [("cntp", 12 + j) for j in range(len(pd))]
                if len(pa) > 1:
                    fw.op("dve", lambda e: e.tensor_reduce(out=bs[:, 3:4], in_=bs[:, 8:8 + len(pa)], axis=AX.X, op=ALU.add), CA, ["sA"])
                    sA, sAr = bs[:, 3:4], ["sA"]
                else:
                    sA, sAr = bs[:, 8:9], CA
                if len(pd) == 0:
                    fw.ts("dve", bs[:, 7:8], sA, -1.0, None, ALU.mult, None, sAr, ["comb"])
                else:
                    if len(pd) > 1:
                        fw.op("dve", lambda e: e.tensor_reduce(out=bs[:, 6 + 10:7 + 10], in_=bs[:, 12:12 + len(pd)], axis=AX.X, op=ALU.add), CD, ["sD"])
                        sD, sDr = bs[:, 16:17], ["sD"]
                    else:
                        sD, sDr = bs[:, 12:13], CD
                    fw.stt(bs[:, 7:8], sD, 2.0, sA, ALU.mult, ALU.subtract, sDr + sAr, ["comb"])
                fw.stt(bs[:, 4:5], bs[:, 7:8], cthr, steps[:, k:k + 1], ALU.is_ge, ALU.mult, ["comb", "steps"], ["incr"])
                kn = min(k + 1, NIT - 1)
                fw.stt(bs[:, 5 + m1:6 + m1], bs[:, 4:5], steps[:, kn:kn + 1], bs[:, 5 + m0:6 + m0], ALU.subtract, ALU.add,
                       ["incr", "steps", ("mid", m0)], [("mid", m1)])
                yield
            mf = NIT % 2
            fw.ts("dve", mbi[:, 0:nk], scores[:, 0:nk], bs[:, 5 + mf:6 + mf], NEGM, ALU.is_lt, ALU.mult, SC + [("mid", mf)],
                  [("mb", s)] + [("mbp", s, p[1]) for p in pieces])
            yield

        def run_all(gen):
            for _ in gen:
                pass

        def thread_a(i, tick):
            s = i % 2
            nkb = 4 * (i + 1)
            mbi = mb[s]
            fw.dma("sp", qTt[s][:].rearrange("p a b -> p (a b)"), QTd[i], writes=[("qTt", s)])
            nc8 = (nkb + 7) // 8
            kvbuf = {}

            def nbk(c8):
                return min(8, nkb - c8 * 8)

            def prep_load(c8):
                kb0, nb_ = c8 * 8, nbk(c8)
                kv = c8 % 2
                fw.dma("sp", KTc[kv][:, :, 0:nb_ * 128], KTd[:, :, kb0 * 128:(kb0 + nb_) * 128].rearrange("q p t -> p q t"), writes=[("KTc", kv)])
                fw.dma("pool", Vc[kv][:, :, 0:nb_ * 130], Vd[:, :, kb0:kb0 + nb_, :].rearrange("q p b c -> p q (b c)"), writes=[("Vc", kv)])

            def prep_mask(c8):
                kb0, nb_ = c8 * 8, nbk(c8)
                mt = c8 % 2
                tbk = 0 if cnt_.get("ifl") == 5 else 5
                t8 = psb(tbk).rearrange("p (a b) -> p a b", b=128)
                for jj in range(nb_):
                    kbg = kb0 + jj
                    fw.tr(t8[:, jj, :], mbi[:, kbg * 128:(kbg + 1) * 128], ident[:], [("mb", s), "ident"], [("ps", tbk)], sig=(jj == nb_ - 1))
                fw.copy("dve", mbT[mt][:, 0:nb_ * 128], psb(tbk)[:, 0:nb_ * 128], [("ps", tbk)], [("mbT", mt)])

            units = [(c8, pr, g) for c8 in range(nc8) for pr in range(4) for g in range(nbk(c8) // 4)]
            LB = {}

            def qk(u):
                c8, pr, g = u
                kv = mt = c8 % 2
                lbs = []
                for a in range(2):
                    lb = 1 + cnt_["lb"] % 4
                    cnt_["lb"] += 1
                    lbs.append(lb)
                    fw.mm(PS[lb][:], ident[:], mbT[mt][:, g * 512:(g + 1) * 512], True, False, [("mbT", mt), "ident"], [("ps", lb)], sig=False)
                for jj in range(4):
                    kbl = g * 4 + jj
                    for a in range(2):
                        base = 64 * a
                        fw.mm(PS[lbs[a]][:, jj * 128:(jj + 1) * 128], KTc[kv][base:base + 64, pr, kbl * 128:(kbl + 1) * 128],
                              qTt[s][base:base + 64, pr, :], False, jj == 3, [("KTc", kv), ("qTt", s)], [("ps", lbs[a])], sig=(jj == 3))
                LB[u] = lbs

            def pv(u):
                c8, pr, g = u
                kv = c8 % 2
                nb_ = nbk(c8)
                lbs = LB.pop(u)
                ps_ = []
                for a in range(2):
                    p = cnt_["pt"] % 3
                    cnt_["pt"] += 1
                    ps_.append(p)
                    fw.act(PT[p][:], PS[lbs[a]][:], AF.Exp, [("ps", lbs[a]), "negm"], [("PT", p)], bias=negm[:, 0:1], scale=0.125)
                for a in range(2):
                    p = ps_[a]
                    for jj in range(4):
                        kbl = g * 4 + jj
                        vcol = kbl * 130 + a * 65
                        fw.mm(PS[6 + a][0:65, 0:128], Vc[kv][:, pr, vcol:vcol + 65], PT[p][:, jj * 128:(jj + 1) * 128],
                              kbl == 0, kbl == nb_ - 1, [("Vc", kv), ("PT", p)], [("ps", 6 + a)], sig=(jj == 3))
                    tick()
                if g == nb_ // 4 - 1:
                    for a in range(2):
                        h = 2 * pr + a
                        acc = t1[0:65, h * 128:(h + 1) * 128]
                        if c8 == 0:
                            fw.copy("dve", acc, PS[6 + a][0:65, 0:128], [("ps", 6 + a)], [("t", 0)])
                        else:
                            fw.tt("dve", acc, acc, PS[6 + a][0:65, 0:128], ALU.add, [("ps", 6 + a), ("t", 0)], [("t", 0)])

            prep_load(0)
            if nc8 > 1:
                prep_load(1)
            prep_mask(0)
            qk(units[0])
            for idx, u in enumerate(units):
                c8, pr, g = u
                if idx + 1 < len(units):
                    qk(units[idx + 1])
                pv(u)
                first = (pr == 0 and g == 0)
                last = (idx + 1 == len(units)) or units[idx + 1][0] != c8
                if first and c8 + 1 < nc8:
                    prep_mask(c8 + 1)
                if last and c8 + 2 < nc8:
                    prep_load(c8 + 2)
            fw.copy("act", OTs8[:, 0:4, :], t1[0:64, 0:512].rearrange("p (h t) -> p h t", t=128), [("t", 0)], [("OTs8", 0)])
            fw.copy("pool", OTs8[:, 4:8, :], t1[0:64, 512:1024].rearrange("p (h t) -> p h t", t=128), [("t", 0)], [("OTs8", 1)])
            for h in range(8):
                fw.mm(PS[6][:, h:h + 1], t1[64:65, h * 128:(h + 1) * 128], one_t[64:65, 0:1], True, True, [("t", 0), "one_t"], [("ps", 6)], sig=(h == 7))
            fw.op("dve", lambda e: e.reciprocal(out=rinv8[:], in_=PS[6][:, 0:8]), [("ps", 6)], ["rinv8"])
            tick()
            for h in range(8):
                for nn in range(2):
                    yb = 1 + 2 * (h % 2) + nn
                    fw.mm(PS[yb][:], OTs8[:, h, :], Wp[:, h, nn * 512:(nn + 1) * 512], True, True, [("OTs8", h // 4), ("Wp", h)], [("ps", yb)])
                    ya = yacc[:, nn * 512:(nn + 1) * 512]
                    if h == 0:
                        fw.ts("dve", ya, PS[yb][:], rinv8[:, h:h + 1], None, ALU.mult, None, [("ps", yb), "rinv8"], [("yacc", nn)])
                    else:
                        fw.stt(ya, PS[yb][:], rinv8[:, h:h + 1], ya, ALU.mult, ALU.add, [("ps", yb), "rinv8", ("yacc", nn)], [("yacc", nn)])
                tick()
            fw.dma("sp", zTt[:].rearrange("p a b -> p (a b)"), ZTd[i], writes=["zTt"])
            fw.dma("sp", sgat[:], SGAd[i], writes=["sgat"])
            fw.dma("sp", sgbt[:], SGBd[i], writes=["sgbt"])
            for nn in range(2):
                for kc in range(8):
                    fw.mm(PS[1 + nn][:], zTt[:, kc, :], Wr[:, kc, nn * 512:(nn + 1) * 512], kc == 0, kc == 7, ["zTt", ("Wr", kc)], [("ps", 1 + nn)])
                hs = slice(nn * 512, (nn + 1) * 512)
                fw.tt("dve", t1[:, hs], PS[1 + nn][:], sgbt[:, hs], ALU.mult, [("ps", 1 + nn), "sgbt"], [("t", 0)])
                fw.tt("pool", t2[:, hs], yacc[:, hs], sgat[:, hs], ALU.mult, [("yacc", nn), "sgat"], [("t", 1)])
                fw.tt("pool", mg[:, hs], t1[:, hs], t2[:, hs], ALU.add, [("t", 0), ("t", 1)], [("mg", nn)])
                tick()
            tbk = 0 if cnt_.get("ifl") == 5 else 5
            t8 = psb(tbk).rearrange("p (a b) -> p a b", b=128)
            for kc in range(8):
                fw.tr(t8[:, kc, :], mg[:, kc * 128:(kc + 1) * 128], ident[:], [("mg", kc // 4), "ident"], [("ps", tbk)], sig=(kc == 7))
            fw.copy("act", zTt[:], t8, [("ps", tbk)], ["zTt"])
            fw.dma("sp", yacc[:], xo[i * 128:(i + 1) * 128, :], writes=[("yacc", 0), ("yacc", 1)])
            for nn in range(2):
                for kc in range(8):
                    fw.mm(PS[1 + nn][:], zTt[:, kc, :], Wo[:, kc, nn * 512:(nn + 1) * 512], kc == 0, kc == 7, ["zTt", ("Wo", kc)], [("ps", 1 + nn)])
                hs = slice(nn * 512, (nn + 1) * 512)
                fw.tt("dve", t2[:, hs], PS[1 + nn][:], yacc[:, hs], ALU.add, [("ps", 1 + nn), ("yacc", nn)], [("t", 1)])
                tick()
            fw.dma("sp", X1d[i], t2[:], reads=[("t", 1)])

        run_all(thread_b(0))
        for i in range(nown_lim):
            if i + 1 < nown_lim:
                gb = thread_b(i + 1)
                n_b = 8 * (i + 2) + NIT * ((512 * (i + 2) + 2047) // 2048 + 2) + 4
                n_a = 8 * (i + 1) + 8 + 4
                state = {"acc": 0.0, "done": False}

                def tick(gb=gb, state=state, ratio=n_b / n_a):
                    if state["done"]:
                        return
                    state["acc"] += ratio
                    while state["acc"] >= 1.0:
                        state["acc"] -= 1.0
                        try:
                            next(gb)
                        except StopIteration:
                            state["done"] = True
                            return
                thread_a(i, tick)
                run_all(gb)
            else:
                thread_a(i, lambda: None)
        fw.barrier()
    if stop_after <= 3:
        fw.finish()
        return nc

    with ExitStack() as es:
        Wg = T(es, "Wg", [128, 8, DFF], BF16)
        Wu = T(es, "Wu", [128, 8, DFF], BF16)
        Wd = T(es, "Wd", [128, 22, D], BF16)
        A2 = T(es, "A2", [128, D], F32)
        B2 = T(es, "B2", [128, D], F32)
        G2 = T(es, "G2", [128, D], F32)
        FN = T(es, "FN", [128, D], F32)
        fw.dma("pool", A2[:], MODd[3], writes=["A2"])
        fw.dma("pool", B2[:], MODd[4], writes=["B2"])
        fw.dma("pool", G2[:], MODd[5], writes=["G2"])
        fw.dma("pool", FN[:], MODd[6], writes=["FN"])
        if nb_lim == NB:
            for kc in range(8):
                fw.dma(("sp", "pool")[kc % 2], Wg[:, kc, :], WGd[:, kc, :], writes=[("Wg", kc)])
            for kc in range(8):
                fw.dma(("sp", "pool")[kc % 2], Wu[:, kc, :], WUd[:, kc, :], writes=[("Wu", kc)])
            for q4 in range(2):
                fw.dma(("sp", "pool")[q4 % 2], Wd[:, q4 * 11:(q4 + 1) * 11, :], WDd[:, q4 * 11:(q4 + 1) * 11, :], writes=[("Wd", fc) for fc in range(q4 * 11, (q4 + 1) * 11)])
        else:
            with ExitStack() as es_w:
                stg3 = [T(es_w, "stg3_%d" % i, [128, DFF], F32) for i in range(2)]
                k = 0
                for (wsrc, wdst, nm) in ((w_g, Wg, "Wg"), (w_u, Wu, "Wu")):
                    for kc in range(8):
                        s = k % 2
                        fw.dma("sp", stg3[s][:], wsrc[kc * 128:(kc + 1) * 128, :], writes=[("stg3", s)])
                        fw.copy(("act", "dve")[k % 2], wdst[:, kc, :], stg3[s][:], [("stg3", s)], [(nm, kc)])
                        k += 1
                for fc in range(22):
                    s = k % 2
                    fw.dma("sp", stg3[s][:, 0:D], w_d[fc * 128:(fc + 1) * 128, :], writes=[("stg3", s)])
                    fw.copy(("act", "dve")[k % 2], Wd[:, fc, :], stg3[s][:, 0:D], [("stg3", s)], [("Wd", fc)])
                    k += 1
                fw.barrier()
        x1t = [T(es, "x1t%d" % i, [128, D], F32) for i in range(2)]
        h2T = T(es, "h2T", [128, 8, 512], BF16)
        aT = T(es, "aT", [128, 22, 512], BF16)
        tmp3 = T(es, "tmp3", [128, D], F32)
        hb3 = T(es, "hb3", [128, D], BF16)
        t3 = T(es, "t3", [128, D], F32)
        sl = [T(es, "sl%d" % i, [128, 512], F32) for i in range(2)]
        st3 = T(es, "st3", [128, 8], F32)
        ngrp = (nown_lim + 3) // 4

        def norm_group(g):
            nbg = min(4, nown_lim - 4 * g)
            for bi in range(nbg):
                    i = 4 * g + bi
                    s = i % 2
                    fw.dma("sp", x1t[s][:], X1d[i], writes=[("x1t", s)])
                    fw.act(hb3[:], x1t[s][:], AF.Square, [("x1t", s)], ["hb3", "ss"], accum_out=st3[:, 0:1])
                    fw.ts("dve", st3[:, 1:2], st3[:, 0:1], 1.0 / D, EPS, ALU.mult, ALU.add, ["ss"], ["vv"])
                    fw.act(st3[:, 2:3], st3[:, 1:2], AF.Sqrt, ["vv"], ["sd"])
                    fw.op("dve", lambda e: e.reciprocal(out=st3[:, 3:4], in_=st3[:, 2:3]), ["sd"], ["rstd"])
                    fw.stt(tmp3[:], x1t[s][:], st3[:, 3:4], A2[:], ALU.mult, ALU.mult, [("x1t", s), "rstd", "A2"], ["tmp3"])
                    fw.tt("pool", hb3[:], tmp3[:], B2[:], ALU.add, ["tmp3", "B2"], ["hb3"])
                    tb_ = psb(7).rearrange("p (a b) -> p a b", b=128)
                    for kc in range(8):
                        fw.tr(tb_[:, kc, :], hb3[:, kc * 128:(kc + 1) * 128], ident[:], ["hb3", "ident"], [("ps", 7)], sig=(kc == 7))
                    fw.copy("act", h2T[:, :, bi * 128:(bi + 1) * 128], tb_, [("ps", 7)], [("h2T", bi)])

        norm_group(0)
        for g in range(ngrp):
            nbg = min(4, nown_lim - 4 * g)
            NT = nbg * 128
            H2 = [("h2T", bi) for bi in range(nbg)]
            for fc in range(22):
                gb_, ub_ = fc % 2, 2 + fc % 2
                for kc in range(8):
                    fw.mm(PS[gb_][:, 0:NT], Wg[:, kc, fc * 128:(fc + 1) * 128], h2T[:, kc, 0:NT], kc == 0, kc == 7, H2 + [("Wg", kc)], [("ps", gb_)])
                for kc in range(8):
                    fw.mm(PS[ub_][:, 0:NT], Wu[:, kc, fc * 128:(fc + 1) * 128], h2T[:, kc, 0:NT], kc == 0, kc == 7, H2 + [("Wu", kc)], [("ps", ub_)])
                fw.act(sl[fc % 2][:, 0:NT], PS[gb_][:, 0:NT], AF.Silu, [("ps", gb_)], [("sl", fc % 2)])
                fw.tt("dve", aT[:, fc, 0:NT], sl[fc % 2][:, 0:NT], PS[ub_][:, 0:NT], ALU.mult, [("sl", fc % 2), ("ps", ub_)], [("aT", fc)])
            AT = [("aT", fc) for fc in range(22)]
            if g + 1 < ngrp:
                norm_group(g + 1)
            for bi in range(nbg):
                i = 4 * g + bi
                s = i % 2
                fw.dma("sp", x1t[s][:], X1d[i], writes=[("x1t", s)])
                for nn in range(2):
                    db = 4 + nn
                    for fc in range(22):
                        fw.mm(PS[db][:], aT[:, fc, bi * 128:(bi + 1) * 128], Wd[:, fc, nn * 512:(nn + 1) * 512], fc == 0, fc == 21,
                              [("aT", fc), ("Wd", fc)], [("ps", db)])
                    hs = slice(nn * 512, (nn + 1) * 512)
                    fw.tt("dve", t3[:, hs], PS[db][:], G2[:, hs], ALU.mult, [("ps", db), "G2"], [("t3", nn)])
                    fw.tt("pool", t3[:, hs], t3[:, hs], x1t[s][:, hs], ALU.add, [("t3", nn), ("x1t", s)], [("t3", nn)])
                T3 = [("t3", 0), ("t3", 1)]
                fw.act(hb3[:], t3[:], AF.Square, T3, ["hb3", "ss2"], accum_out=st3[:, 4:5])
                fw.ts("dve", st3[:, 5:6], st3[:, 4:5], 1.0 / D, EPS, ALU.mult, ALU.add, ["ss2"], ["vv2"])
                fw.act(st3[:, 6:7], st3[:, 5:6], AF.Sqrt, ["vv2"], ["sd2"])
                fw.op("dve", lambda e: e.reciprocal(out=st3[:, 7:8], in_=st3[:, 6:7]), ["sd2"], ["rstd2"])
                fw.stt(tmp3[:], t3[:], st3[:, 7:8], FN[:], ALU.mult, ALU.mult, T3 + ["rstd2", "FN"], ["tmp3"])
                fw.dma("sp", out_d[i * 128:(i + 1) * 128, :], tmp3[:], reads=["tmp3"])
        fw.barrier()

    fw.finish()
    return nc


def _rope_tab(pos, dim, theta, scale=1.0):
    inv = 1.0 / (theta ** (np.arange(0, dim, 2, dtype=np.float64) / dim))
    ang = pos.astype(np.float64)[:, None] * inv[None, :]
    cs, sn = np.cos(ang) * scale, np.sin(ang) * scale
    return np.concatenate([cs, cs, -sn, sn], axis=1).astype(np.float32)


def make_consts(j):
    cst = {}
    cst["ident"] = np.eye(128, dtype=np.float32).astype(ml_dtypes.bfloat16)
    pos = np.arange(S)
    tA = _rope_tab(pos, 16, 500000.0)
    cst["ropeA"] = np.ascontiguousarray(tA.reshape(NB, 128, 32).transpose(1, 0, 2))
    tRk = _rope_tab(pos, 128, 10000.0, scale=128 ** -0.5)
    tRq = _rope_tab(pos, 128, 10000.0)
    cst["ropeR"] = np.ascontiguousarray(tRk.reshape(NB, 128, 256))
    own = np.array([4 * i + j for i in range(NOWN)])
    cst["ropeAo"] = np.ascontiguousarray(tA.reshape(NB, 128, 32)[own].transpose(1, 0, 2))
    cst["ropeRq"] = np.ascontiguousarray(tRq.reshape(NB, 128, 256)[own])
    cst["ropeRk"] = np.ascontiguousarray(tRk.reshape(NB, 128, 256)[own])
    c = np.zeros((128, 8 + 1024 + 16), np.float32)
    c[:, 1032:1048] = (0.5 ** np.arange(1, 17))[None, :]
    c[:, j] = 1.0
    g = np.array(GAMMA, np.float64)
    p = np.arange(128, dtype=np.float64)
    c[:, 4:8] = (g[None, :] ** (127.0 - p)[:, None])
    qd = g[:, None] ** (p[None, :] + 1.0)
    c[:, 8:520] = qd.reshape(1, 512)
    diff = p[None, :] - p[:, None]
    dm = np.where(diff[None] >= 0, g[:, None, None] ** np.maximum(diff, 0.0)[None], 0.0)
    c[:, 520:1032] = dm.transpose(1, 0, 2).reshape(128, 512)
    cst["cst"] = c
    md = np.zeros((NOWN, 128, 512), np.float32)
    for i in range(NOWN):
        qpos = 128 * (4 * i + j) + np.arange(128)
        kpos = 512 * i + np.arange(512)
        md[i] = np.where(kpos[None, :] <= qpos[:, None], 0.0, -1e30)
    cst["maskd"] = md
    return cst


def make_in_maps(inputs):
    x = np.asarray(inputs["x"], np.float32)
    c = np.asarray(inputs["c"], np.float32)
    f = lambda k: np.ascontiguousarray(np.asarray(inputs[k], np.float32))
    shared = {
        "w_ada": f("w_ada"), "b_ada": f("b_ada").reshape(1, -1),
        "nws": np.stack([f("norm1_w"), f("norm2_w"), f("final_norm_w"), f("gn_w")], 0).reshape(1, 4, D),
        "w_in": f("w_in"), "w_attn_proj": f("w_attn_proj"), "w_ret_proj": f("w_ret_proj"), "w_out": f("w_out"),
        "w_ffn_gate": f("w_ffn_gate"), "w_ffn_up": f("w_ffn_up"), "w_ffn_down": f("w_ffn_down"),
    }
    maps = []
    for core in range(8):
        b, j = core // 4, core % 4
        m = dict(shared)
        m["xf"] = np.ascontiguousarray(x[b])
        m["xo"] = np.ascontiguousarray(x[b].reshape(NOWN, 4, 128, D)[:, j].reshape(NOWN * 128, D))
        m["c_l"] = np.ascontiguousarray(c[b].reshape(8, 128).T)
        m.update(make_consts(j))
        maps.append(m)
    return maps


def kernel(**inputs):
    nc = build_program()
    maps = make_in_maps(inputs)
    res = run_bass_kernel_spmd(nc, maps, core_ids=list(range(8)))
    out = np.zeros((2, S, D), np.float32)
    for core in range(8):
        b, j = core // 4, core % 4
        o = np.asarray(res.results[core]["out"]).reshape(NOWN, 128, D)
        out[b].reshape(NOWN, 4, 128, D)[:, j] = o
    return out
```
